# Optimizing a Trainium2 kernel written in Bass

```python
import jax, jax.numpy as jnp
from jax import lax
import numpy as np

D_MODEL = 1024
BATCH = 4
SEQ = 4096
DEPTH = 1
DEC_BATCH = 128
DEC_SEQ = 8
PAST_LEN = 2048
PAGE_SIZE = 128

RWKV_HEADS = 8
RWKV_HEAD = 64
RWKV_W = RWKV_HEADS * RWKV_HEAD
DECAY_RANK = 64
AAA_RANK = 64
GATE_RANK = 160
RWKV_GN_EPS = 64e-5

NSA_HEADS = 8
NSA_KV = 2
HEAD_DIM = 64
HPG = NSA_HEADS // NSA_KV
NSA_W = NSA_HEADS * HEAD_DIM
KV_W = NSA_KV * HEAD_DIM
L_CMP = 32
D_CMP = 16
CMP_HID = 128
L_SEL = 64
N_SEL = 16
WINDOW = 512
Q_BLK = 128
ROT_DIM = HEAD_DIM // 4
ROPE_THETA = 500000.0

D_FF = -(-8 * D_MODEL // (3 * 256)) * 256
NORM_EPS = 1e-6
NEG = -1e30
FORCE = 1e6

RWKV_COLS = 3 * RWKV_W + DECAY_RANK + AAA_RANK + GATE_RANK
NSA_COLS = NSA_W + 6 * KV_W + 3 * NSA_HEADS
IN_COLS = RWKV_COLS + NSA_COLS + 2 * D_MODEL
RWKV_SPLITS = (RWKV_W, RWKV_W + DECAY_RANK, 2 * RWKV_W + DECAY_RANK, 3 * RWKV_W + DECAY_RANK, 3 * RWKV_W + DECAY_RANK + AAA_RANK)
NSA_SPLITS = tuple(NSA_W + i * KV_W for i in range(7))
IN_SPLITS = (RWKV_COLS, RWKV_COLS + NSA_COLS, RWKV_COLS + NSA_COLS + D_MODEL)

kernel_name = 'rwkv7_nsa_gated_hybrid_step'


def rms_norm(x, g):
    xf = x.astype(jnp.float32)
    y = xf * lax.rsqrt(jnp.mean(xf * xf, axis=-1, keepdims=True) + NORM_EPS)
    return (y * g.astype(jnp.float32)).astype(x.dtype)


def partial_rope(x, pos):
    half = ROT_DIM // 2
    inv = ROPE_THETA ** (-jnp.arange(half, dtype=jnp.float32) / half)
    ang = pos.astype(jnp.float32)[:, None] * inv[None, :]
    cos = jnp.cos(ang)[None, :, None, :]
    sin = jnp.sin(ang)[None, :, None, :]
    xf = x.astype(jnp.float32)
    x1, x2 = xf[..., :half], xf[..., half:ROT_DIM]
    out = jnp.concatenate([x1 * cos - x2 * sin, x2 * cos + x1 * sin, xf[..., ROT_DIM:]], axis=-1)
    return out.astype(x.dtype)


def masked_softmax(s, mask):
    p = jax.nn.softmax(jnp.where(mask, s.astype(jnp.float32), NEG), axis=-1)
    return jnp.where(mask, p, 0.0)


def rwkv7_scan(S0, r, decay, k, v, a_vec, b_vec):
    def step(S, inp):
        r_t, w_t, k_t, v_t, a_t, b_t = inp
        sa = jnp.einsum('bhij,bhj->bhi', S, a_t)
        S = S * w_t[:, :, None, :] + sa[..., None] * b_t[:, :, None, :] + v_t[..., None] * k_t[:, :, None, :]
        return S, jnp.einsum('bhij,bhj->bhi', S, r_t)
    xs = tuple(jnp.moveaxis(z, 1, 0) for z in (r, decay, k, v, a_vec, b_vec))
    S, y = lax.scan(step, S0, xs)
    return jnp.moveaxis(y, 0, 1), S


def rwkv7_time_mix(p, prev_row, S0, lp):
    B, T, _ = p.shape
    f32 = jnp.float32
    shifted = jnp.concatenate([prev_row[:, None, :].astype(p.dtype), p[:, :-1]], axis=1)
    xs = p + (shifted - p) * lp['mu']
    r, wlo, k, v, alo, glo = jnp.split(xs, RWKV_SPLITS, axis=-1)
    w = -jax.nn.softplus(-(lp['w0'] + jnp.tanh(wlo) @ lp['w_decay'])) - 0.5
    decay = jnp.exp(-jnp.exp(w.astype(f32)))
    a = jax.nn.sigmoid(lp['a0'] + alo @ lp['w_aaa'])
    g = jax.nn.sigmoid(glo) @ lp['w_gate']
    hd = lambda z: z.astype(f32).reshape(B, T, RWKV_HEADS, RWKV_HEAD)
    kk = hd(k * lp['k_k'])
    kk = kk * lax.rsqrt(jnp.maximum(jnp.sum(kk * kk, axis=-1, keepdims=True), 1e-24))
    k_h = hd(k * (1 + (a - 1) * lp['k_a']))
    r_h, v_h, a_h = hd(r), hd(v), hd(a)
    y, S = rwkv7_scan(S0.astype(f32), r_h, hd(decay), k_h, v_h, -kk, kk * a_h)
    mean = jnp.mean(y, axis=-1, keepdims=True)
    var = jnp.mean(jnp.square(y - mean), axis=-1, keepdims=True)
    y = ((y - mean) * lax.rsqrt(var + RWKV_GN_EPS)).reshape(B, T, RWKV_W) * lp['ln_w'] + lp['ln_b']
    bonus = jnp.sum(r_h * k_h * lp['r_k'], axis=-1, keepdims=True) * v_h
    out = (y + bonus.reshape(B, T, RWKV_W)) * g
    return out.astype(p.dtype), p[:, -1], S


def compress_blocks(kv, pe, w1, w2):
    B, T = kv.shape[:2]
    n_chunk = T // D_CMP
    ch = kv[:, :n_chunk * D_CMP].reshape(B, n_chunk, D_CMP, 2, NSA_KV, HEAD_DIM)
    pe_t = jnp.swapaxes(pe, 0, 1)[:, :, None, :]
    h_first = jnp.einsum('bnlcgd,cldh->bncgh', ch + pe_t[:D_CMP], w1[:, :D_CMP])
    h_second = jnp.einsum('bnlcgd,cldh->bncgh', ch + pe_t[D_CMP:], w1[:, D_CMP:])
    h = jax.nn.silu(h_first[:, :-1] + h_second[:, 1:])
    out = jnp.einsum('bncgh,chd->bncgd', h, w2)
    return out[:, :, 0], out[:, :, 1]


def nsa_compress_select(q, pos, kv_cmp, lp):
    B, tq = q.shape[:2]
    T_all = kv_cmp.shape[1]
    kc, vc = compress_blocks(kv_cmp, lp['nsa_pe_cmp'], lp['nsa_w_cmp1'], lp['nsa_w_cmp2'])
    n_c = kc.shape[1]
    n_s = -(-T_all // L_SEL)
    qg = q.reshape(B, tq, NSA_KV, HPG, HEAD_DIM)
    s = jnp.einsum('btghd,bngd->bghtn', qg, kc) * HEAD_DIM ** -0.5
    ends = jnp.arange(n_c) * D_CMP + (L_CMP - 1)
    p = masked_softmax(s, ends[None, :] <= pos[:, None])
    o_c = jnp.einsum('bghtn,bngd->btghd', p, vc.astype(jnp.float32)).reshape(B, tq, NSA_HEADS, HEAD_DIM)
    ci = np.arange(n_c)[:, None]
    sj = np.arange(n_s)[None, :]
    overlap = jnp.asarray(((ci * D_CMP < (sj + 1) * L_SEL) & (ci * D_CMP + L_CMP > sj * L_SEL)).astype(np.float32))
    imp = jnp.einsum('bghtn,ns->bgts', p, overlap)
    cur = (pos // L_SEL)[None, None, :, None]
    j = jnp.arange(n_s)[None, None, None, :]
    score = jnp.where((j == 0) | (j == cur) | (j == cur - 1), FORCE, jnp.where(j <= cur, imp, -1.0))
    _, idx = lax.top_k(score, min(N_SEL, n_s))
    valid = idx <= cur
    return o_c, idx, valid, n_s


def selection_blocks(kv, n_s):
    B, T = kv.shape[:2]
    kv = jnp.pad(kv, ((0, 0), (0, n_s * L_SEL - T), (0, 0), (0, 0), (0, 0)))
    kv = kv.reshape(B, n_s, L_SEL, 2, NSA_KV, HEAD_DIM).transpose(3, 0, 4, 1, 2, 5)
    return kv[0], kv[1]


def selection_attn(q, pos, idx, valid, kb, vb):
    B, tq = q.shape[:2]
    qg = q.reshape(B, tq, NSA_KV, HPG, HEAD_DIM)
    bi = jnp.arange(B)[:, None, None, None]
    gi = jnp.arange(NSA_KV)[None, :, None, None]
    kg = kb[bi, gi, idx]
    vg = vb[bi, gi, idx]
    n_k = idx.shape[-1]
    s = jnp.einsum('btghd,bgtnld->bghtnl', qg, kg) * HEAD_DIM ** -0.5
    tok = idx[..., None] * L_SEL + jnp.arange(L_SEL)
    mask = (valid[..., None] & (tok <= pos[None, None, :, None, None]))[:, :, None]
    p = masked_softmax(s.reshape(B, NSA_KV, HPG, tq, n_k * L_SEL), mask.reshape(B, NSA_KV, 1, tq, n_k * L_SEL))
    o = jnp.einsum('bghtm,bgtmd->btghd', p, vg.reshape(B, NSA_KV, tq, n_k * L_SEL, HEAD_DIM).astype(jnp.float32))
    return o.reshape(B, tq, NSA_HEADS, HEAD_DIM)


def window_attn(q, pos, kv, kpos):
    B, tq = q.shape[:2]
    qg = q.reshape(B, tq, NSA_KV, HPG, HEAD_DIM)
    s = jnp.einsum('btghd,bsgd->bghts', qg, kv[:, :, 0]) * HEAD_DIM ** -0.5
    mask = (kpos[None, :] <= pos[:, None]) & (kpos[None, :] > pos[:, None] - WINDOW) & (kpos[None, :] >= 0)
    p = masked_softmax(s, mask)
    o = jnp.einsum('bghts,bsgd->btghd', p, kv[:, :, 1].astype(jnp.float32))
    return o.reshape(B, tq, NSA_HEADS, HEAD_DIM)


def nsa_prompt(q, pos, kv_cmp, kv_slc, kv_win, lp):
    B, T = q.shape[:2]
    o_c, idx, valid, n_s = nsa_compress_select(q, pos, kv_cmp, lp)
    q_r = partial_rope(q, pos)
    kb, vb = selection_blocks(kv_slc, n_s)
    kw = jnp.pad(kv_win, ((0, 0), (WINDOW, 0), (0, 0), (0, 0), (0, 0)))

    def query_block(nb):
        start = nb * Q_BLK
        qb = lax.dynamic_slice_in_dim(q_r, start, Q_BLK, axis=1)
        pb = start + jnp.arange(Q_BLK)
        o_s = selection_attn(qb, pb, lax.dynamic_slice_in_dim(idx, start, Q_BLK, axis=2),
                             lax.dynamic_slice_in_dim(valid, start, Q_BLK, axis=2), kb, vb)
        kwb = lax.dynamic_slice_in_dim(kw, start, WINDOW + Q_BLK, axis=1)
        o_w = window_attn(qb, pb, kwb, start - WINDOW + jnp.arange(WINDOW + Q_BLK))
        return o_s, o_w

    o_s, o_w = lax.map(query_block, jnp.arange(T // Q_BLK))
    unblock = lambda o: jnp.moveaxis(o, 0, 1).reshape(B, T, NSA_HEADS, HEAD_DIM)
    return o_c, unblock(o_s), unblock(o_w)


def nsa_sample(q, pos, kv_cmp, kv_slc, kv_win, win_pos, lp):
    o_c, idx, valid, n_s = nsa_compress_select(q, pos, kv_cmp, lp)
    q_r = partial_rope(q, pos)
    kb, vb = selection_blocks(kv_slc, n_s)
    return o_c, selection_attn(q_r, pos, idx, valid, kb, vb), window_attn(q_r, pos, kv_win, win_pos)


def gather_pages(cache, page_table):
    g = cache[page_table]
    return g.reshape(page_table.shape[0], -1, *cache.shape[2:])


def trunk_layer(x, pos, lp, S0, shift0, past):
    B, T, _ = x.shape
    h = rms_norm(x, lp['g_mix'])
    proj = h @ lp['w_in']
    p_rwkv, p_nsa, gate_r, gate_n = jnp.split(proj, IN_SPLITS, axis=-1)
    y_r, shift_new, S_new = rwkv7_time_mix(p_rwkv, shift0, S0, lp)
    q, kc, vc, ks, vs, kw, vw, g_nsa = jnp.split(p_nsa, NSA_SPLITS, axis=-1)
    q = q.reshape(B, T, NSA_HEADS, HEAD_DIM)
    kvh = lambda z: z.reshape(B, T, NSA_KV, HEAD_DIM)
    new_cmp = jnp.stack([kvh(kc), kvh(vc)], axis=2)
    new_slc = jnp.stack([partial_rope(kvh(ks), pos), kvh(vs)], axis=2)
    new_win = jnp.stack([partial_rope(kvh(kw), pos), kvh(vw)], axis=2)
    if past is None:
        o_c, o_s, o_w = nsa_prompt(q, pos, new_cmp, new_slc, new_win, lp)
        win_state = new_win[:, T - min(WINDOW, T):]
    else:
        cmp_past, slc_past, win_buf, buf_pos = past
        win_all = jnp.concatenate([win_buf.astype(new_win.dtype), new_win], axis=1)
        win_pos = jnp.concatenate([buf_pos, pos])
        o_c, o_s, o_w = nsa_sample(q, pos, jnp.concatenate([cmp_past.astype(new_cmp.dtype), new_cmp], axis=1),
                                   jnp.concatenate([slc_past.astype(new_slc.dtype), new_slc], axis=1),
                                   win_all, win_pos, lp)
        n_all = win_all.shape[1]
        win_state = win_all[:, n_all - min(WINDOW, n_all):]
    gts = jax.nn.sigmoid(g_nsa.astype(jnp.float32)).reshape(B, T, NSA_HEADS, 3)
    o = gts[..., 0:1] * o_c + gts[..., 1:2] * o_s + gts[..., 2:3] * o_w
    y_n = o.reshape(B, T, NSA_W).astype(x.dtype)
    merged = jax.nn.sigmoid(gate_r) * (y_r @ lp['w_br_rwkv']) + jax.nn.sigmoid(gate_n) * (y_n @ lp['w_br_nsa'])
    x = x + merged @ lp['w_out']
    h2 = rms_norm(x, lp['g_ffn'])
    x = x + (jax.nn.silu(h2 @ lp['w_ffn_gate']) * (h2 @ lp['w_ffn_up'])) @ lp['w_ffn_down']
    return x, (new_cmp, new_slc, win_state, S_new, shift_new)


def setup_inputs(seed: int = 0) -> dict:
    key = jax.random.key(seed)
    keys = jax.random.split(key, 40)
    ks = iter([keys[i] for i in range(40)])
    nrm = lambda shape, scale: scale * jax.random.normal(next(ks), shape, jnp.float32)
    uni = lambda shape, lo, hi: jax.random.uniform(next(ks), shape, jnp.float32, lo, hi)
    n_pages = PAST_LEN // PAGE_SIZE
    n_used = DEC_BATCH * n_pages
    n_pool = n_used + (n_used + 3) // 4
    w_buf = min(WINDOW, PAST_LEN)
    out = {}
    out['x_prompt'] = nrm((BATCH, SEQ, D_MODEL), 1.0)
    out['x_sample'] = nrm((DEC_BATCH, DEC_SEQ, D_MODEL), 1.0)
    out['cache_cmp_kv'] = nrm((DEPTH, n_pool, PAGE_SIZE, 2, NSA_KV, HEAD_DIM), 1.0)
    out['cache_slc_kv'] = nrm((DEPTH, n_pool, PAGE_SIZE, 2, NSA_KV, HEAD_DIM), 1.0)
    out['cache_win_kv'] = nrm((DEPTH, DEC_BATCH, w_buf, 2, NSA_KV, HEAD_DIM), 1.0)
    out['state_rwkv'] = nrm((DEPTH, DEC_BATCH, RWKV_HEADS, RWKV_HEAD, RWKV_HEAD), 0.5)
    out['state_rwkv_shift'] = nrm((DEPTH, DEC_BATCH, RWKV_COLS), 1.0)
    perm = jax.random.permutation(next(ks), n_pool)
    out['page_table'] = perm[:n_used].reshape(DEC_BATCH, n_pages).astype(jnp.int32)
    out['g_mix'] = 1.0 + nrm((DEPTH, D_MODEL), 0.02)
    out['w_in'] = nrm((DEPTH, D_MODEL, IN_COLS), D_MODEL ** -0.5)
    out['rwkv_mu'] = uni((DEPTH, RWKV_COLS), 0.0, 1.0)
    out['rwkv_w0'] = uni((DEPTH, RWKV_W), -5.0, -0.5)
    out['rwkv_w_decay'] = nrm((DEPTH, DECAY_RANK, RWKV_W), 0.5 * DECAY_RANK ** -0.5)
    out['rwkv_a0'] = nrm((DEPTH, RWKV_W), 0.5)
    out['rwkv_w_aaa'] = nrm((DEPTH, AAA_RANK, RWKV_W), AAA_RANK ** -0.5)
    out['rwkv_w_gate'] = nrm((DEPTH, GATE_RANK, RWKV_W), GATE_RANK ** -0.5)
    out['rwkv_k_k'] = 0.85 + nrm((DEPTH, RWKV_W), 0.05)
    out['rwkv_k_a'] = 1.0 + nrm((DEPTH, RWKV_W), 0.05)
    out['rwkv_r_k'] = nrm((DEPTH, RWKV_HEADS, RWKV_HEAD), 0.1)
    out['rwkv_ln_w'] = 1.0 + nrm((DEPTH, RWKV_W), 0.02)
    out['rwkv_ln_b'] = nrm((DEPTH, RWKV_W), 0.01)
    out['nsa_pe_cmp'] = nrm((DEPTH, 2, L_CMP, HEAD_DIM), 0.1)
    out['nsa_w_cmp1'] = nrm((DEPTH, 2, L_CMP, HEAD_DIM, CMP_HID), (L_CMP * HEAD_DIM) ** -0.5)
    out['nsa_w_cmp2'] = nrm((DEPTH, 2, CMP_HID, HEAD_DIM), CMP_HID ** -0.5)
    out['w_br_rwkv'] = nrm((DEPTH, RWKV_W, D_MODEL), RWKV_W ** -0.5)
    out['w_br_nsa'] = nrm((DEPTH, NSA_W, D_MODEL), NSA_W ** -0.5)
    out['w_out'] = nrm((DEPTH, D_MODEL, D_MODEL), D_MODEL ** -0.5)
    out['g_ffn'] = 1.0 + nrm((DEPTH, D_MODEL), 0.02)
    out['w_ffn_gate'] = nrm((DEPTH, D_MODEL, D_FF), D_MODEL ** -0.5)
    out['w_ffn_up'] = nrm((DEPTH, D_MODEL, D_FF), D_MODEL ** -0.5)
    out['w_ffn_down'] = nrm((DEPTH, D_FF, D_MODEL), D_FF ** -0.5)
    out['g_final'] = 1.0 + nrm((D_MODEL,), 0.02)
    return out


def reference(x_prompt, x_sample, cache_cmp_kv, cache_slc_kv, cache_win_kv, state_rwkv, state_rwkv_shift,
              page_table, g_mix, w_in, rwkv_mu, rwkv_w0, rwkv_w_decay, rwkv_a0, rwkv_w_aaa, rwkv_w_gate,
              rwkv_k_k, rwkv_k_a, rwkv_r_k, rwkv_ln_w, rwkv_ln_b, nsa_pe_cmp, nsa_w_cmp1, nsa_w_cmp2,
              w_br_rwkv, w_br_nsa, w_out, g_ffn, w_ffn_gate, w_ffn_up, w_ffn_down, g_final):
    B, T = x_prompt.shape[:2]
    TS = x_sample.shape[1]
    past_len = page_table.shape[1] * cache_cmp_kv.shape[2]
    pos_p = jnp.arange(T)
    pos_s = past_len + jnp.arange(TS)
    w_buf = cache_win_kv.shape[2]
    buf_pos = past_len - w_buf + jnp.arange(w_buf)
    xp, xs = x_prompt, x_sample
    outs_p, outs_s = [], []
    for l in range(DEPTH):
        lp = {'g_mix': g_mix[l], 'w_in': w_in[l], 'mu': rwkv_mu[l], 'w0': rwkv_w0[l],
              'w_decay': rwkv_w_decay[l], 'a0': rwkv_a0[l], 'w_aaa': rwkv_w_aaa[l], 'w_gate': rwkv_w_gate[l],
              'k_k': rwkv_k_k[l], 'k_a': rwkv_k_a[l], 'r_k': rwkv_r_k[l], 'ln_w': rwkv_ln_w[l], 'ln_b': rwkv_ln_b[l],
              'nsa_pe_cmp': nsa_pe_cmp[l], 'nsa_w_cmp1': nsa_w_cmp1[l], 'nsa_w_cmp2': nsa_w_cmp2[l],
              'w_br_rwkv': w_br_rwkv[l], 'w_br_nsa': w_br_nsa[l], 'w_out': w_out[l], 'g_ffn': g_ffn[l],
              'w_ffn_gate': w_ffn_gate[l], 'w_ffn_up': w_ffn_up[l], 'w_ffn_down': w_ffn_down[l]}
        xp, st_p = trunk_layer(xp, pos_p, lp, jnp.zeros((B, RWKV_HEADS, RWKV_HEAD, RWKV_HEAD), jnp.float32),
                               jnp.zeros((B, RWKV_COLS), xp.dtype), None)
        past = (gather_pages(cache_cmp_kv[l], page_table), gather_pages(cache_slc_kv[l], page_table),
                cache_win_kv[l], buf_pos)
        xs, st_s = trunk_layer(xs, pos_s, lp, state_rwkv[l], state_rwkv_shift[l], past)
        outs_p.append(st_p)
        outs_s.append(st_s)
    stack = lambda outs, i: jnp.stack([o[i] for o in outs])
    y_prompt = rms_norm(xp, g_final)
    y_sample = rms_norm(xs, g_final)
    return (y_prompt, y_sample,
            stack(outs_p, 0), stack(outs_p, 1), stack(outs_p, 2), stack(outs_p, 3), stack(outs_p, 4),
            stack(outs_s, 0), stack(outs_s, 1), stack(outs_s, 2), stack(outs_s, 3), stack(outs_s, 4))
```

```python
from contextlib import ExitStack
import numpy as np
import concourse.bass as bass
import concourse.mybir as mybir
from concourse.bass_utils import run_bass_kernel_spmd

F32 = mybir.dt.float32
BF16 = mybir.dt.bfloat16
I32 = mybir.dt.int32
ALU = mybir.AluOpType
AF = mybir.ActivationFunctionType
AX = mybir.AxisListType

ENGS = ("pe", "act", "dve", "pool", "sp")

NCTX, NMAIN, NSMP = 16, 16, 4
NT = NCTX + NMAIN + NSMP
NOUT = NMAIN + NSMP
D = 1024
RW = 1824
CH = 32
EXPM05 = float(np.exp(-0.5))


class Res:
    __slots__ = ("name", "writer", "readers", "dsem", "ndma", "dma_rd")

    def __init__(self, name):
        self.name = name
        self.writer = None
        self.readers = []
        self.dsem = None
        self.ndma = 0
        self.dma_rd = []


class Prog:
    def __init__(self, nc, stack):
        self.nc = nc
        self.stack = stack
        self.streams = {e: [] for e in ENGS}
        self.count = {e: 0 for e in ENGS}
        self.sem = {e: stack.enter_context(nc.semaphore("cnt_" + e)) for e in ENGS if e != "sp"}
        self.waited = {e: {} for e in ENGS}
        self.res = {}
        self.n_inst = 0

    def R(self, name):
        r = self.res.get(name)
        if r is None:
            r = Res(name)
            self.res[name] = r
        return r

    def _need(self, eng, src, val, out):
        if src == eng and eng == "pe":
            return
        w = self.waited[eng]
        if w.get(src, 0) >= val:
            return
        w[src] = val
        out.append(("c", src, val))

    def _need_dma(self, eng, r, out):
        if r.dsem is None or r.ndma == 0:
            return
        w = self.waited[eng]
        key = ("d", r.name)
        val = 16 * r.ndma
        if w.get(key, 0) >= val:
            return
        w[key] = val
        out.append(("d", r.dsem, val))

    def _deps(self, eng, reads, writes):
        waits = []
        for rn in reads:
            r = self.R(rn)
            if r.writer is not None:
                self._need(eng, r.writer[0], r.writer[1], waits)
            self._need_dma(eng, r, waits)
        for rn in writes:
            r = self.R(rn)
            if r.writer is not None:
                self._need(eng, r.writer[0], r.writer[1], waits)
            for (e2, c2) in r.readers:
                self._need(eng, e2, c2, waits)
            self._need_dma(eng, r, waits)
            for (dr, val) in r.dma_rd:
                w = self.waited[eng]
                key = ("d", dr.name)
                if w.get(key, 0) < val:
                    w[key] = val
                    waits.append(("d", dr.dsem, val))
        return waits

    def op(self, eng, fn, reads=(), writes=()):
        waits = self._deps(eng, reads, writes)
        self.count[eng] += 1
        c = self.count[eng]
        for rn in reads:
            self.R(rn).readers.append((eng, c))
        for rn in writes:
            r = self.R(rn)
            r.writer = (eng, c)
            r.readers = []
            r.dma_rd = []
        self.streams[eng].append((waits, fn, ("c", eng)))
        self.n_inst += 1

    def dma(self, q, fn, res, reads=(), writes=()):
        r = self.R(res)
        if r.dsem is None:
            r.dsem = self.stack.enter_context(self.nc.semaphore("d_" + r.name))
        waits = self._deps(q, reads, writes)
        r.ndma += 1
        for rn in reads:
            if rn != res:
                self.R(rn).dma_rd.append((r, 16 * r.ndma))
        self.streams[q].append((waits, fn, ("d", r.dsem)))
        self.n_inst += 1

    def barrier(self):
        for eng in ENGS:
            waits = []
            for r in self.res.values():
                self._need_dma(eng, r, waits)
            for e in ENGS:
                if e != "sp" and e != eng and self.count[e] > 0:
                    self._need(eng, e, self.count[e], waits)
            if waits:
                self.streams[eng].append((waits, None, None))

    def emit(self):
        prog = self

        def run(engname, engobj):
            for (waits, fn, inc) in prog.streams[engname]:
                for w in waits:
                    if w[0] == "c":
                        engobj.wait_ge(prog.sem[w[1]], w[2])
                    else:
                        engobj.wait_ge(w[1], w[2])
                if fn is None:
                    continue
                ins = fn(engobj)
                if inc[0] == "c":
                    ins.then_inc(prog.sem[inc[1]], 1)
                else:
                    ins.then_inc(inc[1], 16)

        with self.nc.Block() as block:
            @block.tensor
            def _(e):
                run("pe", e)

            @block.scalar
            def _(e):
                run("act", e)

            @block.vector
            def _(e):
                run("dve", e)

            @block.gpsimd
            def _(e):
                run("pool", e)

            @block.sync
            def _(e):
                run("sp", e)


def build_program(do_rwkv=True, do_dense=True, do_nsa=True, dbg=False, nsa_stop=99, nsa_m=NMAIN, nsa_seq=16, dbg_br=False):
    nc = bass.Bass("TRN2", target_bir_lowering=False)
    din = lambda name, shape, dt=F32: nc.dram_tensor(name, list(shape), dt, kind="ExternalInput").ap()
    dout = lambda name, shape, dt=F32: nc.dram_tensor(name, list(shape), dt, kind="ExternalOutput").ap()

    xrows = din("xrows", [NT * 128, D])
    w_in = din("w_in", [D, 5176])
    g_mix = din("g_mix", [128, 8])
    mu_b = din("mu_b", [128, RW])
    par_b = din("par_b", [128, 7, 512])
    w_decay = din("w_decay", [64, 512])
    w_aaa = din("w_aaa", [64, 512])
    w_gate = din("w_gate", [160, 512])
    ident = din("ident", [128, 128])
    masks_bf = din("masks_bf", [128, 3, 128])
    masks_f = din("masks_f", [128, 2, 128])
    chunk_ind = din("chunk_ind", [128, 4])
    rowm_d = din("rowm", [128, 2])
    rowvalid = din("rowvalid", [128, NT])
    cs_tab = din("cs_tab", [128, NT, 2, 8])
    shift0 = din("shift0", [16, RW])
    state0 = din("state0", [16, 8, 64, 64])
    cwin = din("cwin", [16, 512, 256])
    valid_d = din("valid_t", [128, NCTX + NMAIN])
    validc_d = din("valid_c", [128, 2])
    ovl_d = din("ovl", [128, 2, 64])
    E_d = din("E_ind", [64, (NCTX + NMAIN) * 128])
    ones_d = din("ones_row", [1, (NCTX + NMAIN) * 128])
    w1_d = din("w1h", [64, 2, 32, 128])
    pe_d = din("peh", [64, 2, 32])
    w2_d = din("w2h", [128, 2, 64])
    maskc_d = din("maskc", [128, 2, NMAIN * 128])
    allowed_d = din("allowed", [128, NMAIN, 64])
    fbias_d = din("fbias", [128, NMAIN, 64])
    tri_d = din("tri", [128, 2, 128])
    page_tab = din("page_tab", [1, 256], I32)
    iota_p = din("iota_p", [128, 1])
    ovl_abs = din("ovl_abs", [128, 64])
    allowed_s = din("allowed_s", [128, 64])
    fbias_s = din("fbias_s", [128, 64])
    maskw0 = din("maskw0", [128, 32])
    cache_c = din("cache_c", [2560, 128, 256])
    cache_s = din("cache_s", [2560, 128, 256])

    w_br_rwkv = din("w_br_rwkv", [512, D])
    w_br_nsa = din("w_br_nsa", [512, D])
    w_out = din("w_out", [D, D])
    g_ffn = din("g_ffn", [128, 8])
    w_ffn_gate = din("w_ffn_gate", [D, 2816])
    w_ffn_up = din("w_ffn_up", [D, 2816])
    w_ffn_down = din("w_ffn_down", [2816, D])
    g_fin_b = din("g_fin_b", [128, D])
    x2d = nc.dram_tensor("x2_scratch", [NOUT * 128, D], F32).ap()
    o_y = dout("o_y", [NOUT * 128, D])
    o_kv = dout("o_kv", [NOUT * 128, 768])
    o_swin = dout("o_swin", [16, 512, 256])
    o_prow = dout("o_prow", [NOUT * 128, RW]) if dbg else None
    o_pshift = dout("o_pshift", [1, RW])
    o_sshift = dout("o_sshift", [16, RW])
    o_pstate = dout("o_pstate", [8, 64, 64])
    o_sstate = dout("o_sstate", [16, 8, 64, 64])
    o_yr = dout("o_yr", [NOUT * 128, 512]) if dbg else None
    o_yn = dout("o_yn", [NOUT * 128, 512]) if (dbg and do_nsa) else None
    o_br = dout("o_br", [NMAIN * 128, 3, 512]) if (dbg and do_nsa) else None
    o_vca = dout("o_vca", [128, 2, 129]) if (dbg and do_nsa) else None
    o_kca = dout("o_kca", [65, 256]) if (dbg and do_nsa) else None
    o_ksa = dout("o_ksa", [128, 4096]) if (dbg and do_nsa) else None
    o_vsa = dout("o_vsa", [128, 32, 2, 65]) if (dbg and do_nsa) else None

    with ExitStack() as st:
        P = Prog(nc, st)
        cur_stack = [st]
        sb = lambda name, shape, dt=F32: cur_stack[-1].enter_context(nc.sbuf_tensor("s_" + name, list(shape), dt))
        psum = lambda name, shape, dt=F32: st.enter_context(nc.psum_tensor(name, list(shape), dt))

        def MM(out, lhsT, rhs, r, w, start=True, stop=True):
            P.op("pe", lambda e: e.matmul(out, lhsT=lhsT, rhs=rhs, start=start, stop=stop), r, w)

        def TR(out, in_, idn, r, w):
            P.op("pe", lambda e: e.transpose(out=out, in_=in_, identity=idn), r, w)

        def TT(eng, out, in0, in1, op, r, w):
            P.op(eng, lambda e: e.tensor_tensor(out=out, in0=in0, in1=in1, op=op), r, w)

        def TS(eng, out, in0, s1, s2, op0, op1, r, w):
            if s2 is None:
                P.op(eng, lambda e: e.tensor_scalar(out=out, in0=in0, scalar1=s1, scalar2=None, op0=op0), r, w)
            else:
                P.op(eng, lambda e: e.tensor_scalar(out=out, in0=in0, scalar1=s1, scalar2=s2, op0=op0, op1=op1), r, w)

        def STT(eng, out, in0, scalar, in1, op0, op1, r, w):
            P.op(eng, lambda e: e.scalar_tensor_tensor(out=out, in0=in0, scalar=scalar, in1=in1, op0=op0, op1=op1), r, w)

        def ACT(out, in_, func, r, w, bias=None, scale=None, accum=None):
            kw = {}
            if bias is not None:
                kw["bias"] = bias
            if scale is not None:
                kw["scale"] = scale
            if accum is not None:
                kw["accum_out"] = accum
            P.op("act", lambda e: e.activation(out=out, in_=in_, func=func, **kw), r, w)

        def CP(eng, out, in_, r, w):
            if eng == "act":
                P.op("act", lambda e: e.copy(out=out, in_=in_), r, w)
            else:
                P.op(eng, lambda e: e.tensor_copy(out=out, in_=in_), r, w)

        def RED(eng, out, in_, op, r, w):
            P.op(eng, lambda e: e.tensor_reduce(out=out, in_=in_, axis=AX.X, op=op), r, w)

        def RECIP(out, in_, r, w):
            P.op("dve", lambda e: e.reciprocal(out=out, in_=in_), r, w)

        def MSET(eng, ap, val, w):
            P.op(eng, lambda e: e.memset(ap, val), (), w)

        def DMA(q, out, in_, res, r=(), w=()):
            P.dma(q, lambda e: e.dma_start(out=out, in_=in_), res, r, w)

        identf = sb("identf", [128, 128])
        identb = sb("identb", [128, 128], BF16)
        gmix = sb("gmix", [128, 8])
        rvalid = sb("rvalid", [128, NT])
        cstab = sb("cstab", [128, NT, 2, 8])

        DMA("sp", identf[:], ident[:, :], "identf", w=["identf"])
        DMA("pool", identb[:], ident[:, :], "identb", w=["identb"])
        DMA("sp", gmix[:], g_mix[:, :], "gmix", w=["gmix"])
        w_in_v = w_in.rearrange("(k p) n -> p k n", p=128)
        DMA("sp", rvalid[:], rowvalid[:, :], "rvalid", w=["rvalid"])
        DMA("sp", cstab[:], cs_tab[:, :, :, :], "cstab", w=["cstab"])

        ps_tr = psum("ps_tr", [128, 8, 128], BF16)
        NB = 6
        banks = [psum("bank%d" % i, [128, 512]) for i in range(NB)]
        banks_id = {id(b): i for i, b in enumerate(banks)}
        bank_i = [0]

        def bank():
            i = bank_i[0] % NB
            bank_i[0] += 1
            return banks[i], "bank%d" % i

        ps_f = psum("ps_f", [128, 512])

        xt = [sb("xt%d" % i, [128, D]) for i in range(2)]
        sq = sb("sq", [128, D], BF16)
        ss = sb("ss", [128, 1])
        rstd = sb("rstd", [128, 1])
        xn = sb("xn", [128, D], BF16)
        hT = sb("hT", [128, 8, 128], BF16)
        yrd = nc.dram_tensor("yr_scratch", [128, 4, NOUT * 128], BF16).ap()
        ynd = nc.dram_tensor("yn_scratch", [128, 4, NOUT * 128], BF16).ap()

        def load_norm(t, X=None, xn_=None, hdst=None, hname="hT"):
            if X is None:
                X = xt[t % 2]
                xn_ = "xt%d" % (t % 2)
            if hdst is None:
                hdst = hT[:]
            DMA("sp", X[:], xrows[t * 128:(t + 1) * 128, :], xn_, w=[xn_])
            ACT(sq[:], X[:], AF.Square, [xn_], ["sq", "ss"], accum=ss[:])
            TS("dve", rstd[:], ss[:], 1.0 / D, 1e-6, ALU.mult, ALU.add, ["ss"], ["rstd"])
            P.op("act", lambda e: e.sqrt(out=rstd[:], in_=rstd[:]), ["rstd"], ["rstd"])
            RECIP(rstd[:], rstd[:], ["rstd"], ["rstd"])
            TS("dve", xn[:], X[:], rstd[:, 0:1], None, ALU.mult, None, [xn_, "rstd"], ["xn"])
            for k in range(8):
                TR(ps_tr[:, k, :], xn[:, k * 128:(k + 1) * 128], identb[:], ["xn", "identb"], ["ps_tr"])
            TT("dve", hdst, ps_tr[:], gmix[:].unsqueeze(2).to_broadcast([128, 8, 128]), ALU.mult,
               ["ps_tr", "gmix"], [hname])

        st_kv = ExitStack()
        cur_stack.append(st_kv)
        NPT = NCTX + NMAIN
        wkv = sb("wkv", [128, 8, 768], BF16)
        kv = [sb("kv%d" % i, [128, 768]) for i in range(2)]
        kvb = sb("kvb", [128, 768], BF16)
        rt = sb("rt", [128, 6, 2, 2, 8])
        DMA("pool", wkv[:], w_in_v[:, :, RW + 512:RW + 512 + 768], "wkv", w=["wkv"])
        DMA("act", o_swin[:, 0:504, :], cwin[:, 8:512, :], "swin_copy")
        if do_nsa:
            KsA = [sb("KsA%d" % g, [128, NPT * 128], BF16) for g in range(2)]
            KwA = [sb("KwA%d" % g, [65, NPT * 128], BF16) for g in range(2)]
            VsA = sb("VsA", [128, NPT, 2, 65], BF16)
            VwA = sb("VwA", [128, NPT, 2, 65], BF16)
            kcT = sb("kcT", [128, NPT * 128], BF16)
            vcT = sb("vcT", [128, NPT * 128], BF16)
            KcA = [sb("KcA%d" % g, [65, 256], BF16) for g in range(2)]
            VcA = [sb("VcA%d" % g, [128, 2, 129], BF16) for g in range(2)]
            validt = sb("validt", [128, NPT])
            validc = sb("validc", [128, 2])
            ovl = sb("ovl", [128, 2, 64], BF16)
            rmax = sb("rmax", [128, 12])
            sqk = sb("sqk", [128, 768])
            r12 = sb("r12", [128, 12])
            DMA("sp", validt[:], valid_d[:, :], "validt", w=["validt"])
            DMA("sp", validc[:], validc_d[:, :], "validc", w=["validc"])
            DMA("pool", ovl[:], ovl_d[:, :, :], "ovl", w=["ovl"])
            for g in range(2):
                for c0 in range(0, NPT * 128, 1024):
                    DMA("pool", KsA[g][64:128, c0:c0 + 1024], E_d[:, c0:c0 + 1024], "KsA%d" % g, w=["KsA%d" % g])
                    DMA("pool", KwA[g][64:65, c0:c0 + 1024], ones_d[0:1, c0:c0 + 1024], "KwA%d" % g, w=["KwA%d" % g])
                DMA("pool", KcA[g][64:65, :], ones_d[0:1, 0:256], "KcA%d" % g, w=["KcA%d" % g])
                CP("pool", VsA[:, :, g, 64], validt[:], ["validt"], ["VsA"])
                CP("pool", VwA[:, :, g, 64], validt[:], ["validt"], ["VwA"])
                CP("pool", VcA[g][:, :, 64], validc[:], ["validc"], ["VcA%d" % g])
                CP("pool", VcA[g][:, :, 65:129], ovl[:], ["ovl"], ["VcA%d" % g])
            MSET("dve", rmax[:], 0.0, ["rmax"])
        tiles1 = list(range(NT)) if do_nsa else list(range(NCTX, NT))
        for t in tiles1:
            to = t - NCTX
            is_smp = t >= NPT
            load_norm(t)
            KV = kv[t % 2]
            kvn = "kv%d" % (t % 2)
            for half in range(2):
                n0 = half * 384
                pb, pbn = bank()
                for k in range(8):
                    MM(pb[:, 0:384], hT[:, k, :], wkv[:, k, n0:n0 + 384], ["hT", "wkv"], [pbn], start=(k == 0), stop=(k == 7))
                CP("act", KV[:, n0:n0 + 384], pb[:, 0:384], [pbn], [kvn])
            kview = KV[:, 256:768].rearrange("p (a b) -> p a b", a=2)[:, :, 0:128].rearrange("p a (g d) -> p a g d", g=2)
            x1 = kview[:, :, :, 0:8]
            x2 = kview[:, :, :, 8:16]
            cosb = cstab[:, t, 0, :].unsqueeze(1).unsqueeze(1).to_broadcast([128, 2, 2, 8])
            sinb = cstab[:, t, 1, :].unsqueeze(1).unsqueeze(1).to_broadcast([128, 2, 2, 8])
            TT("dve", rt[:, 0], x1, cosb, ALU.mult, [kvn, "cstab"], ["rt0"])
            TT("dve", rt[:, 1], x2, sinb, ALU.mult, [kvn, "cstab"], ["rt1"])
            TT("dve", rt[:, 2], x2, cosb, ALU.mult, [kvn, "cstab"], ["rt2"])
            TT("dve", rt[:, 3], x1, sinb, ALU.mult, [kvn, "cstab"], ["rt3"])
            TT("dve", x1, rt[:, 0], rt[:, 1], ALU.subtract, ["rt0", "rt1"], [kvn])
            TT("dve", x2, rt[:, 2], rt[:, 3], ALU.add, ["rt2", "rt3"], [kvn])
            if t >= NCTX:
                DMA("sp", o_kv[to * 128:(to + 1) * 128, :], KV[:], kvn, r=[kvn])
            if is_smp:
                for cb in range(4):
                    j = 4 * (t - NPT) + cb
                    DMA("act", o_swin[j, 504:512, :], KV[32 * cb + 24:32 * cb + 32, 512:768], kvn, r=[kvn])
            if do_nsa and not is_smp and nsa_stop >= 1:
                tc_ = slice(t * 128, (t + 1) * 128)
                CP("act", kvb[:], KV[:], [kvn], ["kvb"])
                CP("pool", VsA[:, t, :, 0:64], KV[:, 384:512].rearrange("p (g d) -> p g d", g=2), [kvn], ["VsA"])
                CP("pool", VwA[:, t, :, 0:64], KV[:, 640:768].rearrange("p (g d) -> p g d", g=2), [kvn], ["VwA"])
                for g in range(2 if nsa_stop >= 1.5 else 0):
                    TR(ps_tr[0:64, g, :], kvb[:, 256 + g * 64:256 + (g + 1) * 64], identb[:], ["kvb", "identb"], ["ps_tr"])
                    TR(ps_tr[0:64, 2 + g, :], kvb[:, 512 + g * 64:512 + (g + 1) * 64], identb[:], ["kvb", "identb"], ["ps_tr"])
                if nsa_stop >= 1.5:
                    TR(ps_tr[:, 4, :], kvb[:, 0:128], identb[:], ["kvb", "identb"], ["ps_tr"])
                    TR(ps_tr[:, 5, :], kvb[:, 128:256], identb[:], ["kvb", "identb"], ["ps_tr"])
                for g in range(2 if nsa_stop >= 1.5 else 0):
                    CP("dve", KsA[g][0:64, tc_], ps_tr[0:64, g, :], ["ps_tr"], ["KsA%d" % g])
                    CP("dve", KwA[g][0:64, tc_], ps_tr[0:64, 2 + g, :], ["ps_tr"], ["KwA%d" % g])
                if nsa_stop >= 1.5:
                    CP("dve", kcT[:, tc_], ps_tr[:, 4, :], ["ps_tr"], ["kcT"])
                    CP("dve", vcT[:, tc_], ps_tr[:, 5, :], ["ps_tr"], ["vcT"])
                if nsa_stop >= 2:
                    TT("pool", sqk[:], KV[:], KV[:], ALU.mult, [kvn], ["sqk"])
                    RED("dve", r12[:], sqk[:].rearrange("p (a d) -> p a d", a=12), ALU.add, ["sqk"], ["r12"])
                    TT("dve", rmax[:], rmax[:], r12[:], ALU.max, ["rmax", "r12"], ["rmax"])
        if do_nsa and nsa_stop >= 2:
            w1d = sb("w1d", [128, 2, 32, 128], BF16)
            peT = sb("peT", [64, 2, 32], BF16)
            w2s = sb("w2s", [128, 2, 64], BF16)
            bcs = sb("bcs", [128, 2])
            hc = sb("hc", [128, 255], BF16)
            kct = sb("kct", [128, 64])
            DMA("pool", w1d[0:64], w1_d[:, :, :, :], "w1d", w=["w1d"])
            DMA("pool", w1d[64:128], w1_d[:, :, :, :], "w1d", w=["w1d"])
            DMA("pool", peT[:], pe_d[:, :, :], "peT", w=["peT"])
            DMA("pool", w2s[:], w2_d[:, :, :], "w2s", w=["w2s"])
            for c in range(2):
                for l in range(32):
                    MM(ps_f[:, c:c + 1], w1d[0:64, c, l, :], peT[:, c, l:l + 1], ["w1d", "peT"], ["ps_f"], start=(l == 0), stop=(l == 31))
            CP("dve", bcs[:], ps_f[:, 0:2], ["ps_f"], ["bcs"])
            for c in range(2):
                src = kcT if c == 0 else vcT
                srcn = "kcT" if c == 0 else "vcT"
                srcv = src[:].rearrange("p (n l) -> p l n", l=16)
                for g in range(2):
                    gs = slice(64 * g, 64 * g + 64)
                    hb, hbn = bank()
                    for l in range(16):
                        MM(hb[:, 0:255], w1d[gs, c, l, :], srcv[gs, l, 0:255], ["w1d", srcn], [hbn], start=(l == 0), stop=False)
                        MM(hb[:, 0:255], w1d[gs, c, 16 + l, :], srcv[gs, l, 1:256], ["w1d", srcn], [hbn], start=False, stop=(l == 15))
                    ACT(hc[:], hb[:, 0:255], AF.Silu, [hbn, "bcs"], ["hc"], bias=bcs[:, c:c + 1])
                    if c == 0:
                        ob, obn = bank()
                        MM(ob[0:64, 0:255], w2s[:, 0, :], hc[:], ["w2s", "hc"], [obn])
                        CP("dve", KcA[g][0:64, 0:255], ob[0:64, 0:255], [obn], ["KcA%d" % g])
                        MSET("pool", KcA[g][0:64, 255:256], 0.0, ["KcA%d" % g])
                        for nt_, nn in ((0, 128), (1, 127)):
                            ob2, ob2n = bank()
                            MM(ob2[0:nn, 0:64], hc[:, nt_ * 128:nt_ * 128 + nn], w2s[:, 0, :], ["hc", "w2s"], [ob2n])
                            ACT(kct[0:nn, :], ob2[0:nn, 0:64], AF.Square, [ob2n], ["kct", "r12"], accum=r12[0:nn, 0:1])
                            TT("dve", rmax[0:nn, 0:1], rmax[0:nn, 0:1], r12[0:nn, 0:1], ALU.max, ["rmax", "r12"], ["rmax"])
                    else:
                        for nt_, nn in ((0, 128), (1, 127)):
                            ob2, ob2n = bank()
                            MM(ob2[0:nn, 0:64], hc[:, nt_ * 128:nt_ * 128 + nn], w2s[:, 1, :], ["hc", "w2s"], [ob2n])
                            CP("dve", VcA[g][0:nn, nt_, 0:64], ob2[0:nn, 0:64], [ob2n], ["VcA%d" % g])
        if do_nsa and dbg and nsa_stop >= 2:
            DMA("pool", o_vca[:, :, :], VcA[0][:], "VcA0", r=["VcA0"])
            DMA("pool", o_kca[:, :], KcA[0][:], "KcA0", r=["KcA0"])
            DMA("pool", o_ksa[:, :], KsA[0][:], "KsA0", r=["KsA0"])
            DMA("pool", o_vsa[:, :, :, :], VsA[:], "VsA", r=["VsA"])
        if do_nsa and nsa_stop >= 3:
            km = sb("km", [128, 1])
            k1 = sb("k1", [1, 1])
            onesf = sb("onesf", [1, 128])
            kmax8 = sb("kmax8", [128, 1])
            MSET("pool", onesf[:], 1.0, ["onesf"])
            RED("dve", km[:], rmax[:], ALU.max, ["rmax"], ["km"])
            TR(ps_f[0:1, 128:256], km[:, 0:1], identf[:], ["km", "identf"], ["ps_f"])
            RED("dve", k1[:], ps_f[0:1, 128:256], ALU.max, ["ps_f"], ["k1"])
            P.op("act", lambda e: e.sqrt(out=k1[:], in_=k1[:]), ["k1"], ["k1"])
            MM(ps_f[:, 300:301], onesf[0:1, :], k1[0:1, 0:1], ["onesf", "k1"], ["ps_f"])
            TS("dve", kmax8[:], ps_f[:, 300:301], 0.125, None, ALU.mult, None, ["ps_f"], ["kmax8"])
        if do_nsa and nsa_stop >= 4:
            BIG = 30000.0
            wq = sb("wq", [128, 8, 536], BF16)
            DMA("pool", wq[:, :, 0:512], w_in_v[:, :, RW:RW + 512], "wq", w=["wq"])
            DMA("pool", wq[:, :, 512:536], w_in_v[:, :, 3104:3128], "wq", w=["wq"])
            maskc = sb("maskc", [128, 2, NMAIN * 128], BF16)
            allowed = sb("allowed", [128, NMAIN, 64])
            fbias = sb("fbias", [128, NMAIN, 64])
            tri = sb("tri", [128, 2, 128], BF16)
            DMA("pool", maskc[:], maskc_d[:, :, :], "maskc", w=["maskc"])
            DMA("sp", allowed[:], allowed_d[:, :, :], "allowed", w=["allowed"])
            DMA("sp", fbias[:], fbias_d[:, :, :], "fbias", w=["fbias"])
            DMA("pool", tri[:], tri_d[:, :, :], "tri", w=["tri"])
            qf = sb("qf", [128, 512])
            qr = sb("qr", [128, 512])
            gts = sb("gts", [128, 24])
            qsq = sb("qsq", [128, 512])
            qss = sb("qss", [128, 8])
            negc = sb("negc", [128, 8])
            negcb = sb("negcb", [128, 8])
            QC = sb("QC", [128, 8, 65], BF16)
            QS = sb("QS", [128, 8, 128], BF16)
            QcT = sb("QcT", [65, 8, 128], BF16)
            QsT = sb("QsT", [128, 8, 128], BF16)
            PT = [sb("PT%d" % i, [128, 512], BF16) for i in range(3)]
            pt_i = [0]
            imp = sb("imp", [128, 64])
            sc = sb("sc", [128, 64])
            sc2 = sb("sc2", [128, 64])
            mx8 = sb("mx8", [128, 8])
            sel = sb("sel", [128, 64])
            rden = sb("rden", [128, 4])
            coef = sb("coef", [128, 4])
            ynf = sb("ynf", [128, 512])
            ynb = sb("ynb", [128, 512], BF16)
            ynTt = sb("ynTt", [128, 4, 128], BF16)
            qrt = sb("qrt", [128, 4, 8, 8])
            dbgo = sb("dbgo", [128, 3, 512]) if dbg_br else None
            zl = sb("zl", [128, 128], BF16)
            zr = sb("zr", [128, 512], BF16)
            MSET("pool", zl[:], 0.0, ["zl"])
            MSET("pool", zr[:], 0.0, ["zr"])

            def zero_psum(ps, psn, ncol):
                MSET("dve", ps[:, 0:ncol], 0.0, [psn])
            hbank = [banks.pop() for _ in range(4)]
            hbn = ["bank%d" % banks_id[id(b)] for b in hbank]
            sbanks = [(b, "bank%d" % banks_id[id(b)]) for b in banks] + [(ps_f, "ps_f")]
            bank_i[0] = 0

            def bank():
                i = bank_i[0] % len(sbanks)
                bank_i[0] += 1
                return sbanks[i]

            def nextPT():
                i = pt_i[0] % 3
                pt_i[0] += 1
                return PT[i], "PT%d" % i

            def accum_branch(g, bi, first, PS=slice(0, 128)):
                for hh in range(4):
                    TS("dve", rden[PS, hh:hh + 1], hbank[hh][PS, 64:65], 1e-30, None, ALU.max, None, [hbn[hh]], ["rden"])
                RECIP(rden[PS, :], rden[PS, :], ["rden"], ["rden"])
                TT("dve", coef[PS, :], rden[PS, :], gts[PS, :].rearrange("p (h b) -> p h b", b=3)[:, 4 * g:4 * g + 4, bi], ALU.mult,
                   ["rden", "gts"], ["coef"])
                for hh in range(4):
                    h = 4 * g + hh
                    if dbg_br:
                        TS("dve", dbgo[PS, bi, h * 64:(h + 1) * 64], hbank[hh][PS, 0:64], rden[PS, hh:hh + 1], None, ALU.mult, None, [hbn[hh], "rden"], ["dbgo"])
                    ysl = ynf[PS, h * 64:(h + 1) * 64]
                    if first:
                        TS("dve", ysl, hbank[hh][PS, 0:64], coef[PS, hh:hh + 1], None, ALU.mult, None, [hbn[hh], "coef"], ["ynf"])
                    else:
                        STT("dve", ysl, hbank[hh][PS, 0:64], coef[PS, hh:hh + 1], ysl, ALU.mult, ALU.add, [hbn[hh], "coef", "ynf"], ["ynf"])

            for m in range(nsa_m):
                t = NCTX + m
                load_norm(t)
                qb, qbn = bank()
                gb2, gb2n = bank()
                for k in range(8):
                    MM(qb[:, :], hT[:, k, :], wq[:, k, 0:512], ["hT", "wq"], [qbn], start=(k == 0), stop=(k == 7))
                for k in range(8):
                    MM(gb2[:, 0:24], hT[:, k, :], wq[:, k, 512:536], ["hT", "wq"], [gb2n], start=(k == 0), stop=(k == 7))
                CP("act", qf[:], qb[:, :], [qbn], ["qf"])
                ACT(gts[:], gb2[:, 0:24], AF.Sigmoid, [gb2n], ["gts"])
                TT("pool", qsq[:], qf[:], qf[:], ALU.mult, ["qf"], ["qsq"])
                RED("dve", qss[:], qsq[:].rearrange("p (h d) -> p h d", h=8), ALU.add, ["qsq"], ["qss"])
                P.op("act", lambda e: e.sqrt(out=qss[:], in_=qss[:]), ["qss"], ["qss"])
                TS("dve", negc[:], qss[:], kmax8[:, 0:1], -1.0, ALU.mult, ALU.mult, ["qss", "kmax8"], ["negc"])
                TS("dve", negcb[:], negc[:], -BIG, None, ALU.add, None, ["negc"], ["negcb"])
                CP("pool", qr[:], qf[:], ["qf"], ["qr"])
                q4 = qf[:].rearrange("p (h d) -> p h d", h=8)
                qr4 = qr[:].rearrange("p (h d) -> p h d", h=8)
                cosq = cstab[:, t, 0, :].unsqueeze(1).to_broadcast([128, 8, 8])
                sinq = cstab[:, t, 1, :].unsqueeze(1).to_broadcast([128, 8, 8])
                TT("dve", qrt[:, 0], q4[:, :, 0:8], cosq, ALU.mult, ["qf", "cstab"], ["qrt0"])
                TT("dve", qrt[:, 1], q4[:, :, 8:16], sinq, ALU.mult, ["qf", "cstab"], ["qrt1"])
                TT("dve", qrt[:, 2], q4[:, :, 8:16], cosq, ALU.mult, ["qf", "cstab"], ["qrt2"])
                TT("dve", qrt[:, 3], q4[:, :, 0:8], sinq, ALU.mult, ["qf", "cstab"], ["qrt3"])
                TT("dve", qr4[:, :, 0:8], qrt[:, 0], qrt[:, 1], ALU.subtract, ["qrt0", "qrt1"], ["qr"])
                TT("dve", qr4[:, :, 8:16], qrt[:, 2], qrt[:, 3], ALU.add, ["qrt2", "qrt3"], ["qr"])
                TS("dve", QC[:, :, 0:64], q4, 0.125, None, ALU.mult, None, ["qf"], ["QC"])
                CP("dve", QC[:, :, 64], negc[:], ["negc"], ["QC"])
                TS("pool", QS[:, :, 0:64], qr4, 0.125, None, ALU.mult, None, ["qr"], ["QS"])
                for h in range(8):
                    TR(ps_tr[0:65, h, :], QC[:, h, :], identb[:], ["QC", "identb"], ["ps_tr"])
                CP("act", QcT[:], ps_tr[0:65, :, :], ["ps_tr"], ["QcT"])
                for g in range(2):
                    for nt_, nn in ((0, 128), (1, 127)):
                        sb_, sbn = bank()
                        MM(sb_[0:nn, :], KcA[g][:, nt_ * 128:nt_ * 128 + nn], QcT[:, 4 * g:4 * g + 4, :], ["KcA%d" % g, "QcT"], [sbn])
                        Pm, pmn = nextPT()
                        ACT(Pm[0:nn, :], sb_[0:nn, :], AF.Exp, [sbn], [pmn])
                        TT("dve", Pm[0:nn, :].rearrange("p (h t) -> p h t", h=4), Pm[0:nn, :].rearrange("p (h t) -> p h t", h=4),
                           maskc[0:nn, nt_, m * 128:(m + 1) * 128].unsqueeze(1).to_broadcast([nn, 4, 128]), ALU.mult, [pmn, "maskc"], [pmn])
                        for hh in range(4):
                            MM(hbank[hh][:, 0:129], Pm[0:nn, hh * 128:(hh + 1) * 128], VcA[g][0:nn, nt_, :], [pmn, "VcA%d" % g], [hbn[hh]],
                               start=(nt_ == 0), stop=(nt_ == 1))
                    accum_branch(g, 0, True)
                    for hh in range(4):
                        if hh == 0:
                            TS("dve", imp[:], hbank[0][:, 65:129], rden[:, 0:1], None, ALU.mult, None, [hbn[0], "rden"], ["imp"])
                        else:
                            STT("dve", imp[:], hbank[hh][:, 65:129], rden[:, hh:hh + 1], imp[:], ALU.mult, ALU.add, [hbn[hh], "rden", "imp"], ["imp"])
                    TT("dve", sc[:], imp[:], allowed[:, m, :], ALU.mult, ["imp", "allowed"], ["sc"])
                    TT("dve", sc[:], sc[:], fbias[:, m, :], ALU.add, ["sc", "fbias"], ["sc"])
                    P.op("dve", lambda e: e.max(out=mx8[:], in_=sc[:]), ["sc"], ["mx8"])
                    P.op("dve", lambda e: e.match_replace(out=sc2[:], in_to_replace=mx8[:], in_values=sc[:], imm_value=-2e9), ["sc", "mx8"], ["sc2"])
                    P.op("dve", lambda e: e.max(out=mx8[:], in_=sc2[:]), ["sc2"], ["mx8"])
                    TS("dve", sel[:], sc[:], mx8[:, 7:8], None, ALU.is_ge, None, ["sc", "mx8"], ["sel"])
                    TT("dve", sel[:], sel[:], allowed[:, m, :], ALU.mult, ["sel", "allowed"], ["sel"])
                    MSET("dve", sel[:, 0:1], 1.0, ["sel"])
                    for hh in range(4):
                        h = 4 * g + hh
                        TS("dve", QS[:, h, 64:128], sel[:], BIG, negcb[:, h:h + 1], ALU.mult, ALU.add, ["sel", "negcb"], ["QS"])
                    for hh in range(4):
                        h = 4 * g + hh
                        TR(ps_tr[:, h, :], QS[:, h, :], identb[:], ["QS", "identb"], ["ps_tr"])
                    CP("act", QsT[:, 4 * g:4 * g + 4, :], ps_tr[:, 4 * g:4 * g + 4, :], ["ps_tr"], ["QsT"])
                    def pv_sel(kt, Pm, pmn, g=g, t=t):
                        for hh in range(4):
                            MM(hbank[hh][:, 0:65], Pm[:, hh * 128:(hh + 1) * 128], VsA[:, kt, g, :], [pmn, "VsA"], [hbn[hh]],
                               start=(kt == 0), stop=(kt == t))
                    pend = None
                    for kt in range(t + 1):
                        sb_, sbn = bank()
                        MM(sb_[:, :], KsA[g][:, kt * 128:(kt + 1) * 128], QsT[:, 4 * g:4 * g + 4, :], ["KsA%d" % g, "QsT"], [sbn])
                        Pm, pmn = nextPT()
                        ACT(Pm[:], sb_[:, :], AF.Exp, [sbn], [pmn])
                        if kt == t:
                            TT("pool", Pm[:].rearrange("p (h t) -> p h t", h=4), Pm[:].rearrange("p (h t) -> p h t", h=4),
                               tri[:, 0, :].unsqueeze(1).to_broadcast([128, 4, 128]), ALU.mult, [pmn, "tri"], [pmn])
                        if pend is not None:
                            pv_sel(*pend)
                        pend = (kt, Pm, pmn)
                    pv_sel(*pend)
                    accum_branch(g, 1, False)
                    kts = [kt for kt in range(t - 4, t + 1) if kt >= 0]
                    def pv_win(kt, Pm, pmn, g=g, kts=kts):
                        for hh in range(4):
                            MM(hbank[hh][:, 0:65], Pm[:, hh * 128:(hh + 1) * 128], VwA[:, kt, g, :], [pmn, "VwA"], [hbn[hh]],
                               start=(kt == kts[0]), stop=(kt == kts[-1]))
                    pend = None
                    for kt in kts:
                        sb_, sbn = bank()
                        MM(sb_[:, :], KwA[g][:, kt * 128:(kt + 1) * 128], QsT[0:65, 4 * g:4 * g + 4, :], ["KwA%d" % g, "QsT"], [sbn])
                        Pm, pmn = nextPT()
                        ACT(Pm[:], sb_[:, :], AF.Exp, [sbn], [pmn])
                        if kt == t or kt == t - 4:
                            TT("pool", Pm[:].rearrange("p (h t) -> p h t", h=4), Pm[:].rearrange("p (h t) -> p h t", h=4),
                               tri[:, 0 if kt == t else 1, :].unsqueeze(1).to_broadcast([128, 4, 128]), ALU.mult, [pmn, "tri"], [pmn])
                        if pend is not None:
                            pv_win(*pend)
                        pend = (kt, Pm, pmn)
                    pv_win(*pend)
                    accum_branch(g, 2, False)
                if dbg:
                    DMA("sp", o_yn[m * 128:(m + 1) * 128, :], ynf[:], "ynf", r=["ynf"])
                    if dbg_br:
                        DMA("sp", o_br[m * 128:(m + 1) * 128, :, :], dbgo[:], "dbgo", r=["dbgo"])
                CP("act", ynb[:], ynf[:], ["ynf"], ["ynb"])
                for q in range(4):
                    TR(ps_tr[:, q, :], ynb[:, q * 128:(q + 1) * 128], identb[:], ["ynb", "identb"], ["ps_tr"])
                CP("dve", ynTt[:], ps_tr[:, 0:4, :], ["ps_tr"], ["ynTt"])
                DMA("sp", ynd[:, :, m * 128:(m + 1) * 128], ynTt[:], "ynTt", r=["ynTt"])
            NSEQ = nsa_seq
            R32 = slice(0, 32)
            rawc2 = [sb("rawc%d" % i, [128, 16, 256], BF16) for i in range(2)]
            raws = sb("raws", [128, 16, 256], BF16)
            raww = sb("raww", [128, 4, 256], BF16)
            KsS, KwS, VsS, VwS, kcS, vcS, KcS = KsA, KwA, VsA, VwA, kcT, vcT, KcA
            VcS = [VcA[g][:, 0, :] for g in range(2)]
            pti = sb("pti", [1, 256], I32)
            ptf = sb("ptf", [1, 256])
            idxf = sb("idxf", [128, 256])
            idxi = sb("idxi", [128, 256], I32)
            iop = sb("iop", [128, 1])
            ovla = sb("ovla", [128, 64], BF16)
            alls = sb("alls", [128, 64])
            fbs = sb("fbs", [128, 64])
            mw0 = sb("mw0", [128, 32], BF16)
            xs32 = xt[0][0:32, :]
            hTs = sb("hTs", [128, 8, 32], BF16)
            kvs = kv[0][0:32, :]
            kvsb = kvb[0:32, :]
            sqr = maskc[:].rearrange("p a (k c) -> p (a k) c", c=256)
            rms_ = sb("rms_", [128, 64])
            rmx = sb("rmx", [128, 1])
            kmx8 = sb("kmx8", [128, 1])
            ynTs = sb("ynTs", [128, 4, 32], BF16)
            DMA("sp", pti[:], page_tab[:, :], "pti", w=["pti"])
            DMA("sp", iop[:], iota_p[:, :], "iop", w=["iop"])
            DMA("pool", ovla[:], ovl_abs[:, :], "ovla", w=["ovla"])
            DMA("sp", alls[:], allowed_s[:, :], "alls", w=["alls"])
            DMA("sp", fbs[:], fbias_s[:, :], "fbs", w=["fbs"])
            DMA("pool", mw0[:], maskw0[:, :], "mw0", w=["mw0"])
            CP("dve", ptf[:], pti[:], ["pti"], ["ptf"])
            sbk, sbkn = bank()
            MM(sbk[:, 0:256], onesf[0:1, :], ptf[0:1, :], ["onesf", "ptf"], [sbkn])
            TS("dve", idxf[:], sbk[:, 0:256], 128.0, iop[:, 0:1], ALU.mult, ALU.add, [sbkn, "iop"], ["idxf"])
            CP("dve", idxi[:], idxf[:], ["idxf"], ["idxi"])
            for g in range(2):
                MSET("pool", VcS[g][:, 64:65], 1.0, ["VcA%d" % g])
                CP("pool", VcS[g][:, 65:129], ovla[:], ["ovla"], ["VcA%d" % g])
            MSET("pool", VsS[:, 0:16, :, 64:65], 1.0, ["VsA"])
            MSET("pool", VwS[:, 0:4, :, 64:65], 1.0, ["VwA"])
            cache_c2 = cache_c.rearrange("n t c -> (n t) c")
            cache_s2 = cache_s.rearrange("n t c -> (n t) c")
            def gather_c(jj):
                buf = rawc2[jj % 2]
                bn = "rawc%d" % (jj % 2)
                for pg in range(16):
                    col = jj * 16 + pg
                    P.dma("pool", (lambda e, pg=pg, col=col, buf=buf: e.indirect_dma_start(
                        out=buf[:, pg, :], out_offset=None, in_=cache_c2[:, :],
                        in_offset=bass.IndirectOffsetOnAxis(ap=idxi[:, col:col + 1], axis=0))), bn, ["idxi"], [bn])

            if NSEQ > 0:
                gather_c(0)
            for j in range(NSEQ):
                ts_ = NPT + j // 4
                r0 = 32 * (j % 4)
                row0 = ts_ * 128 + r0
                rawc = rawc2[j % 2]
                rcn = "rawc%d" % (j % 2)
                for pg in range(16):
                    col = j * 16 + pg
                    P.dma("pool", (lambda e, pg=pg, col=col: e.indirect_dma_start(
                        out=raws[:, pg, :], out_offset=None, in_=cache_s2[:, :],
                        in_offset=bass.IndirectOffsetOnAxis(ap=idxi[:, col:col + 1], axis=0))), "raws", ["idxi"], ["raws"])
                if j + 1 < NSEQ:
                    gather_c(j + 1)
                DMA("pool", raww[:], cwin[j].rearrange("(k p) c -> p k c", p=128), "raww", w=["raww"])
                DMA("sp", xs32, xrows[row0:row0 + 32, :], "xt0", w=["xt0"])
                ACT(sq[R32, :], xs32, AF.Square, ["xt0"], ["sq", "ss"], accum=ss[R32, :])
                TS("dve", rstd[R32, :], ss[R32, :], 1.0 / D, 1e-6, ALU.mult, ALU.add, ["ss"], ["rstd"])
                P.op("act", lambda e: e.sqrt(out=rstd[R32, :], in_=rstd[R32, :]), ["rstd"], ["rstd"])
                RECIP(rstd[R32, :], rstd[R32, :], ["rstd"], ["rstd"])
                TS("dve", xn[R32, :], xs32, rstd[R32, 0:1], None, ALU.mult, None, ["xt0", "rstd"], ["xn"])
                for k in range(8):
                    TR(ps_tr[:, k, 0:32], xn[R32, k * 128:(k + 1) * 128], identb[R32, R32], ["xn", "identb"], ["ps_tr"])
                TT("dve", hTs[:], ps_tr[:, :, 0:32], gmix[:].unsqueeze(2).to_broadcast([128, 8, 32]), ALU.mult, ["ps_tr", "gmix"], ["hTs"])
                for half in range(2):
                    n0 = half * 384
                    pb, pbn = bank()
                    for k in range(8):
                        MM(pb[R32, 0:384], hTs[:, k, :], wkv[:, k, n0:n0 + 384], ["hTs", "wkv"], [pbn], start=(k == 0), stop=(k == 7))
                    CP("act", kvs[:, n0:n0 + 384], pb[R32, 0:384], [pbn], ["kv0"])
                kview = kvs[:, 256:768].rearrange("p (a b) -> p a b", a=2)[:, :, 0:128].rearrange("p a (g d) -> p a g d", g=2)
                x1 = kview[:, :, :, 0:8]
                x2 = kview[:, :, :, 8:16]
                cosb = cstab[R32, NPT, 0, :].unsqueeze(1).unsqueeze(1).to_broadcast([32, 2, 2, 8])
                sinb = cstab[R32, NPT, 1, :].unsqueeze(1).unsqueeze(1).to_broadcast([32, 2, 2, 8])
                TT("dve", rt[R32, 0], x1, cosb, ALU.mult, ["kv0", "cstab"], ["rt0"])
                TT("dve", rt[R32, 1], x2, sinb, ALU.mult, ["kv0", "cstab"], ["rt1"])
                TT("dve", rt[R32, 2], x2, cosb, ALU.mult, ["kv0", "cstab"], ["rt2"])
                TT("dve", rt[R32, 3], x1, sinb, ALU.mult, ["kv0", "cstab"], ["rt3"])
                TT("dve", x1, rt[R32, 0], rt[R32, 1], ALU.subtract, ["rt0", "rt1"], ["kv0"])
                TT("dve", x2, rt[R32, 2], rt[R32, 3], ALU.add, ["rt2", "rt3"], ["kv0"])
                CP("act", kvsb, kvs, ["kv0"], ["kvb"])
                qb, qbn = bank()
                gb2, gb2n = bank()
                for k in range(8):
                    MM(qb[R32, :], hTs[:, k, :], wq[:, k, 0:512], ["hTs", "wq"], [qbn], start=(k == 0), stop=(k == 7))
                for k in range(8):
                    MM(gb2[R32, 0:24], hTs[:, k, :], wq[:, k, 512:536], ["hTs", "wq"], [gb2n], start=(k == 0), stop=(k == 7))
                CP("act", qf[R32, :], qb[R32, :], [qbn], ["qf"])
                ACT(gts[R32, :], gb2[R32, 0:24], AF.Sigmoid, [gb2n], ["gts"])
                CP("pool", VsS[:, 0:16, :, 0:64], raws[:, :, 128:256].rearrange("p k (g d) -> p k g d", g=2), ["raws"], ["VsA"])
                CP("pool", VwS[:, 0:4, :, 0:64], raww[:, :, 128:256].rearrange("p k (g d) -> p k g d", g=2), ["raww"], ["VwA"])
                CP("pool", VsS[R32, 16, :, 0:64], kvs[:, 384:512].rearrange("p (g d) -> p g d", g=2), ["kv0"], ["VsA"])
                CP("pool", VwS[R32, 4, :, 0:64], kvs[:, 640:768].rearrange("p (g d) -> p g d", g=2), ["kv0"], ["VwA"])
                for g in range(2):
                    CP("pool", VsS[R32, 16, g, 64:65], rvalid[R32, NPT:NPT + 1], ["rvalid"], ["VsA"])
                    CP("pool", VwS[R32, 4, g, 64:65], rvalid[R32, NPT:NPT + 1], ["rvalid"], ["VwA"])
                for pg0 in range(0, 16, 8):
                    for g in range(2):
                        for pp in range(8):
                            TR(ps_tr[0:64, pp, :], raws[:, pg0 + pp, g * 64:(g + 1) * 64], identb[:], ["raws", "identb"], ["ps_tr"])
                        CP("dve", KsS[g][0:64, pg0 * 128:(pg0 + 8) * 128].rearrange("p (k t) -> p k t", k=8), ps_tr[0:64, :, :], ["ps_tr"], ["KsA%d" % g])
                    for c in range(2):
                        dst = kcS if c == 0 else vcS
                        for pp in range(8):
                            TR(ps_tr[:, pp, :], rawc[:, pg0 + pp, c * 128:(c + 1) * 128], identb[:], [rcn, "identb"], ["ps_tr"])
                        CP("dve", dst[:, pg0 * 128:(pg0 + 8) * 128].rearrange("p (k t) -> p k t", k=8), ps_tr[:, :, :], ["ps_tr"], ["kcT" if c == 0 else "vcT"])
                for g in range(2):
                    for pp in range(4):
                        TR(ps_tr[0:64, g * 4 + pp, :], raww[:, pp, g * 64:(g + 1) * 64], identb[:], ["raww", "identb"], ["ps_tr"])
                for g in range(2):
                    CP("dve", KwS[g][0:64, 0:512].rearrange("p (k t) -> p k t", k=4), ps_tr[0:64, 4 * g:4 * g + 4, :], ["ps_tr"], ["KwA%d" % g])
                for g in range(2):
                    TR(ps_tr[0:64, g, 0:32], kvsb[:, 256 + g * 64:256 + (g + 1) * 64], identb[R32, R32], ["kvb", "identb"], ["ps_tr"])
                    TR(ps_tr[0:64, 2 + g, 0:32], kvsb[:, 512 + g * 64:512 + (g + 1) * 64], identb[R32, R32], ["kvb", "identb"], ["ps_tr"])
                for g in range(2):
                    CP("dve", KsS[g][0:64, 2048:2080], ps_tr[0:64, g, 0:32], ["ps_tr"], ["KsA%d" % g])
                    CP("dve", KwS[g][0:64, 512:544], ps_tr[0:64, 2 + g, 0:32], ["ps_tr"], ["KwA%d" % g])
                MSET("dve", rmx[:], 0.0, ["rmx"])
                for (src, srcn, nk) in ((rawc, rcn, 16), (raws, "raws", 16), (raww, "raww", 4)):
                    TT("pool", sqr[:, 0:nk, :], src[:, 0:nk, :], src[:, 0:nk, :], ALU.mult, [srcn], ["maskc"])
                    RED("dve", rms_[:, 0:nk * 4], sqr[:, 0:nk, :].rearrange("p k (a d) -> p (k a) d", a=4), ALU.add, ["maskc"], ["rms_"])
                    RED("dve", r12[:, 0:1], rms_[:, 0:nk * 4], ALU.max, ["rms_"], ["r12"])
                    TT("dve", rmx[:], rmx[:], r12[:, 0:1], ALU.max, ["rmx", "r12"], ["rmx"])
                TT("pool", sqk[R32, :], kvs, kvs, ALU.mult, ["kv0"], ["sqk"])
                RED("dve", r12[R32, :], sqk[R32, :].rearrange("p (a d) -> p a d", a=12), ALU.add, ["sqk"], ["r12"])
                RED("dve", r12[R32, 0:1], r12[R32, :], ALU.max, ["r12"], ["r12"])
                TT("dve", rmx[R32, :], rmx[R32, :], r12[R32, 0:1], ALU.max, ["rmx", "r12"], ["rmx"])
                for c in range(2):
                    src = kcS if c == 0 else vcS
                    srcn = "kcT" if c == 0 else "vcT"
                    srcv = src[:].rearrange("p (n l) -> p l n", l=16)
                    for g in range(2):
                        gs = slice(64 * g, 64 * g + 64)
                        hb, hbn_ = bank()
                        for l in range(16):
                            MM(hb[:, 0:127], w1d[gs, c, l, :], srcv[gs, l, 0:127], ["w1d", srcn], [hbn_], start=(l == 0), stop=False)
                            MM(hb[:, 0:127], w1d[gs, c, 16 + l, :], srcv[gs, l, 1:128], ["w1d", srcn], [hbn_], start=False, stop=(l == 15))
                        ACT(hc[:, 0:127], hb[:, 0:127], AF.Silu, [hbn_, "bcs"], ["hc"], bias=bcs[:, c:c + 1])
                        ob, obn = bank()
                        if c == 0:
                            MM(ob[0:64, 0:127], w2s[:, 0, :], hc[:, 0:127], ["w2s", "hc"], [obn])
                            CP("dve", KcS[g][0:64, 0:127], ob[0:64, 0:127], [obn], ["KcA%d" % g])
                            ob2, ob2n = bank()
                            MM(ob2[0:127, 0:64], hc[:, 0:127], w2s[:, 0, :], ["hc", "w2s"], [ob2n])
                            ACT(kct[0:127, :], ob2[0:127, 0:64], AF.Square, [ob2n], ["kct", "r12"], accum=r12[0:127, 0:1])
                            TT("dve", rmx[0:127, :], rmx[0:127, :], r12[0:127, 0:1], ALU.max, ["rmx", "r12"], ["rmx"])
                        else:
                            MM(ob[0:127, 0:64], hc[:, 0:127], w2s[:, 1, :], ["hc", "w2s"], [obn])
                            CP("dve", VcS[g][0:127, 0:64], ob[0:127, 0:64], [obn], ["VcA%d" % g])
                sbk, sbkn = bank()
                TR(sbk[0:1, 0:128], rmx[:, 0:1], identf[:], ["rmx", "identf"], [sbkn])
                RED("dve", k1[:], sbk[0:1, 0:128], ALU.max, [sbkn], ["k1"])
                P.op("act", lambda e: e.sqrt(out=k1[:], in_=k1[:]), ["k1"], ["k1"])
                sbk2, sbk2n = bank()
                MM(sbk2[:, 0:1], onesf[0:1, :], k1[0:1, 0:1], ["onesf", "k1"], [sbk2n])
                TS("dve", kmx8[:], sbk2[:, 0:1], 0.125, None, ALU.mult, None, [sbk2n], ["kmx8"])
                TT("pool", qsq[R32, :], qf[R32, :], qf[R32, :], ALU.mult, ["qf"], ["qsq"])
                RED("dve", qss[R32, :], qsq[R32, :].rearrange("p (h d) -> p h d", h=8), ALU.add, ["qsq"], ["qss"])
                P.op("act", lambda e: e.sqrt(out=qss[R32, :], in_=qss[R32, :]), ["qss"], ["qss"])
                TS("dve", negc[R32, :], qss[R32, :], kmx8[R32, 0:1], -1.0, ALU.mult, ALU.mult, ["qss", "kmx8"], ["negc"])
                TS("dve", negcb[R32, :], negc[R32, :], -BIG, None, ALU.add, None, ["negc"], ["negcb"])
                CP("pool", qr[R32, :], qf[R32, :], ["qf"], ["qr"])
                q4 = qf[R32, :].rearrange("p (h d) -> p h d", h=8)
                qr4 = qr[R32, :].rearrange("p (h d) -> p h d", h=8)
                cosq = cstab[R32, NPT, 0, :].unsqueeze(1).to_broadcast([32, 8, 8])
                sinq = cstab[R32, NPT, 1, :].unsqueeze(1).to_broadcast([32, 8, 8])
                TT("dve", qrt[R32, 0], q4[:, :, 0:8], cosq, ALU.mult, ["qf", "cstab"], ["qrt0"])
                TT("dve", qrt[R32, 1], q4[:, :, 8:16], sinq, ALU.mult, ["qf", "cstab"], ["qrt1"])
                TT("dve", qrt[R32, 2], q4[:, :, 8:16], cosq, ALU.mult, ["qf", "cstab"], ["qrt2"])
                TT("dve", qrt[R32, 3], q4[:, :, 0:8], sinq, ALU.mult, ["qf", "cstab"], ["qrt3"])
                TT("dve", qr4[:, :, 0:8], qrt[R32, 0], qrt[R32, 1], ALU.subtract, ["qrt0", "qrt1"], ["qr"])
                TT("dve", qr4[:, :, 8:16], qrt[R32, 2], qrt[R32, 3], ALU.add, ["qrt2", "qrt3"], ["qr"])
                TS("dve", QC[R32, :, 0:64], q4, 0.125, None, ALU.mult, None, ["qf"], ["QC"])
                CP("dve", QC[R32, :, 64], negc[R32, :], ["negc"], ["QC"])
                TS("pool", QS[R32, :, 0:64], qr4, 0.125, None, ALU.mult, None, ["qr"], ["QS"])
                for h in range(8):
                    TR(ps_tr[0:65, h, 0:32], QC[R32, h, :], identb[R32, R32], ["QC", "identb"], ["ps_tr"])
                CP("act", QcT[:, :, 0:32], ps_tr[0:65, :, 0:32], ["ps_tr"], ["QcT"])
                for g in range(2):
                    sb_, sbn = bank()
                    MM(sb_[0:127, 0:128], KcS[g][:, 0:127], QcT[:, 4 * g:4 * g + 4, 0:32], ["KcA%d" % g, "QcT"], [sbn])
                    Pm, pmn = nextPT()
                    ACT(Pm[0:127, 0:128], sb_[0:127, 0:128], AF.Exp, [sbn], [pmn])
                    for hh in range(4):
                        MM(hbank[hh][R32, 0:129], Pm[0:127, hh * 32:(hh + 1) * 32], VcS[g][0:127, :], [pmn, "VcA%d" % g], [hbn[hh]])
                    accum_branch(g, 0, True, R32)
                    for hh in range(4):
                        if hh == 0:
                            TS("dve", imp[R32, :], hbank[0][R32, 65:129], rden[R32, 0:1], None, ALU.mult, None, [hbn[0], "rden"], ["imp"])
                        else:
                            STT("dve", imp[R32, :], hbank[hh][R32, 65:129], rden[R32, hh:hh + 1], imp[R32, :], ALU.mult, ALU.add, [hbn[hh], "rden", "imp"], ["imp"])
                    TT("dve", sc[R32, :], imp[R32, :], alls[R32, :], ALU.mult, ["imp", "alls"], ["sc"])
                    TT("dve", sc[R32, :], sc[R32, :], fbs[R32, :], ALU.add, ["sc", "fbs"], ["sc"])
                    P.op("dve", lambda e: e.max(out=mx8[R32, :], in_=sc[R32, :]), ["sc"], ["mx8"])
                    P.op("dve", lambda e: e.match_replace(out=sc2[R32, :], in_to_replace=mx8[R32, :], in_values=sc[R32, :], imm_value=-2e9), ["sc", "mx8"], ["sc2"])
                    P.op("dve", lambda e: e.max(out=mx8[R32, :], in_=sc2[R32, :]), ["sc2"], ["mx8"])
                    TS("dve", sel[R32, :], sc[R32, :], mx8[R32, 7:8], None, ALU.is_ge, None, ["sc", "mx8"], ["sel"])
                    TT("dve", sel[R32, :], sel[R32, :], alls[R32, :], ALU.mult, ["sel", "alls"], ["sel"])
                    MSET("dve", sel[R32, 0:1], 1.0, ["sel"])
                    for hh in range(4):
                        h = 4 * g + hh
                        TS("dve", QS[R32, h, 64:128], sel[R32, :], BIG, negcb[R32, h:h + 1], ALU.mult, ALU.add, ["sel", "negcb"], ["QS"])
                    for hh in range(4):
                        h = 4 * g + hh
                        TR(ps_tr[:, h, 0:32], QS[R32, h, :], identb[R32, R32], ["QS", "identb"], ["ps_tr"])
                    CP("act", QsT[:, 4 * g:4 * g + 4, 0:32], ps_tr[:, 4 * g:4 * g + 4, 0:32], ["ps_tr"], ["QsT"])
                    def pv_ssel(kt, nk, Pm, pmn, g=g):
                        for hh in range(4):
                            MM(hbank[hh][R32, 0:65], Pm[0:nk, hh * 32:(hh + 1) * 32], VsS[0:nk, kt, g, :], [pmn, "VsA"], [hbn[hh]],
                               start=(kt == 0), stop=(kt == 16))
                    pend = None
                    for kt in range(17):
                        nk = 128 if kt < 16 else 32
                        sb_, sbn = bank()
                        MM(sb_[0:nk, 0:128], KsS[g][:, kt * 128:kt * 128 + nk], QsT[:, 4 * g:4 * g + 4, 0:32], ["KsA%d" % g, "QsT"], [sbn])
                        Pm, pmn = nextPT()
                        ACT(Pm[0:nk, 0:128], sb_[0:nk, 0:128], AF.Exp, [sbn], [pmn])
                        if kt == 16:
                            TT("pool", Pm[R32, 0:128].rearrange("p (h t) -> p h t", h=4), Pm[R32, 0:128].rearrange("p (h t) -> p h t", h=4),
                               tri[R32, 0, 0:32].unsqueeze(1).to_broadcast([32, 4, 32]), ALU.mult, [pmn, "tri"], [pmn])
                        if pend is not None:
                            pv_ssel(*pend)
                        pend = (kt, nk, Pm, pmn)
                    pv_ssel(*pend)
                    accum_branch(g, 1, False, R32)
                    def pv_swin(kt, nk, Pm, pmn, g=g):
                        for hh in range(4):
                            MM(hbank[hh][R32, 0:65], Pm[0:nk, hh * 32:(hh + 1) * 32], VwS[0:nk, kt, g, :], [pmn, "VwA"], [hbn[hh]],
                               start=(kt == 0), stop=(kt == 4))
                    pend = None
                    for kt in range(5):
                        nk = 128 if kt < 4 else 32
                        sb_, sbn = bank()
                        MM(sb_[0:nk, 0:128], KwS[g][:, kt * 128:kt * 128 + nk], QsT[0:65, 4 * g:4 * g + 4, 0:32], ["KwA%d" % g, "QsT"], [sbn])
                        Pm, pmn = nextPT()
                        ACT(Pm[0:nk, 0:128], sb_[0:nk, 0:128], AF.Exp, [sbn], [pmn])
                        if kt == 0:
                            TT("pool", Pm[:, 0:128].rearrange("p (h t) -> p h t", h=4), Pm[:, 0:128].rearrange("p (h t) -> p h t", h=4),
                               mw0[:, :].unsqueeze(1).to_broadcast([128, 4, 32]), ALU.mult, [pmn, "mw0"], [pmn])
                        if kt == 4:
                            TT("pool", Pm[R32, 0:128].rearrange("p (h t) -> p h t", h=4), Pm[R32, 0:128].rearrange("p (h t) -> p h t", h=4),
                               tri[R32, 0, 0:32].unsqueeze(1).to_broadcast([32, 4, 32]), ALU.mult, [pmn, "tri"], [pmn])
                        if pend is not None:
                            pv_swin(*pend)
                        pend = (kt, nk, Pm, pmn)
                    pv_swin(*pend)
                    accum_branch(g, 2, False, R32)
                to = NMAIN + j // 4
                if dbg:
                    DMA("sp", o_yn[to * 128 + r0:to * 128 + r0 + 32, :], ynf[R32, :], "ynf", r=["ynf"])
                CP("act", ynb[R32, :], ynf[R32, :], ["ynf"], ["ynb"])
                for q in range(4):
                    TR(ps_tr[:, q, 0:32], ynb[R32, q * 128:(q + 1) * 128], identb[R32, R32], ["ynb", "identb"], ["ps_tr"])
                CP("dve", ynTs[:], ps_tr[:, 0:4, 0:32], ["ps_tr"], ["ynTs"])
                DMA("sp", ynd[:, :, to * 128 + r0:to * 128 + r0 + 32], ynTs[:], "ynTs", r=["ynTs"])
            if NSEQ < 16:
                MSET("pool", ynTt[:], 0.0, ["ynTt"])
                for to in range(NMAIN, NOUT):
                    DMA("sp", ynd[:, :, to * 128:(to + 1) * 128], ynTt[:], "ynTt", r=["ynTt"])
            banks.extend(hbank)

        P.barrier()
        cur_stack.pop()
        st_kv.close()
        bank_i[0] = 0

        def bank():
            i = bank_i[0] % len(banks)
            bank_i[0] += 1
            return banks[i], "bank%d" % banks_id[id(banks[i])]

        if do_rwkv:
            st_rw = ExitStack()
            cur_stack.append(st_rw)
            mub = sb("mub", [128, RW])
            parb = sb("parb", [128, 7, 512])
            wrw = sb("wrw", [128, 8, RW], BF16)
            wdec = sb("wdec", [64, 512], BF16)
            waaa = sb("waaa", [64, 512], BF16)
            wgat = sb("wgat", [128, 2, 512], BF16)
            mskb = sb("mskb", [128, 3, 128], BF16)
            mskf = sb("mskf", [128, 2, 128])
            cind = sb("cind", [128, 4])
            DMA("sp", mub[:], mu_b[:, :], "mub", w=["mub"])
            DMA("act", parb[:], par_b[:, :, :], "parb", w=["parb"])
            for k in range(8):
                DMA("pool", wrw[:, k, :], w_in_v[:, k, 0:RW], "wrw", w=["wrw"])
            DMA("pool", wdec[:], w_decay[:, :], "wdec", w=["wdec"])
            DMA("pool", waaa[:], w_aaa[:, :], "waaa", w=["waaa"])
            DMA("pool", wgat[:, 0, :], w_gate[0:128, :], "wgat", w=["wgat"])
            DMA("pool", wgat[0:32, 1, :], w_gate[128:160, :], "wgat", w=["wgat"])
            DMA("pool", mskb[:], masks_bf[:, :, :], "mskb", w=["mskb"])
            DMA("sp", mskf[:], masks_f[:, :, :], "mskf", w=["mskf"])
            DMA("sp", cind[:], chunk_ind[:, :], "cind", w=["cind"])
            rowm = sb("rowm_sb", [128, 2])
            DMA("sp", rowm[:], rowm_d[:, :], "rowm", w=["rowm"])
            ps_y1 = banks.pop()
            NBv = len(banks)
            bank_i[0] = 0

            def bank():
                i = bank_i[0] % NBv
                bank_i[0] += 1
                return banks[i], "bank%d" % banks_id[id(banks[i])]

            p_sb = [sb("p_sb0", [128, RW])] * 2
            sh = sb("sh", [128, RW])
            xs = sb("xs", [128, RW])
            lr = sb("lr", [128, 288], BF16)
            lrT = sb("lrT", [128, 4, 128], BF16)
            F = {n: sb("f_" + n, [128, 512]) for n in
                 ("ld", "asig", "gsb", "kkn", "kh", "bv", "tA", "tB", "tC", "Ein", "Eneg", "Eex", "Etot", "Vbar")}
            F["kk"] = F["kkn"]
            F["cum"] = F["tB"]
            F["Y1"], F["Y"], F["ym"], F["yn"] = F["Ein"], F["Eneg"], F["Eex"], F["Etot"]
            Bq = {n: sb("b_" + n, [128, 512], BF16) for n in ("Rt", "At", "Bt", "Kt", "Bh", "Kh", "Vb", "U", "MV", "yrb", "U3", "Vb3")}
            s8 = {n: sb("s8_" + n, [128, 8]) for n in ("ssq", "rn", "bsum", "m8", "v8", "r8")}
            ART = sb("ART", [64, 8, 2, 128], BF16)
            BT = sb("BT", [64, 8, 128], BF16)
            KT = sb("KT", [64, 8, 128], BF16)
            AbT = sb("AbT", [64, 8, 128], BF16)
            G = sb("G", [128, 8, 4, 128], BF16)
            QL = [sb("QL%d" % i, [128, 2, 8, 128], BF16) for i in range(2)]
            ZZ = [sb("ZZ%d" % i, [128, 2, 8, 128], BF16) for i in range(2)]
            Hs = sb("Hs", [64, 8, 64])
            Hb = sb("Hb", [64, 8, 64], BF16)
            WcT = sb("WcT", [64, 8, 4])
            Sraw = sb("Sraw", [128, 4, 64])
            Sout = sb("Sout", [128, 4, 64])
            yrf = sb("yrf", [128, 512])
            yrTt = sb("yrTt", [128, 4, 128], BF16)

            def par(i):
                return parb[:, i, :]

            def v8(ap):
                return ap.rearrange("p (h d) -> p h d", h=8)

            def b8(ap):
                return ap.unsqueeze(2).to_broadcast([128, 8, 64])

            MSET("dve", Hs[:], 0.0, ["Hs"])
            MSET("dve", Hb[:], 0.0, ["Hb"])
            MSET("pool", sh[:], 0.0, ["sh"])
            MSET("pool", Bq["U"][:], 0.0, ["b_U"])
            MSET("pool", Bq["U3"][:], 0.0, ["b_U3"])

            def state_out(dst):
                for hp in range(4):
                    TR(ps_f[:, hp * 64:(hp + 1) * 64], Hs[:, 2 * hp:2 * hp + 2, :], identf[0:64, 0:64], ["Hs", "identf"], ["ps_f"])
                CP("dve", Sout[:], ps_f[:, 0:256].rearrange("p (a b) -> p a b", a=4), ["ps_f"], ["Sout"])
                DMA("sp", dst.rearrange("(hp two) i j -> (two i) hp j", two=2), Sout[:], "Sout", r=["Sout"])

            for t in range(NT):
                is_smp = t >= NCTX + NMAIN
                to = t - NCTX
                load_norm(t)
                Pt = p_sb[t % 2]
                ptn = "p_sb0"
                if (not is_smp) and t > 0:
                    DMA("act", sh[0:1, :], Pt[127:128, :], "sh", r=[ptn], w=["sh"])
                for ci, n0 in enumerate((0, 512, 1024, 1536)):
                    wd = min(512, RW - n0)
                    pb, pbn = bank()
                    for k in range(8):
                        MM(pb[:, 0:wd], hT[:, k, :], wrw[:, k, n0:n0 + wd], ["hT", "wrw"], [pbn], start=(k == 0), stop=(k == 7))
                    CP("act" if ci % 2 == 0 else "dve", Pt[:, n0:n0 + wd], pb[:, 0:wd], [pbn], [ptn])
                if not is_smp:
                    DMA("act", sh[1:128, :], Pt[0:127, :], "sh", r=[ptn], w=["sh"])
                    if t == NCTX + NMAIN - 1:
                        DMA("sp", o_pshift[0:1, :], Pt[127:128, :], ptn, r=[ptn])
                else:
                    if t == NCTX + NMAIN:
                        MSET("pool", sh[:], 0.0, ["sh"])
                    for cb in range(4):
                        j = 4 * (t - NCTX - NMAIN) + cb
                        DMA("act", sh[32 * cb + 25:32 * cb + 32, :], Pt[32 * cb + 24:32 * cb + 31, :], "sh", r=[ptn], w=["sh"])
                        DMA("act", sh[32 * cb + 24:32 * cb + 25, :], shift0[j:j + 1, :], "sh", w=["sh"])
                        DMA("sp", o_sshift[j:j + 1, :], Pt[32 * cb + 31:32 * cb + 32, :], ptn, r=[ptn])
                if dbg and t >= NCTX:
                    DMA("sp", o_prow[to * 128:(to + 1) * 128, :], Pt[:], ptn, r=[ptn])
                TT("pool", sh[:], sh[:], Pt[:], ALU.subtract, ["sh", ptn], ["sh"])
                TT("pool", sh[:], sh[:], mub[:], ALU.mult, ["sh", "mub"], ["sh"])
                TT("dve", xs[:], Pt[:], sh[:], ALU.add, [ptn, "sh"], ["xs"])
                r_ = xs[:, 0:512]
                k_ = xs[:, 576:1088]
                v_ = xs[:, 1088:1600]
                ACT(lr[:, 0:64], xs[:, 512:576], AF.Tanh, ["xs"], ["lr"])
                ACT(lr[:, 128:288], xs[:, 1664:1824], AF.Sigmoid, ["xs"], ["lr"])
                CP("dve", lr[:, 64:128], xs[:, 1600:1664], ["xs"], ["lr"])
                TR(ps_tr[0:64, 0, :], lr[:, 0:64], identb[:], ["lr", "identb"], ["ps_tr"])
                TR(ps_tr[0:64, 1, :], lr[:, 64:128], identb[:], ["lr", "identb"], ["ps_tr"])
                TR(ps_tr[:, 2, :], lr[:, 128:256], identb[:], ["lr", "identb"], ["ps_tr"])
                TR(ps_tr[0:32, 3, :], lr[:, 256:288], identb[:], ["lr", "identb"], ["ps_tr"])
                CP("dve", lrT[0:64, 0:2, :], ps_tr[0:64, 0:2, :], ["ps_tr"], ["lrT"])
                CP("dve", lrT[:, 2, :], ps_tr[:, 2, :], ["ps_tr"], ["lrT"])
                CP("dve", lrT[0:32, 3, :], ps_tr[0:32, 3, :], ["ps_tr"], ["lrT"])
                zb, zbn = bank()
                MM(zb[:, :], lrT[0:64, 0, :], wdec[:, :], ["lrT", "wdec"], [zbn])
                ab, abn = bank()
                MM(ab[:, :], lrT[0:64, 1, :], waaa[:, :], ["lrT", "waaa"], [abn])
                gb, gbn = bank()
                MM(gb[:, :], lrT[:, 2, :], wgat[:, 0, :], ["lrT", "wgat"], [gbn], start=True, stop=False)
                MM(gb[:, :], lrT[0:32, 3, :], wgat[0:32, 1, :], ["lrT", "wgat"], [gbn], start=False, stop=True)
                TT("dve", F["tA"][:], zb[:, :], par(0), ALU.add, [zbn, "parb"], ["f_tA"])
                ACT(F["tA"][:], F["tA"][:], AF.Sigmoid, ["f_tA"], ["f_tA"])
                TS("dve", F["ld"][:], F["tA"][:], -EXPM05, rvalid[:, t:t + 1], ALU.mult, ALU.mult, ["f_tA", "rvalid"], ["f_ld"])
                TT("dve", F["tB"][:], ab[:, :], par(1), ALU.add, [abn, "parb"], ["f_tB"])
                ACT(F["asig"][:], F["tB"][:], AF.Sigmoid, ["f_tB"], ["f_asig"])
                CP("act", F["gsb"][:], gb[:, :], [gbn], ["f_gsb"])
                TT("dve", F["kk"][:], k_, par(2), ALU.mult, ["xs", "parb"], ["f_kkn"])
                TT("pool", F["tC"][:], F["kk"][:], F["kk"][:], ALU.mult, ["f_kkn"], ["f_tC"])
                RED("dve", s8["ssq"][:], v8(F["tC"][:]), ALU.add, ["f_tC"], ["s8_ssq"])
                TS("dve", s8["ssq"][:], s8["ssq"][:], 1e-24, None, ALU.max, None, ["s8_ssq"], ["s8_ssq"])
                P.op("act", lambda e: e.sqrt(out=s8["rn"][:], in_=s8["ssq"][:]), ["s8_ssq"], ["s8_rn"])
                RECIP(s8["rn"][:], s8["rn"][:], ["s8_rn"], ["s8_rn"])
                TT("dve", v8(F["kkn"][:]), v8(F["kk"][:]), b8(s8["rn"][:]), ALU.mult, ["f_kkn", "s8_rn"], ["f_kkn"])
                STT("dve", F["tB"][:], F["asig"][:], -1.0, par(3), ALU.add, ALU.mult, ["f_asig", "parb"], ["f_tB"])
                TT("pool", F["tB"][:], F["tB"][:], k_, ALU.mult, ["f_tB", "xs"], ["f_tB"])
                TT("pool", F["kh"][:], F["tB"][:], k_, ALU.add, ["f_tB", "xs"], ["f_kh"])
                TT("pool", F["bv"][:], F["kkn"][:], F["asig"][:], ALU.mult, ["f_kkn", "f_asig"], ["f_bv"])
                TT("pool", F["tC"][:], r_, F["kh"][:], ALU.mult, ["xs", "f_kh"], ["f_tC"])
                TT("pool", F["tC"][:], F["tC"][:], par(4), ALU.mult, ["f_tC", "parb"], ["f_tC"])
                RED("dve", s8["bsum"][:], v8(F["tC"][:]), ALU.add, ["f_tC"], ["s8_bsum"])
                cb_, cbn = bank()
                MM(cb_[:, :], mskf[:, 0, :], F["ld"][:], ["mskf", "f_ld"], [cbn])
                tb_, tbn = bank()
                MM(tb_[:, :], mskf[:, 1, :], F["ld"][:], ["mskf", "f_ld"], [tbn])
                for h in range(8):
                    MM(ps_f[0:64, h * 4:(h + 1) * 4], F["ld"][:, h * 64:(h + 1) * 64], cind[:, :], ["f_ld", "cind"], ["ps_f"])
                ACT(WcT[:], ps_f[0:64, 0:32].rearrange("p (h c) -> p h c", h=8), AF.Exp, ["ps_f"], ["WcT"])
                ACT(F["Ein"][:], cb_[:, :], AF.Exp, [cbn], ["f_Ein"])
                ACT(F["Eneg"][:], cb_[:, :], AF.Exp, [cbn], ["f_Eneg"], scale=-1.0)
                TT("dve", F["tA"][:], cb_[:, :], F["ld"][:], ALU.subtract, [cbn, "f_ld"], ["f_tA"])
                ACT(F["Eex"][:], F["tA"][:], AF.Exp, ["f_tA"], ["f_Eex"])
                CP("dve", F["cum"][:], cb_[:, :], [cbn], ["f_tB"])
                TT("dve", F["tC"][:], tb_[:, :], F["cum"][:], ALU.subtract, [tbn, "f_tB"], ["f_tC"])
                ACT(F["Etot"][:], F["tC"][:], AF.Exp, ["f_tC"], ["f_Etot"])
                TT("dve", Bq["Rt"][:], r_, F["Ein"][:], ALU.mult, ["xs", "f_Ein"], ["b_Rt"])
                STT("dve", Bq["At"][:], F["kkn"][:], -1.0, F["Eex"][:], ALU.mult, ALU.mult, ["f_kkn", "f_Eex"], ["b_At"])
                TT("pool", Bq["Bt"][:], F["bv"][:], F["Eneg"][:], ALU.mult, ["f_bv", "f_Eneg"], ["b_Bt"])
                TT("dve", Bq["Kt"][:], F["kh"][:], F["Eneg"][:], ALU.mult, ["f_kh", "f_Eneg"], ["b_Kt"])
                TT("pool", Bq["Bh"][:], F["bv"][:], F["Etot"][:], ALU.mult, ["f_bv", "f_Etot"], ["b_Bh"])
                TT("dve", Bq["Kh"][:], F["kh"][:], F["Etot"][:], ALU.mult, ["f_kh", "f_Etot"], ["b_Kh"])
                CP("act", Bq["Vb"][:], v_, ["xs"], ["b_Vb"])
                TS("pool", Bq["Vb3"][64:128, :], v_[64:128, :], rowm[64:128, 0:1], None, ALU.mult, None, ["xs", "rowm"], ["b_Vb3"])
                for (src, dst, dn) in (("At", ART[:, :, 0, :], "ART"), ("Rt", ART[:, :, 1, :], "ART"), ("Bt", BT[:], "BT"), ("Kt", KT[:], "KT")):
                    for h in range(8):
                        TR(ps_tr[0:64, h, :], Bq[src][:, h * 64:(h + 1) * 64], identb[:], ["b_" + src, "identb"], ["ps_tr"])
                    CP("act" if src in ("At", "Bt") else "dve", dst, ps_tr[0:64, :, :], ["ps_tr"], [dn])
                m4 = mskb[:, 0:2, :].unsqueeze(1).to_broadcast([128, 2, 2, 128])
                for h in range(8):
                    b1, b1n = bank()
                    MM(b1[:, 0:256], BT[:, h, :], ART[:, h, :, :], ["BT", "ART"], [b1n])
                    MM(b1[:, 256:512], KT[:, h, :], ART[:, h, :, :], ["KT", "ART"], [b1n])
                    TT("dve", G[:, h, :, :].rearrange("p (a b) s -> p a b s", a=2), b1[:, :].rearrange("p (a b s) -> p a b s", a=2, b=2), m4, ALU.mult,
                       [b1n, "mskb"], ["G"])
                cur = 0
                for g in range(2):
                    b2, b2n = bank()
                    for hh in range(4):
                        h = 4 * g + hh
                        MM(b2[:, hh * 128:(hh + 1) * 128], ART[:, h, 0, :], BT[:, h, :], ["ART", "BT"], [b2n])
                    TT("dve", QL[cur][:, 1, 4 * g:4 * g + 4, :], b2[:, :].rearrange("p (a s) -> p a s", a=4),
                       mskb[:, 2, :].unsqueeze(1).to_broadcast([128, 4, 128]), ALU.mult, [b2n, "mskb"], ["QL%d" % cur])
                CP("pool", QL[cur][:, 0, :, :], G[:, :, 0, :], ["G"], ["QL%d" % cur])
                idb8 = identb[:].unsqueeze(1).unsqueeze(1).to_broadcast([128, 2, 8, 128])
                TT("pool", ZZ[0][:], QL[cur][:], idb8, ALU.add, ["QL%d" % cur, "identb"], ["ZZ0"])
                zc = 0
                nlev = 2 if is_smp else 4
                for lev in range(1, nlev + 1):
                    last = lev == nlev
                    nxt = 1 - cur
                    zn = 1 - zc
                    for g in range(2):
                        hs = slice(4 * g, 4 * g + 4)
                        bq, bqn = bank()
                        for hh in range(4):
                            h = 4 * g + hh
                            MM(bq[:, hh * 128:(hh + 1) * 128], QL[cur][:, 1, h, :], QL[cur][:, 0, h, :], ["QL%d" % cur], [bqn])
                        CP("act", QL[nxt][:, 0, hs, :], bq[:, :].rearrange("p (a s) -> p a s", a=4), [bqn], ["QL%d" % nxt])
                        if not last:
                            bl, bln = bank()
                            for hh in range(4):
                                h = 4 * g + hh
                                MM(bl[:, hh * 128:(hh + 1) * 128], QL[cur][:, 0, h, :], QL[cur][:, 1, h, :], ["QL%d" % cur], [bln])
                            CP("act", QL[nxt][:, 1, hs, :], bl[:, :].rearrange("p (a s) -> p a s", a=4), [bln], ["QL%d" % nxt])
                        bz, bzn = bank()
                        for hh in range(4):
                            h = 4 * g + hh
                            MM(bz[:, hh * 128:(hh + 1) * 128], ZZ[zc][:, 1, h, :], QL[nxt][:, 0, h, :], ["ZZ%d" % zc, "QL%d" % nxt], [bzn])
                        TT("dve", ZZ[zn][:, 0, hs, :], bz[:, :].rearrange("p (a s) -> p a s", a=4), ZZ[zc][:, 0, hs, :], ALU.add,
                           [bzn, "ZZ%d" % zc], ["ZZ%d" % zn])
                        if not last:
                            bzt, bztn = bank()
                            for hh in range(4):
                                h = 4 * g + hh
                                MM(bzt[:, hh * 128:(hh + 1) * 128], QL[nxt][:, 0, h, :], ZZ[zc][:, 1, h, :], ["ZZ%d" % zc, "QL%d" % nxt], [bztn])
                            TT("dve", ZZ[zn][:, 1, hs, :], bzt[:, :].rearrange("p (a s) -> p a s", a=4), ZZ[zc][:, 1, hs, :], ALU.add,
                               [bztn, "ZZ%d" % zc], ["ZZ%d" % zn])
                    cur = nxt
                    zc = zn
                Zf = ZZ[zc]
                zfn = "ZZ%d" % zc
                bm, bmn = bank()
                for h in range(8):
                    MM(bm[:, h * 64:(h + 1) * 64], G[:, h, 2, :], Bq["Vb"][:, h * 64:(h + 1) * 64], ["G", "b_Vb"], [bmn])
                CP("act", Bq["MV"][:], bm[:, :], [bmn], ["b_MV"])
                bvb, bvbn = bank()
                for h in range(8):
                    MM(bvb[:, h * 64:(h + 1) * 64], Zf[:, 0, h, :], Bq["MV"][:, h * 64:(h + 1) * 64], [zfn, "b_MV"], [bvbn])
                CP("act", F["Vbar"][:], bvb[:, :], [bvbn], ["f_Vbar"])
                for g in range(2):
                    ba, ban = bank()
                    for hh in range(4):
                        h = 4 * g + hh
                        MM(ba[0:64, hh * 128:(hh + 1) * 128], Bq["At"][:, h * 64:(h + 1) * 64], Zf[:, 0, h, :], ["b_At", zfn], [ban])
                    CP("dve", AbT[:, 4 * g:4 * g + 4, :], ba[0:64, :].rearrange("p (a s) -> p a s", a=4), [ban], ["AbT"])
                for c in range(4):
                    r0 = 32 * c
                    rs = slice(r0, r0 + 32)
                    if is_smp:
                        j = 4 * (t - NCTX - NMAIN) + c
                        DMA("sp", Sraw[:], state0[j].rearrange("(hp two) i j -> (two i) hp j", two=2), "Sraw", w=["Sraw"])
                        for hp in range(4):
                            TR(ps_f[0:64, hp * 128:(hp + 1) * 128], Sraw[:, hp, :], identf[:], ["Sraw", "identf"], ["ps_f"])
                        CP("dve", Hs[:], ps_f[0:64, :].rearrange("p (h i) -> p h i", h=8), ["ps_f"], ["Hs"])
                        CP("act", Hb[:], Hs[:], ["Hs"], ["Hb"])
                    bu, bun = bank()
                    hn, hnn = bank()
                    if c < 3:
                        ms = rs if c < 2 else slice(64, 128)
                        for h in range(8):
                            MM(bu[rs, h * 64:(h + 1) * 64], AbT[:, h, rs], Hb[:, h, :], ["AbT", "Hb"], [bun])
                        TT("dve", Bq["U"][rs, :], bu[rs, :], F["Vbar"][rs, :], ALU.add, [bun, "f_Vbar"], ["b_U"])
                        for h in range(8):
                            MM(ps_y1[ms, h * 64:(h + 1) * 64], ART[:, h, 1, ms], Hb[:, h, :], ["ART", "Hb"], ["ps_y1"])
                        for h in range(8):
                            hsl = slice(h * 64, (h + 1) * 64)
                            MM(hn[0:64, hsl], Bq["Bh"][rs, hsl], Bq["U"][rs, hsl], ["b_Bh", "b_U"], [hnn], start=True, stop=False)
                            MM(hn[0:64, hsl], Bq["Kh"][rs, hsl], Bq["Vb"][rs, hsl], ["b_Kh", "b_Vb"], [hnn], start=False, stop=True)
                    else:
                        ms = slice(64, 128)
                        for h in range(8):
                            MM(bu[ms, h * 64:(h + 1) * 64], AbT[:, h, ms], Hb[:, h, :], ["AbT", "Hb"], [bun])
                        TT("dve", F["tA"][ms, :], bu[ms, :], F["Vbar"][ms, :], ALU.add, [bun, "f_Vbar"], ["f_tA"])
                        TS("dve", Bq["U3"][ms, :], F["tA"][ms, :], rowm[ms, 0:1], None, ALU.mult, None, ["f_tA", "rowm"], ["b_U3"])
                        STT("dve", Bq["U"][ms, :], Bq["U"][ms, :], rowm[ms, 1:2], Bq["U3"][ms, :], ALU.mult, ALU.add, ["b_U", "b_U3", "rowm"], ["b_U"])
                        by3, by3n = bank()
                        for h in range(8):
                            MM(by3[ms, h * 64:(h + 1) * 64], ART[:, h, 1, ms], Hb[:, h, :], ["ART", "Hb"], [by3n])
                        TS("dve", F["tC"][ms, :], by3[ms, :], rowm[ms, 0:1], None, ALU.mult, None, [by3n, "rowm"], ["f_tC"])
                        for h in range(8):
                            hsl = slice(h * 64, (h + 1) * 64)
                            MM(hn[0:64, hsl], Bq["Bh"][ms, hsl], Bq["U3"][ms, hsl], ["b_Bh", "b_U3"], [hnn], start=True, stop=False)
                            MM(hn[0:64, hsl], Bq["Kh"][ms, hsl], Bq["Vb3"][ms, hsl], ["b_Kh", "b_Vb3"], [hnn], start=False, stop=True)
                    TT("dve", Hs[:], Hs[:], WcT[:, :, c:c + 1].to_broadcast([64, 8, 64]), ALU.mult, ["Hs", "WcT"], ["Hs"])
                    TT("dve", Hs[:], Hs[:], hn[0:64, :].rearrange("p (h i) -> p h i", h=8), ALU.add, ["Hs", hnn], ["Hs"])
                    CP("act", Hb[:], Hs[:], ["Hs"], ["Hb"])
                    if is_smp:
                        state_out(o_sstate[j])
                if t == NCTX + NMAIN - 1:
                    state_out(o_pstate[:, :, :])
                if t < NCTX:
                    continue
                CP("act", F["Y1"][:], ps_y1[:, :], ["ps_y1"], ["f_Ein"])
                STT("dve", F["Y1"][64:128, :], F["Y1"][64:128, :], rowm[64:128, 1:2], F["tC"][64:128, :], ALU.mult, ALU.add,
                    ["f_Ein", "f_tC", "rowm"], ["f_Ein"])
                by, byn = bank()
                for h in range(8):
                    hsl = slice(h * 64, (h + 1) * 64)
                    MM(by[:, hsl], G[:, h, 1, :], Bq["U"][:, hsl], ["G", "b_U"], [byn], start=True, stop=False)
                    MM(by[:, hsl], G[:, h, 3, :], Bq["Vb"][:, hsl], ["G", "b_Vb"], [byn], start=False, stop=True)
                TT("dve", F["Y"][:], by[:, :], F["Y1"][:], ALU.add, [byn, "f_Ein"], ["f_Eneg"])
                RED("dve", s8["m8"][:], v8(F["Y"][:]), ALU.add, ["f_Eneg"], ["s8_m8"])
                TS("dve", s8["m8"][:], s8["m8"][:], 1.0 / 64, None, ALU.mult, None, ["s8_m8"], ["s8_m8"])
                TT("dve", v8(F["ym"][:]), v8(F["Y"][:]), b8(s8["m8"][:]), ALU.subtract, ["f_Eneg", "s8_m8"], ["f_Eex"])
                TT("pool", F["tA"][:], F["ym"][:], F["ym"][:], ALU.mult, ["f_Eex"], ["f_tA"])
                RED("dve", s8["v8"][:], v8(F["tA"][:]), ALU.add, ["f_tA"], ["s8_v8"])
                TS("dve", s8["v8"][:], s8["v8"][:], 1.0 / 64, 64e-5, ALU.mult, ALU.add, ["s8_v8"], ["s8_v8"])
                P.op("act", lambda e: e.sqrt(out=s8["r8"][:], in_=s8["v8"][:]), ["s8_v8"], ["s8_r8"])
                RECIP(s8["r8"][:], s8["r8"][:], ["s8_r8"], ["s8_r8"])
                TT("dve", v8(F["yn"][:]), v8(F["ym"][:]), b8(s8["r8"][:]), ALU.mult, ["f_Eex", "s8_r8"], ["f_Etot"])
                TT("pool", F["yn"][:], F["yn"][:], par(5), ALU.mult, ["f_Etot", "parb"], ["f_Etot"])
                TT("pool", F["yn"][:], F["yn"][:], par(6), ALU.add, ["f_Etot", "parb"], ["f_Etot"])
                TT("dve", v8(F["tB"][:]), v8(v_), b8(s8["bsum"][:]), ALU.mult, ["xs", "s8_bsum"], ["f_tB"])
                TT("pool", F["yn"][:], F["yn"][:], F["tB"][:], ALU.add, ["f_Etot", "f_tB"], ["f_Etot"])
                TT("dve", yrf[:], F["yn"][:], F["gsb"][:], ALU.mult, ["f_Etot", "f_gsb"], ["yrf"])
                if dbg:
                    DMA("sp", o_yr[to * 128:(to + 1) * 128, :], yrf[:], "yrf", r=["yrf"])
                CP("act", Bq["yrb"][:], yrf[:], ["yrf"], ["b_yrb"])
                for q in range(4):
                    TR(ps_tr[:, q, :], Bq["yrb"][:, q * 128:(q + 1) * 128], identb[:], ["b_yrb", "identb"], ["ps_tr"])
                CP("dve", yrTt[:], ps_tr[:, 0:4, :], ["ps_tr"], ["yrTt"])
                DMA("sp", yrd[:, :, to * 128:(to + 1) * 128], yrTt[:], "yrTt", r=["yrTt"])

        if do_rwkv:
            P.barrier()
            cur_stack.pop()
            st_rw.close()
            banks.append(ps_y1)

        NBd = len(banks)
        bank_i[0] = 0

        def bank():
            i = bank_i[0] % NBd
            bank_i[0] += 1
            return banks[i], "bank%d" % banks_id[id(banks[i])]

        if do_dense:
            h2T = sb("h2T", [128, 8, NOUT * 128], BF16)
            st_a = ExitStack()
            cur_stack.append(st_a)
            wgr = sb("wgr", [128, 8, 1024], BF16)
            wgn = sb("wgn", [128, 8, 1024], BF16)
            wbr = sb("wbr", [128, 4, 1024], BF16)
            wbn = sb("wbn", [128, 4, 1024], BF16)
            wo = sb("wo", [128, 8, 1024], BF16)
            gffn = sb("gffn", [128, 8])
            DMA("pool", wgr[:], w_in_v[:, :, 3128:4152], "wgr", w=["wgr"])
            DMA("pool", wgn[:], w_in_v[:, :, 4152:5176], "wgn", w=["wgn"])
            DMA("pool", wbr[:], w_br_rwkv.rearrange("(k p) n -> p k n", p=128), "wbr", w=["wbr"])
            DMA("pool", wbn[:], w_br_nsa.rearrange("(k p) n -> p k n", p=128), "wbn", w=["wbn"])
            DMA("pool", wo[:], w_out.rearrange("(k p) n -> p k n", p=128), "wo", w=["wo"])
            DMA("sp", gffn[:], g_ffn[:, :], "gffn", w=["gffn"])
            xg = sb("xg", [128, 4, D])
            hT4 = sb("hT4", [128, 8, 512], BF16)
            mT = sb("mT", [128, 8, 512], BF16)
            sg1 = sb("sg1", [128, 512])
            sg2 = sb("sg2", [128, 512])
            x2t = [sb("x2t%d" % i, [128, D]) for i in range(2)]
            yrT = sb("yrg", [128, 4, 512], BF16)
            ynT = sb("yng", [128, 4, 512], BF16)
            for grp in range(NOUT // 4):
                cols = slice(0, 512)
                DMA("sp", yrT[:], yrd[:, :, grp * 512:(grp + 1) * 512], "yrT", w=["yrT"])
                if do_nsa:
                    DMA("sp", ynT[:], ynd[:, :, grp * 512:(grp + 1) * 512], "ynT", w=["ynT"])
                elif grp == 0:
                    MSET("pool", ynT[:], 0.0, ["ynT"])
                for i in range(4):
                    t = NCTX + 4 * grp + i
                    load_norm(t, xg[:, i, :], "xg%d" % i, hT4[:, :, i * 128:(i + 1) * 128], "hT4")
                for m in range(8):
                    ms = slice(m * 128, (m + 1) * 128)
                    b_gr, n_gr = bank()
                    for k in range(8):
                        MM(b_gr[:, :], wgr[:, k, ms], hT4[:, k, :], ["wgr", "hT4"], [n_gr], start=(k == 0), stop=(k == 7))
                    b_gn, n_gn = bank()
                    for k in range(8):
                        MM(b_gn[:, :], wgn[:, k, ms], hT4[:, k, :], ["wgn", "hT4"], [n_gn], start=(k == 0), stop=(k == 7))
                    b_yr, n_yr = bank()
                    for k in range(4):
                        MM(b_yr[:, :], wbr[:, k, ms], yrT[:, k, cols], ["wbr", "yrT"], [n_yr], start=(k == 0), stop=(k == 3))
                    b_yn, n_yn = bank()
                    for k in range(4):
                        MM(b_yn[:, :], wbn[:, k, ms], ynT[:, k, cols], ["wbn", "ynT"], [n_yn], start=(k == 0), stop=(k == 3))
                    ACT(sg1[:], b_gr[:, :], AF.Sigmoid, [n_gr], ["sg1"])
                    ACT(sg2[:], b_gn[:, :], AF.Sigmoid, [n_gn], ["sg2"])
                    TT("dve", sg1[:], sg1[:], b_yr[:, :], ALU.mult, ["sg1", n_yr], ["sg1"])
                    TT("dve", sg2[:], sg2[:], b_yn[:, :], ALU.mult, ["sg2", n_yn], ["sg2"])
                    TT("pool", mT[:, m, :], sg1[:], sg2[:], ALU.add, ["sg1", "sg2"], ["mT"])
                for i in range(4):
                    to = 4 * grp + i
                    X2 = x2t[to % 2]
                    x2n = "x2t%d" % (to % 2)
                    for half in range(2):
                        hs_ = slice(half * 512, (half + 1) * 512)
                        b, bn = bank()
                        for m in range(8):
                            MM(b[:, :], mT[:, m, i * 128:(i + 1) * 128], wo[:, m, hs_], ["mT", "wo"], [bn], start=(m == 0), stop=(m == 7))
                        TT("dve", X2[:, hs_], b[:, :], xg[:, i, hs_], ALU.add, [bn, "xg%d" % i], [x2n])
                    DMA("sp", x2d[to * 128:(to + 1) * 128, :], X2[:], x2n, r=[x2n])
                    ACT(sq[:], X2[:], AF.Square, [x2n], ["sq", "ss"], accum=ss[:])
                    TS("dve", rstd[:], ss[:], 1.0 / D, 1e-6, ALU.mult, ALU.add, ["ss"], ["rstd"])
                    P.op("act", lambda e: e.sqrt(out=rstd[:], in_=rstd[:]), ["rstd"], ["rstd"])
                    RECIP(rstd[:], rstd[:], ["rstd"], ["rstd"])
                    TS("dve", xn[:], X2[:], rstd[:, 0:1], None, ALU.mult, None, [x2n, "rstd"], ["xn"])
                    for k in range(8):
                        TR(ps_tr[:, k, :], xn[:, k * 128:(k + 1) * 128], identb[:], ["xn", "identb"], ["ps_tr"])
                    TT("dve", h2T[:, :, to * 128:(to + 1) * 128], ps_tr[:], gffn[:].unsqueeze(2).to_broadcast([128, 8, 128]), ALU.mult,
                       ["ps_tr", "gffn"], ["h2T"])
            P.barrier()
            cur_stack.pop()
            st_a.close()
        if do_dense:
            st_b = ExitStack()
            cur_stack.append(st_b)
            NF = 22
            GT = NOUT // 2
            GW = GT * 128
            wd = sb("wd", [128, NF, D], BF16)
            wd_v = w_ffn_down.rearrange("(f p) n -> p f n", p=128)
            for f0 in range(0, NF, 2):
                DMA("pool", wd[:, f0:f0 + 2, :], wd_v[:, f0:f0 + 2, :], "wd", w=["wd"])
            gfin = sb("gfin", [128, D])
            DMA("sp", gfin[:], g_fin_b[:, :], "gfin", w=["gfin"])
            actT = sb("actT", [128, NF, GW], BF16)
            wgu = [sb("wgu%d" % i, [128, 2, 8, 128], BF16) for i in range(3)]
            sgt = sb("sgt", [128, 512])
            x2r = [sb("x2r%d" % i, [128, D]) for i in range(2)]
            yo = [sb("yo%d" % i, [128, D]) for i in range(2)]
            wg_v = w_ffn_gate.rearrange("(k p) n -> p k n", p=128)
            wu_v = w_ffn_up.rearrange("(k p) n -> p k n", p=128)
            for grp in range(2):
                c0 = grp * GW
                for f in range(NF):
                    W = wgu[f % 3]
                    wn = "wgu%d" % (f % 3)
                    DMA("pool", W[:, 0, :, :], wg_v[:, :, f * 128:(f + 1) * 128], wn, w=[wn])
                    DMA("pool", W[:, 1, :, :], wu_v[:, :, f * 128:(f + 1) * 128], wn, w=[wn])
                    for (n0, nw) in ((0, 512), (512, 512), (1024, 256)):
                        bg, bgn = bank()
                        for k in range(8):
                            MM(bg[:, 0:nw], W[:, 0, k, :], h2T[:, k, c0 + n0:c0 + n0 + nw], [wn, "h2T"], [bgn], start=(k == 0), stop=(k == 7))
                        bu, bun = bank()
                        for k in range(8):
                            MM(bu[:, 0:nw], W[:, 1, k, :], h2T[:, k, c0 + n0:c0 + n0 + nw], [wn, "h2T"], [bun], start=(k == 0), stop=(k == 7))
                        ACT(sgt[:, 0:nw], bg[:, 0:nw], AF.Silu, [bgn], ["sgt"])
                        TT("dve", actT[:, f, n0:n0 + nw], sgt[:, 0:nw], bu[:, 0:nw], ALU.mult, ["sgt", bun], ["actT"])
                for i in range(GT):
                    to = grp * GT + i
                    X2 = x2r[to % 2]
                    x2n = "x2r%d" % (to % 2)
                    Y = yo[to % 2]
                    yn_ = "yo%d" % (to % 2)
                    DMA("sp", X2[:], x2d[to * 128:(to + 1) * 128, :], x2n, w=[x2n])
                    for half in range(2):
                        hs_ = slice(half * 512, (half + 1) * 512)
                        b, bn = bank()
                        for f in range(NF):
                            MM(b[:, :], actT[:, f, i * 128:(i + 1) * 128], wd[:, f, hs_], ["actT", "wd"], [bn], start=(f == 0), stop=(f == NF - 1))
                        TT("dve", Y[:, hs_], b[:, :], X2[:, hs_], ALU.add, [bn, x2n], [yn_])
                    ACT(sq[:], Y[:], AF.Square, [yn_], ["sq", "ss"], accum=ss[:])
                    TS("dve", rstd[:], ss[:], 1.0 / D, 1e-6, ALU.mult, ALU.add, ["ss"], ["rstd"])
                    P.op("act", lambda e: e.sqrt(out=rstd[:], in_=rstd[:]), ["rstd"], ["rstd"])
                    RECIP(rstd[:], rstd[:], ["rstd"], ["rstd"])
                    STT("dve", Y[:], Y[:], rstd[:, 0:1], gfin[:], ALU.mult, ALU.mult, [yn_, "rstd", "gfin"], [yn_])
                    DMA("sp", o_y[to * 128:(to + 1) * 128, :], Y[:], yn_, r=[yn_])
            P.barrier()
            cur_stack.pop()
            st_b.close()
        P.barrier()
        P.emit()
    return nc


def _consts():
    idx = np.arange(128)
    same = (idx[:, None] // CH) == (idx[None, :] // CH)
    MsT = (same & (idx[:, None] < idx[None, :])).astype(np.float32)
    MiT = (same & (idx[:, None] <= idx[None, :])).astype(np.float32)
    Ms = MsT.T.copy()
    masks_bf = np.stack([MsT, MiT, Ms], axis=1).astype(np.float32)
    masks_f = np.stack([MiT, same.astype(np.float32)], axis=1).astype(np.float32)
    chunk_ind = (idx[:, None] // CH == np.arange(4)[None, :]).astype(np.float32)
    rowm = np.stack([(idx >= 96), (idx < 96)], 1).astype(np.float32)
    return dict(ident=np.eye(128, dtype=np.float32), masks_bf=masks_bf, masks_f=masks_f, chunk_ind=chunk_ind, rowm=rowm)


def _nsa_tables(half):
    NPT = NCTX + NMAIN
    p = np.arange(128)
    valid_t = np.ones((128, NPT), np.float32)
    if half == 0:
        valid_t[:, 0:NCTX] = 0.0
    n = np.arange(256)
    nabs = n - (0 if half == 1 else 128)
    okc = (nabs >= 0) & (n <= 254)
    valid_c = okc.astype(np.float32).reshape(2, 128).T.copy()
    j = np.arange(64)
    ovl = ((n[:, None] * 16 < (j[None, :] + 1) * 64) & (n[:, None] * 16 + 32 > j[None, :] * 64) & okc[:, None]).astype(np.float32)
    ovl = ovl.reshape(2, 128, 64).transpose(1, 0, 2).copy()
    s_ = np.arange(NPT * 128)
    E = (s_[None, :] // 64 == j[:, None]).astype(np.float32)
    tq = np.arange(NMAIN * 128)
    pos = half * 2048 + tq
    mc = (okc[:, None] & (16 * nabs[:, None] + 31 <= pos[None, :])).astype(np.float32)
    maskc = mc.reshape(2, 128, NMAIN * 128).transpose(1, 0, 2).copy()
    cur = pos // 64
    jabs = j - (0 if half == 1 else 32)
    allowed = ((jabs[None, :] >= 0) & (jabs[None, :] <= cur[:, None]))
    forced = allowed & ((jabs[None, :] == 0) | (jabs[None, :] == cur[:, None]) | (jabs[None, :] == cur[:, None] - 1))
    fb = 1e6 * forced.astype(np.float32) - 1e9 * (1.0 - allowed.astype(np.float32))
    allowed = allowed.astype(np.float32).reshape(NMAIN, 128, 64).transpose(1, 0, 2).copy()
    fb = fb.astype(np.float32).reshape(NMAIN, 128, 64).transpose(1, 0, 2).copy()
    tri = np.stack([(p[:, None] <= p[None, :]), (p[:, None] > p[None, :])], 1).astype(np.float32)
    return dict(valid_t=valid_t, valid_c=valid_c, ovl=ovl, E_ind=E, ones_row=np.ones((1, NPT * 128), np.float32),
                maskc=maskc, allowed=allowed, fbias=fb, tri=tri)


def _smp_tables():
    p = np.arange(128)
    n = np.arange(128)
    j = np.arange(64)
    ovl = ((n[:, None] * 16 < (j[None, :] + 1) * 64) & (n[:, None] * 16 + 32 > j[None, :] * 64) & (n[:, None] <= 126)).astype(np.float32)
    cur = 32
    allowed = np.broadcast_to((j <= cur)[None, :], (128, 64))
    forced = allowed & ((j == 0) | (j == cur) | (j == cur - 1))[None, :]
    fb = (1e6 * forced - 1e9 * (1.0 - allowed)).astype(np.float32)
    t = np.arange(32)
    mw0 = ((p[:, None] > t[None, :] - 24) | (t[None, :] < 24)).astype(np.float32)
    return dict(iota_p=p.astype(np.float32).reshape(128, 1), ovl_abs=ovl, allowed_s=allowed.astype(np.float32).copy(),
                fbias_s=fb.copy(), maskw0=mw0)


def _rope_tab(pos):
    half = 8
    inv = (500000.0 ** (-np.arange(half, dtype=np.float32) / half)).astype(np.float32)
    ang = pos.astype(np.float32)[:, None] * inv[None, :]
    return np.cos(ang).astype(np.float32), np.sin(ang).astype(np.float32)


def make_in_maps(inputs):
    f = lambda a: np.ascontiguousarray(np.asarray(a, dtype=np.float32))
    x_prompt = f(inputs["x_prompt"])
    x_sample = f(inputs["x_sample"])
    C = _consts()
    w_in = f(inputs["w_in"][0])
    g_mix = f(inputs["g_mix"][0]).reshape(8, 128).T.copy()
    mu_b = np.broadcast_to(f(inputs["rwkv_mu"][0])[None, :], (128, RW)).copy()
    pars = [inputs["rwkv_w0"][0], inputs["rwkv_a0"][0], inputs["rwkv_k_k"][0], inputs["rwkv_k_a"][0],
            np.asarray(inputs["rwkv_r_k"][0]).reshape(512), inputs["rwkv_ln_w"][0], inputs["rwkv_ln_b"][0]]
    par_b = np.broadcast_to(np.stack([f(p) for p in pars], 0)[None], (128, 7, 512)).copy()
    shared = dict(w_in=w_in, g_mix=g_mix, mu_b=mu_b, par_b=par_b,
                  w_decay=f(inputs["rwkv_w_decay"][0]), w_aaa=f(inputs["rwkv_w_aaa"][0]),
                  w_gate=f(inputs["rwkv_w_gate"][0]),
                  w_br_rwkv=f(inputs["w_br_rwkv"][0]), w_br_nsa=f(inputs["w_br_nsa"][0]), w_out=f(inputs["w_out"][0]),
                  g_ffn=f(inputs["g_ffn"][0]).reshape(8, 128).T.copy(),
                  w_ffn_gate=f(inputs["w_ffn_gate"][0]), w_ffn_up=f(inputs["w_ffn_up"][0]), w_ffn_down=f(inputs["w_ffn_down"][0]),
                  g_fin_b=np.broadcast_to(f(inputs["g_final"])[None, :], (128, D)).copy(),
                  w1h=f(inputs["nsa_w_cmp1"][0]).transpose(2, 0, 1, 3).copy(),
                  peh=f(inputs["nsa_pe_cmp"][0]).transpose(2, 0, 1).copy(),
                  w2h=f(inputs["nsa_w_cmp2"][0]).transpose(1, 0, 2).copy(), **C)
    cache_c = f(inputs["cache_cmp_kv"][0]).reshape(2560, 128, 256)
    cache_s = f(inputs["cache_slc_kv"][0]).reshape(2560, 128, 256)
    maps = []
    for c in range(8):
        s, half = c // 2, c % 2
        xr = np.zeros((NT * 128, D), np.float32)
        if half == 1:
            xr[0:2048] = x_prompt[s, 0:2048]
        xr[2048:4096] = x_prompt[s, half * 2048:(half + 1) * 2048]
        pos = np.zeros(NT * 128, np.float32)
        pos[0:2048] = np.arange(2048) if half == 1 else 0
        pos[2048:4096] = half * 2048 + np.arange(2048)
        rv = np.ones((NT, 128), np.float32)
        for j in range(16):
            b = 16 * c + j
            r0 = 4096 + j * 32
            xr[r0 + 24:r0 + 32] = x_sample[b]
            pos[r0 + 24:r0 + 32] = 2048 + np.arange(8)
            rv[(r0 // 128), (r0 % 128):(r0 % 128) + 24] = 0.0
        cos, sin = _rope_tab(pos)
        cs = np.stack([cos, sin], 1).reshape(NT, 128, 2, 8).transpose(1, 0, 2, 3).copy()
        nsa = _nsa_tables(half)
        m = dict(shared)
        m.update(nsa)
        m.update(_smp_tables())
        m["page_tab"] = np.ascontiguousarray(np.asarray(inputs["page_table"])[16 * c:16 * c + 16].reshape(1, 256).astype(np.int32))
        m["cache_c"] = cache_c
        m["cache_s"] = cache_s
        m.update(xrows=xr, rowvalid=rv.T.copy(), cs_tab=cs,
                 shift0=f(inputs["state_rwkv_shift"][0, 16 * c:16 * c + 16]),
                 state0=f(inputs["state_rwkv"][0, 16 * c:16 * c + 16]),
                 cwin=f(inputs["cache_win_kv"][0, 16 * c:16 * c + 16]).reshape(16, 512, 256))
        maps.append(m)
    return maps


_NC_CACHE = {}


def kernel(**inputs):
    if "nc" not in _NC_CACHE:
        _NC_CACHE["nc"] = build_program()
    nc = _NC_CACHE["nc"]
    maps = make_in_maps(inputs)
    res = run_bass_kernel_spmd(nc, maps, core_ids=list(range(8)))
    R = res.results
    _NC_CACHE["last"] = R
    B, T = 4, 4096
    p_cmp = np.zeros((1, B, T, 2, 2, 64), np.float32)
    p_slc = np.zeros((1, B, T, 2, 2, 64), np.float32)
    p_win = np.zeros((1, B, 512, 2, 2, 64), np.float32)
    p_rwkv = np.zeros((1, B, 8, 64, 64), np.float32)
    p_shift = np.zeros((1, B, RW), np.float32)
    s_cmp = np.zeros((1, 128, 8, 2, 2, 64), np.float32)
    s_slc = np.zeros((1, 128, 8, 2, 2, 64), np.float32)
    s_win = np.zeros((1, 128, 512, 2, 2, 64), np.float32)
    s_rwkv = np.zeros((1, 128, 8, 64, 64), np.float32)
    s_shift = np.zeros((1, 128, RW), np.float32)
    y_p = np.zeros((B, T, D), np.float32)
    y_s = np.zeros((128, 8, D), np.float32)
    for c in range(8):
        s, half = c // 2, c % 2
        kvr = R[c]["o_kv"]
        main = kvr[0:2048]
        p_cmp[0, s, half * 2048:(half + 1) * 2048] = main[:, 0:256].reshape(2048, 2, 2, 64)
        p_slc[0, s, half * 2048:(half + 1) * 2048] = main[:, 256:512].reshape(2048, 2, 2, 64)
        if half == 1:
            p_win[0, s] = main[2048 - 512:, 512:768].reshape(512, 2, 2, 64)
        smp = kvr[2048:].reshape(16, 32, 768)[:, 24:32]
        s_cmp[0, 16 * c:16 * c + 16] = smp[:, :, 0:256].reshape(16, 8, 2, 2, 64)
        s_slc[0, 16 * c:16 * c + 16] = smp[:, :, 256:512].reshape(16, 8, 2, 2, 64)
        if "o_swin" in R[c]:
            s_win[0, 16 * c:16 * c + 16] = R[c]["o_swin"].reshape(16, 512, 2, 2, 64)
        if "o_pshift" in R[c] and half == 1:
            p_shift[0, s] = R[c]["o_pshift"][0]
            p_rwkv[0, s] = R[c]["o_pstate"]
        if "o_sshift" in R[c]:
            s_shift[0, 16 * c:16 * c + 16] = R[c]["o_sshift"]
            s_rwkv[0, 16 * c:16 * c + 16] = R[c]["o_sstate"]
        if "o_y" in R[c]:
            yr = R[c]["o_y"]
            y_p[s, half * 2048:(half + 1) * 2048] = yr[0:2048]
            y_s[16 * c:16 * c + 16] = yr[2048:].reshape(16, 32, D)[:, 24:32]
    return (y_p, y_s, p_cmp, p_slc, p_win, p_rwkv, p_shift, s_cmp, s_slc, s_win, s_rwkv, s_shift)
```

```python
from contextlib import ExitStack
import numpy as np
import concourse.bass as bass
import concourse.mybir as mybir
from concourse.bass_utils import run_bass_kernel_spmd

F32 = mybir.dt.float32
BF16 = mybir.dt.bfloat16
I32 = mybir.dt.int32
ALU = mybir.AluOpType
AF = mybir.ActivationFunctionType
AX = mybir.AxisListType

ENGS = ("pe", "act", "dve", "pool", "sp")

NCTX, NMAIN, NSMP = 16, 16, 4
NT = NCTX + NMAIN + NSMP
NOUT = NMAIN + NSMP
D = 1024
RW = 1824
CH = 32
EXPM05 = float(np.exp(-0.5))


class Res:
    __slots__ = ("name", "writer", "readers", "dsem", "ndma", "dma_rd")

    def __init__(self, name):
        self.name = name
        self.writer = None
        self.readers = []
        self.dsem = None
        self.ndma = 0
        self.dma_rd = []


class Prog:
    def __init__(self, nc, stack):
        self.nc = nc
        self.stack = stack
        self.streams = {e: [] for e in ENGS}
        self.count = {e: 0 for e in ENGS}
        self.sem = {e: stack.enter_context(nc.semaphore("cnt_" + e)) for e in ENGS if e != "sp"}
        self.waited = {e: {} for e in ENGS}
        self.res = {}
        self.n_inst = 0

    def R(self, name):
        r = self.res.get(name)
        if r is None:
            r = Res(name)
            self.res[name] = r
        return r

    def _need(self, eng, src, val, out):
        if src == eng and eng == "pe":
            return
        w = self.waited[eng]
        if w.get(src, 0) >= val:
            return
        w[src] = val
        out.append(("c", src, val))

    def _need_dma(self, eng, r, out):
        if r.dsem is None or r.ndma == 0:
            return
        w = self.waited[eng]
        key = ("d", r.name)
        val = 16 * r.ndma
        if w.get(key, 0) >= val:
            return
        w[key] = val
        out.append(("d", r.dsem, val))

    def _deps(self, eng, reads, writes):
        waits = []
        for rn in reads:
            r = self.R(rn)
            if r.writer is not None:
                self._need(eng, r.writer[0], r.writer[1], waits)
            self._need_dma(eng, r, waits)
        for rn in writes:
            r = self.R(rn)
            if r.writer is not None:
                self._need(eng, r.writer[0], r.writer[1], waits)
            for (e2, c2) in r.readers:
                self._need(eng, e2, c2, waits)
            self._need_dma(eng, r, waits)
            for (dr, val) in r.dma_rd:
                w = self.waited[eng]
                key = ("d", dr.name)
                if w.get(key, 0) < val:
                    w[key] = val
                    waits.append(("d", dr.dsem, val))
        return waits

    def op(self, eng, fn, reads=(), writes=()):
        waits = self._deps(eng, reads, writes)
        self.count[eng] += 1
        c = self.count[eng]
        for rn in reads:
            self.R(rn).readers.append((eng, c))
        for rn in writes:
            r = self.R(rn)
            r.writer = (eng, c)
            r.readers = []
            r.dma_rd = []
        self.streams[eng].append((waits, fn, ("c", eng)))
        self.n_inst += 1

    def dma(self, q, fn, res, reads=(), writes=()):
        r = self.R(res)
        if r.dsem is None:
            r.dsem = self.stack.enter_context(self.nc.semaphore("d_" + r.name))
        waits = self._deps(q, reads, writes)
        r.ndma += 1
        for rn in reads:
            if rn != res:
                self.R(rn).dma_rd.append((r, 16 * r.ndma))
        self.streams[q].append((waits, fn, ("d", r.dsem)))
        self.n_inst += 1

    def barrier(self):
        for eng in ENGS:
            waits = []
            for r in self.res.values():
                self._need_dma(eng, r, waits)
            for e in ENGS:
                if e != "sp" and e != eng and self.count[e] > 0:
                    self._need(eng, e, self.count[e], waits)
            if waits:
                self.streams[eng].append((waits, None, None))

    def emit(self):
        prog = self

        def run(engname, engobj):
            for (waits, fn, inc) in prog.streams[engname]:
                for w in waits:
                    if w[0] == "c":
                        engobj.wait_ge(prog.sem[w[1]], w[2])
                    else:
                        engobj.wait_ge(w[1], w[2])
                if fn is None:
                    continue
                ins = fn(engobj)
                if inc[0] == "c":
                    ins.then_inc(prog.sem[inc[1]], 1)
                else:
                    ins.then_inc(inc[1], 16)

        with self.nc.Block() as block:
            @block.tensor
            def _(e):
                run("pe", e)

            @block.scalar
            def _(e):
                run("act", e)

            @block.vector
            def _(e):
                run("dve", e)

            @block.gpsimd
            def _(e):
                run("pool", e)

            @block.sync
            def _(e):
                run("sp", e)


def build_program(do_rwkv=True, do_dense=True, do_nsa=True, dbg=False, nsa_stop=99, nsa_m=NMAIN, nsa_seq=16, dbg_br=False):
    nc = bass.Bass("TRN2", target_bir_lowering=False)
    din = lambda name, shape, dt=F32: nc.dram_tensor(name, list(shape), dt, kind="ExternalInput").ap()
    dout = lambda name, shape, dt=F32: nc.dram_tensor(name, list(shape), dt, kind="ExternalOutput").ap()

    xrows = din("xrows", [NT * 128, D])
    w_in = din("w_in", [D, 5176])
    g_mix = din("g_mix", [128, 8])
    mu_b = din("mu_b", [128, RW])
    par_b = din("par_b", [128, 7, 512])
    w_decay = din("w_decay", [64, 512])
    w_aaa = din("w_aaa", [64, 512])
    w_gate = din("w_gate", [160, 512])
    ident = din("ident", [128, 128])
    masks_bf = din("masks_bf", [128, 3, 128])
    masks_f = din("masks_f", [128, 2, 128])
    chunk_ind = din("chunk_ind", [128, 4])
    rowm_d = din("rowm", [128, 2])
    rowvalid = din("rowvalid", [128, NT])
    cs_tab = din("cs_tab", [128, NT, 2, 8])
    shift0 = din("shift0", [16, RW])
    state0 = din("state0", [16, 8, 64, 64])
    cwin = din("cwin", [16, 512, 256])
    valid_d = din("valid_t", [128, NCTX + NMAIN])
    validc_d = din("valid_c", [128, 2])
    ovl_d = din("ovl", [128, 2, 64])
    E_d = din("E_ind", [64, (NCTX + NMAIN) * 128])
    ones_d = din("ones_row", [1, (NCTX + NMAIN) * 128])
    w1_d = din("w1h", [64, 2, 32, 128])
    pe_d = din("peh", [64, 2, 32])
    w2_d = din("w2h", [128, 2, 64])
    maskc_d = din("maskc", [128, 2, NMAIN * 128])
    allowed_d = din("allowed", [128, NMAIN, 64])
    fbias_d = din("fbias", [128, NMAIN, 64])
    tri_d = din("tri", [128, 2, 128])
    page_tab = din("page_tab", [1, 256], I32)
    iota_p = din("iota_p", [128, 1])
    ovl_abs = din("ovl_abs", [128, 64])
    allowed_s = din("allowed_s", [128, 64])
    fbias_s = din("fbias_s", [128, 64])
    maskw0 = din("maskw0", [128, 32])
    cache_c = din("cache_c", [2560, 128, 256])
    cache_s = din("cache_s", [2560, 128, 256])

    w_br_rwkv = din("w_br_rwkv", [512, D])
    w_br_nsa = din("w_br_nsa", [512, D])
    w_out = din("w_out", [D, D])
    g_ffn = din("g_ffn", [128, 8])
    w_ffn_gate = din("w_ffn_gate", [D, 2816])
    w_ffn_up = din("w_ffn_up", [D, 2816])
    w_ffn_down = din("w_ffn_down", [2816, D])
    g_fin_b = din("g_fin_b", [128, D])
    x2d = nc.dram_tensor("x2_scratch", [NOUT * 128, D], F32).ap()
    o_y = dout("o_y", [NOUT * 128, D])
    o_kv = dout("o_kv", [NOUT * 128, 768])
    o_swin = dout("o_swin", [16, 512, 256])
    o_prow = dout("o_prow", [NOUT * 128, RW]) if dbg else None
    o_pshift = dout("o_pshift", [1, RW])
    o_sshift = dout("o_sshift", [16, RW])
    o_pstate = dout("o_pstate", [8, 64, 64])
    o_sstate = dout("o_sstate", [16, 8, 64, 64])
    o_yr = dout("o_yr", [NOUT * 128, 512]) if dbg else None
    o_yn = dout("o_yn", [NOUT * 128, 512]) if (dbg and do_nsa) else None
    o_br = dout("o_br", [NMAIN * 128, 3, 512]) if (dbg and do_nsa) else None
    o_vca = dout("o_vca", [128, 2, 129]) if (dbg and do_nsa) else None
    o_kca = dout("o_kca", [65, 256]) if (dbg and do_nsa) else None
    o_ksa = dout("o_ksa", [128, 4096]) if (dbg and do_nsa) else None
    o_vsa = dout("o_vsa", [128, 32, 2, 65]) if (dbg and do_nsa) else None

    with ExitStack() as st:
        P = Prog(nc, st)
        cur_stack = [st]
        sb = lambda name, shape, dt=F32: cur_stack[-1].enter_context(nc.sbuf_tensor("s_" + name, list(shape), dt))
        psum = lambda name, shape, dt=F32: st.enter_context(nc.psum_tensor(name, list(shape), dt))

        def MM(out, lhsT, rhs, r, w, start=True, stop=True):
            P.op("pe", lambda e: e.matmul(out, lhsT=lhsT, rhs=rhs, start=start, stop=stop), r, w)

        def TR(out, in_, idn, r, w):
            P.op("pe", lambda e: e.transpose(out=out, in_=in_, identity=idn), r, w)

        def TT(eng, out, in0, in1, op, r, w):
            P.op(eng, lambda e: e.tensor_tensor(out=out, in0=in0, in1=in1, op=op), r, w)

        def TS(eng, out, in0, s1, s2, op0, op1, r, w):
            if s2 is None:
                P.op(eng, lambda e: e.tensor_scalar(out=out, in0=in0, scalar1=s1, scalar2=None, op0=op0), r, w)
            else:
                P.op(eng, lambda e: e.tensor_scalar(out=out, in0=in0, scalar1=s1, scalar2=s2, op0=op0, op1=op1), r, w)

        def STT(eng, out, in0, scalar, in1, op0, op1, r, w):
            P.op(eng, lambda e: e.scalar_tensor_tensor(out=out, in0=in0, scalar=scalar, in1=in1, op0=op0, op1=op1), r, w)

        def ACT(out, in_, func, r, w, bias=None, scale=None, accum=None):
            kw = {}
            if bias is not None:
                kw["bias"] = bias
            if scale is not None:
                kw["scale"] = scale
            if accum is not None:
                kw["accum_out"] = accum
            P.op("act", lambda e: e.activation(out=out, in_=in_, func=func, **kw), r, w)

        def CP(eng, out, in_, r, w):
            if eng == "act":
                P.op("act", lambda e: e.copy(out=out, in_=in_), r, w)
            else:
                P.op(eng, lambda e: e.tensor_copy(out=out, in_=in_), r, w)

        def RED(eng, out, in_, op, r, w):
            P.op(eng, lambda e: e.tensor_reduce(out=out, in_=in_, axis=AX.X, op=op), r, w)

        def RECIP(out, in_, r, w):
            P.op("dve", lambda e: e.reciprocal(out=out, in_=in_), r, w)

        def MSET(eng, ap, val, w):
            P.op(eng, lambda e: e.memset(ap, val), (), w)

        def DMA(q, out, in_, res, r=(), w=()):
            P.dma(q, lambda e: e.dma_start(out=out, in_=in_), res, r, w)

        identf = sb("identf", [128, 128])
        identb = sb("identb", [128, 128], BF16)
        gmix = sb("gmix", [128, 8])
        rvalid = sb("rvalid", [128, NT])
        cstab = sb("cstab", [128, NT, 2, 8])

        DMA("sp", identf[:], ident[:, :], "identf", w=["identf"])
        DMA("pool", identb[:], ident[:, :], "identb", w=["identb"])
        DMA("sp", gmix[:], g_mix[:, :], "gmix", w=["gmix"])
        w_in_v = w_in.rearrange("(k p) n -> p k n", p=128)
        DMA("sp", rvalid[:], rowvalid[:, :], "rvalid", w=["rvalid"])
        DMA("sp", cstab[:], cs_tab[:, :, :, :], "cstab", w=["cstab"])

        ps_tr = psum("ps_tr", [128, 8, 128], BF16)
        NB = 6
        banks = [psum("bank%d" % i, [128, 512]) for i in range(NB)]
        banks_id = {id(b): i for i, b in enumerate(banks)}
        bank_i = [0]

        def bank():
            i = bank_i[0] % NB
            bank_i[0] += 1
            return banks[i], "bank%d" % i

        ps_f = psum("ps_f", [128, 512])

        xt = [sb("xt%d" % i, [128, D]) for i in range(2)]
        sq = sb("sq", [128, D], BF16)
        ss = sb("ss", [128, 1])
        rstd = sb("rstd", [128, 1])
        xn = sb("xn", [128, D], BF16)
        hT = sb("hT", [128, 8, 128], BF16)
        yrd = nc.dram_tensor("yr_scratch", [128, 4, NOUT * 128], BF16).ap()
        ynd = nc.dram_tensor("yn_scratch", [128, 4, NOUT * 128], BF16).ap()

        def load_norm(t, X=None, xn_=None, hdst=None, hname="hT"):
            if X is None:
                X = xt[t % 2]
                xn_ = "xt%d" % (t % 2)
            if hdst is None:
                hdst = hT[:]
            DMA("sp", X[:], xrows[t * 128:(t + 1) * 128, :], xn_, w=[xn_])
            ACT(sq[:], X[:], AF.Square, [xn_], ["sq", "ss"], accum=ss[:])
            TS("dve", rstd[:], ss[:], 1.0 / D, 1e-6, ALU.mult, ALU.add, ["ss"], ["rstd"])
            P.op("act", lambda e: e.sqrt(out=rstd[:], in_=rstd[:]), ["rstd"], ["rstd"])
            RECIP(rstd[:], rstd[:], ["rstd"], ["rstd"])
            TS("dve", xn[:], X[:], rstd[:, 0:1], None, ALU.mult, None, [xn_, "rstd"], ["xn"])
            for k in range(8):
                TR(ps_tr[:, k, :], xn[:, k * 128:(k + 1) * 128], identb[:], ["xn", "identb"], ["ps_tr"])
            TT("dve", hdst, ps_tr[:], gmix[:].unsqueeze(2).to_broadcast([128, 8, 128]), ALU.mult,
               ["ps_tr", "gmix"], [hname])

        st_kv = ExitStack()
        cur_stack.append(st_kv)
        NPT = NCTX + NMAIN
        wkv = sb("wkv", [128, 8, 768], BF16)
        kv = [sb("kv%d" % i, [128, 768]) for i in range(2)]
        kvb = sb("kvb", [128, 768], BF16)
        rt = sb("rt", [128, 6, 2, 2, 8])
        DMA("pool", wkv[:], w_in_v[:, :, RW + 512:RW + 512 + 768], "wkv", w=["wkv"])
        DMA("act", o_swin[:, 0:504, :], cwin[:, 8:512, :], "swin_copy")
        if do_nsa:
            KsA = [sb("KsA%d" % g, [128, NPT * 128], BF16) for g in range(2)]
            KwA = [sb("KwA%d" % g, [65, NPT * 128], BF16) for g in range(2)]
            VsA = sb("VsA", [128, NPT, 2, 65], BF16)
            VwA = sb("VwA", [128, NPT, 2, 65], BF16)
            kcT = sb("kcT", [128, NPT * 128], BF16)
            vcT = sb("vcT", [128, NPT * 128], BF16)
            KcA = [sb("KcA%d" % g, [65, 256], BF16) for g in range(2)]
            VcA = [sb("VcA%d" % g, [128, 2, 129], BF16) for g in range(2)]
            validt = sb("validt", [128, NPT])
            validc = sb("validc", [128, 2])
            ovl = sb("ovl", [128, 2, 64], BF16)
            rmax = sb("rmax", [128, 12])
            sqk = sb("sqk", [128, 768])
            r12 = sb("r12", [128, 12])
            DMA("sp", validt[:], valid_d[:, :], "validt", w=["validt"])
            DMA("sp", validc[:], validc_d[:, :], "validc", w=["validc"])
            DMA("pool", ovl[:], ovl_d[:, :, :], "ovl", w=["ovl"])
            for g in range(2):
                for c0 in range(0, NPT * 128, 1024):
                    DMA("pool", KsA[g][64:128, c0:c0 + 1024], E_d[:, c0:c0 + 1024], "KsA%d" % g, w=["KsA%d" % g])
                    DMA("pool", KwA[g][64:65, c0:c0 + 1024], ones_d[0:1, c0:c0 + 1024], "KwA%d" % g, w=["KwA%d" % g])
                DMA("pool", KcA[g][64:65, :], ones_d[0:1, 0:256], "KcA%d" % g, w=["KcA%d" % g])
                CP("pool", VsA[:, :, g, 64], validt[:], ["validt"], ["VsA"])
                CP("pool", VwA[:, :, g, 64], validt[:], ["validt"], ["VwA"])
                CP("pool", VcA[g][:, :, 64], validc[:], ["validc"], ["VcA%d" % g])
                CP("pool", VcA[g][:, :, 65:129], ovl[:], ["ovl"], ["VcA%d" % g])
            MSET("dve", rmax[:], 0.0, ["rmax"])
        tiles1 = list(range(NT)) if do_nsa else list(range(NCTX, NT))
        for t in tiles1:
            to = t - NCTX
            is_smp = t >= NPT
            load_norm(t)
            KV = kv[t % 2]
            kvn = "kv%d" % (t % 2)
            for half in range(2):
                n0 = half * 384
                pb, pbn = bank()
                for k in range(8):
                    MM(pb[:, 0:384], hT[:, k, :], wkv[:, k, n0:n0 + 384], ["hT", "wkv"], [pbn], start=(k == 0), stop=(k == 7))
                CP("act", KV[:, n0:n0 + 384], pb[:, 0:384], [pbn], [kvn])
            kview = KV[:, 256:768].rearrange("p (a b) -> p a b", a=2)[:, :, 0:128].rearrange("p a (g d) -> p a g d", g=2)
            x1 = kview[:, :, :, 0:8]
            x2 = kview[:, :, :, 8:16]
            cosb = cstab[:, t, 0, :].unsqueeze(1).unsqueeze(1).to_broadcast([128, 2, 2, 8])
            sinb = cstab[:, t, 1, :].unsqueeze(1).unsqueeze(1).to_broadcast([128, 2, 2, 8])
            TT("dve", rt[:, 0], x1, cosb, ALU.mult, [kvn, "cstab"], ["rt0"])
            TT("dve", rt[:, 1], x2, sinb, ALU.mult, [kvn, "cstab"], ["rt1"])
            TT("dve", rt[:, 2], x2, cosb, ALU.mult, [kvn, "cstab"], ["rt2"])
            TT("dve", rt[:, 3], x1, sinb, ALU.mult, [kvn, "cstab"], ["rt3"])
            TT("dve", x1, rt[:, 0], rt[:, 1], ALU.subtract, ["rt0", "rt1"], [kvn])
            TT("dve", x2, rt[:, 2], rt[:, 3], ALU.add, ["rt2", "rt3"], [kvn])
            if t >= NCTX:
                DMA("sp", o_kv[to * 128:(to + 1) * 128, :], KV[:], kvn, r=[kvn])
            if is_smp:
                for cb in range(4):
                    j = 4 * (t - NPT) + cb
                    DMA("act", o_swin[j, 504:512, :], KV[32 * cb + 24:32 * cb + 32, 512:768], kvn, r=[kvn])
            if do_nsa and not is_smp and nsa_stop >= 1:
                tc_ = slice(t * 128, (t + 1) * 128)
                CP("act", kvb[:], KV[:], [kvn], ["kvb"])
                CP("pool", VsA[:, t, :, 0:64], KV[:, 384:512].rearrange("p (g d) -> p g d", g=2), [kvn], ["VsA"])
                CP("pool", VwA[:, t, :, 0:64], KV[:, 640:768].rearrange("p (g d) -> p g d", g=2), [kvn], ["VwA"])
                for g in range(2 if nsa_stop >= 1.5 else 0):
                    TR(ps_tr[0:64, g, :], kvb[:, 256 + g * 64:256 + (g + 1) * 64], identb[:], ["kvb", "identb"], ["ps_tr"])
                    TR(ps_tr[0:64, 2 + g, :], kvb[:, 512 + g * 64:512 + (g + 1) * 64], identb[:], ["kvb", "identb"], ["ps_tr"])
                if nsa_stop >= 1.5:
                    TR(ps_tr[:, 4, :], kvb[:, 0:128], identb[:], ["kvb", "identb"], ["ps_tr"])
                    TR(ps_tr[:, 5, :], kvb[:, 128:256], identb[:], ["kvb", "identb"], ["ps_tr"])
                for g in range(2 if nsa_stop >= 1.5 else 0):
                    CP("dve", KsA[g][0:64, tc_], ps_tr[0:64, g, :], ["ps_tr"], ["KsA%d" % g])
                    CP("dve", KwA[g][0:64, tc_], ps_tr[0:64, 2 + g, :], ["ps_tr"], ["KwA%d" % g])
                if nsa_stop >= 1.5:
                    CP("dve", kcT[:, tc_], ps_tr[:, 4, :], ["ps_tr"], ["kcT"])
                    CP("dve", vcT[:, tc_], ps_tr[:, 5, :], ["ps_tr"], ["vcT"])
                if nsa_stop >= 2:
                    TT("pool", sqk[:], KV[:], KV[:], ALU.mult, [kvn], ["sqk"])
                    RED("dve", r12[:], sqk[:].rearrange("p (a d) -> p a d", a=12), ALU.add, ["sqk"], ["r12"])
                    TT("dve", rmax[:], rmax[:], r12[:], ALU.max, ["rmax", "r12"], ["rmax"])
        if do_nsa and nsa_stop >= 2:
            w1d = sb("w1d", [128, 2, 32, 128], BF16)
            peT = sb("peT", [64, 2, 32], BF16)
            w2s = sb("w2s", [128, 2, 64], BF16)
            bcs = sb("bcs", [128, 2])
            hc = sb("hc", [128, 255], BF16)
            kct = sb("kct", [128, 64])
            DMA("pool", w1d[0:64], w1_d[:, :, :, :], "w1d", w=["w1d"])
            DMA("pool", w1d[64:128], w1_d[:, :, :, :], "w1d", w=["w1d"])
            DMA("pool", peT[:], pe_d[:, :, :], "peT", w=["peT"])
            DMA("pool", w2s[:], w2_d[:, :, :], "w2s", w=["w2s"])
            for c in range(2):
                for l in range(32):
                    MM(ps_f[:, c:c + 1], w1d[0:64, c, l, :], peT[:, c, l:l + 1], ["w1d", "peT"], ["ps_f"], start=(l == 0), stop=(l == 31))
            CP("dve", bcs[:], ps_f[:, 0:2], ["ps_f"], ["bcs"])
            for c in range(2):
                src = kcT if c == 0 else vcT
                srcn = "kcT" if c == 0 else "vcT"
                srcv = src[:].rearrange("p (n l) -> p l n", l=16)
                for g in range(2):
                    gs = slice(64 * g, 64 * g + 64)
                    hb, hbn = bank()
                    for l in range(16):
                        MM(hb[:, 0:255], w1d[gs, c, l, :], srcv[gs, l, 0:255], ["w1d", srcn], [hbn], start=(l == 0), stop=False)
                        MM(hb[:, 0:255], w1d[gs, c, 16 + l, :], srcv[gs, l, 1:256], ["w1d", srcn], [hbn], start=False, stop=(l == 15))
                    ACT(hc[:], hb[:, 0:255], AF.Silu, [hbn, "bcs"], ["hc"], bias=bcs[:, c:c + 1])
                    if c == 0:
                        ob, obn = bank()
                        MM(ob[0:64, 0:255], w2s[:, 0, :], hc[:], ["w2s", "hc"], [obn])
                        CP("dve", KcA[g][0:64, 0:255], ob[0:64, 0:255], [obn], ["KcA%d" % g])
                        MSET("pool", KcA[g][0:64, 255:256], 0.0, ["KcA%d" % g])
                        for nt_, nn in ((0, 128), (1, 127)):
                            ob2, ob2n = bank()
                            MM(ob2[0:nn, 0:64], hc[:, nt_ * 128:nt_ * 128 + nn], w2s[:, 0, :], ["hc", "w2s"], [ob2n])
                            ACT(kct[0:nn, :], ob2[0:nn, 0:64], AF.Square, [ob2n], ["kct", "r12"], accum=r12[0:nn, 0:1])
                            TT("dve", rmax[0:nn, 0:1], rmax[0:nn, 0:1], r12[0:nn, 0:1], ALU.max, ["rmax", "r12"], ["rmax"])
                    else:
                        for nt_, nn in ((0, 128), (1, 127)):
                            ob2, ob2n = bank()
                            MM(ob2[0:nn, 0:64], hc[:, nt_ * 128:nt_ * 128 + nn], w2s[:, 1, :], ["hc", "w2s"], [ob2n])
                            CP("dve", VcA[g][0:nn, nt_, 0:64], ob2[0:nn, 0:64], [ob2n], ["VcA%d" % g])
        if do_nsa and dbg and nsa_stop >= 2:
            DMA("pool", o_vca[:, :, :], VcA[0][:], "VcA0", r=["VcA0"])
            DMA("pool", o_kca[:, :], KcA[0][:], "KcA0", r=["KcA0"])
            DMA("pool", o_ksa[:, :], KsA[0][:], "KsA0", r=["KsA0"])
            DMA("pool", o_vsa[:, :, :, :], VsA[:], "VsA", r=["VsA"])
        if do_nsa and nsa_stop >= 3:
            km = sb("km", [128, 1])
            k1 = sb("k1", [1, 1])
            onesf = sb("onesf", [1, 128])
            kmax8 = sb("kmax8", [128, 1])
            MSET("pool", onesf[:], 1.0, ["onesf"])
            RED("dve", km[:], rmax[:], ALU.max, ["rmax"], ["km"])
            TR(ps_f[0:1, 128:256], km[:, 0:1], identf[:], ["km", "identf"], ["ps_f"])
            RED("dve", k1[:], ps_f[0:1, 128:256], ALU.max, ["ps_f"], ["k1"])
            P.op("act", lambda e: e.sqrt(out=k1[:], in_=k1[:]), ["k1"], ["k1"])
            MM(ps_f[:, 300:301], onesf[0:1, :], k1[0:1, 0:1], ["onesf", "k1"], ["ps_f"])
            TS("dve", kmax8[:], ps_f[:, 300:301], 0.125, None, ALU.mult, None, ["ps_f"], ["kmax8"])
        if do_nsa and nsa_stop >= 4:
            BIG = 30000.0
            wq = sb("wq", [128, 8, 536], BF16)
            DMA("pool", wq[:, :, 0:512], w_in_v[:, :, RW:RW + 512], "wq", w=["wq"])
            DMA("pool", wq[:, :, 512:536], w_in_v[:, :, 3104:3128], "wq", w=["wq"])
            maskc = sb("maskc", [128, 2, NMAIN * 128], BF16)
            allowed = sb("allowed", [128, NMAIN, 64])
            fbias = sb("fbias", [128, NMAIN, 64])
            tri = sb("tri", [128, 2, 128], BF16)
            DMA("pool", maskc[:], maskc_d[:, :, :], "maskc", w=["maskc"])
            DMA("sp", allowed[:], allowed_d[:, :, :], "allowed", w=["allowed"])
            DMA("sp", fbias[:], fbias_d[:, :, :], "fbias", w=["fbias"])
            DMA("pool", tri[:], tri_d[:, :, :], "tri", w=["tri"])
            qf = sb("qf", [128, 512])
            qr = sb("qr", [128, 512])
            gts = sb("gts", [128, 24])
            qsq = sb("qsq", [128, 512])
            qss = sb("qss", [128, 8])
            negc = sb("negc", [128, 8])
            negcb = sb("negcb", [128, 8])
            QC = sb("QC", [128, 8, 65], BF16)
            QS = sb("QS", [128, 8, 128], BF16)
            QcT = sb("QcT", [65, 8, 128], BF16)
            QsT = sb("QsT", [128, 8, 128], BF16)
            PT = [sb("PT%d" % i, [128, 512], BF16) for i in range(3)]
            pt_i = [0]
            imp = sb("imp", [128, 64])
            sc = sb("sc", [128, 64])
            sc2 = sb("sc2", [128, 64])
            mx8 = sb("mx8", [128, 8])
            sel = sb("sel", [128, 64])
            rden = sb("rden", [128, 4])
            coef = sb("coef", [128, 4])
            ynf = sb("ynf", [128, 512])
            ynb = sb("ynb", [128, 512], BF16)
            ynTt = sb("ynTt", [128, 4, 128], BF16)
            qrt = sb("qrt", [128, 4, 8, 8])
            dbgo = sb("dbgo", [128, 3, 512]) if dbg_br else None
            zl = sb("zl", [128, 128], BF16)
            zr = sb("zr", [128, 512], BF16)
            MSET("pool", zl[:], 0.0, ["zl"])
            MSET("pool", zr[:], 0.0, ["zr"])

            def zero_psum(ps, psn, ncol):
                MSET("dve", ps[:, 0:ncol], 0.0, [psn])
            hbank = [banks.pop() for _ in range(4)]
            hbn = ["bank%d" % banks_id[id(b)] for b in hbank]
            sbanks = [(b, "bank%d" % banks_id[id(b)]) for b in banks] + [(ps_f, "ps_f")]
            bank_i[0] = 0

            def bank():
                i = bank_i[0] % len(sbanks)
                bank_i[0] += 1
                return sbanks[i]

            def nextPT():
                i = pt_i[0] % 3
                pt_i[0] += 1
                return PT[i], "PT%d" % i

            def accum_branch(g, bi, first, PS=slice(0, 128)):
                for hh in range(4):
                    TS("dve", rden[PS, hh:hh + 1], hbank[hh][PS, 64:65], 1e-30, None, ALU.max, None, [hbn[hh]], ["rden"])
                RECIP(rden[PS, :], rden[PS, :], ["rden"], ["rden"])
                TT("dve", coef[PS, :], rden[PS, :], gts[PS, :].rearrange("p (h b) -> p h b", b=3)[:, 4 * g:4 * g + 4, bi], ALU.mult,
                   ["rden", "gts"], ["coef"])
                for hh in range(4):
                    h = 4 * g + hh
                    if dbg_br:
                        TS("dve", dbgo[PS, bi, h * 64:(h + 1) * 64], hbank[hh][PS, 0:64], rden[PS, hh:hh + 1], None, ALU.mult, None, [hbn[hh], "rden"], ["dbgo"])
                    ysl = ynf[PS, h * 64:(h + 1) * 64]
                    if first:
                        TS("dve", ysl, hbank[hh][PS, 0:64], coef[PS, hh:hh + 1], None, ALU.mult, None, [hbn[hh], "coef"], ["ynf"])
                    else:
                        STT("dve", ysl, hbank[hh][PS, 0:64], coef[PS, hh:hh + 1], ysl, ALU.mult, ALU.add, [hbn[hh], "coef", "ynf"], ["ynf"])

            for m in range(nsa_m):
                t = NCTX + m
                load_norm(t)
                qb, qbn = bank()
                gb2, gb2n = bank()
                for k in range(8):
                    MM(qb[:, :], hT[:, k, :], wq[:, k, 0:512], ["hT", "wq"], [qbn], start=(k == 0), stop=(k == 7))
                for k in range(8):
                    MM(gb2[:, 0:24], hT[:, k, :], wq[:, k, 512:536], ["hT", "wq"], [gb2n], start=(k == 0), stop=(k == 7))
                CP("act", qf[:], qb[:, :], [qbn], ["qf"])
                ACT(gts[:], gb2[:, 0:24], AF.Sigmoid, [gb2n], ["gts"])
                TT("pool", qsq[:], qf[:], qf[:], ALU.mult, ["qf"], ["qsq"])
                RED("dve", qss[:], qsq[:].rearrange("p (h d) -> p h d", h=8), ALU.add, ["qsq"], ["qss"])
                P.op("act", lambda e: e.sqrt(out=qss[:], in_=qss[:]), ["qss"], ["qss"])
                TS("dve", negc[:], qss[:], kmax8[:, 0:1], -1.0, ALU.mult, ALU.mult, ["qss", "kmax8"], ["negc"])
                TS("dve", negcb[:], negc[:], -BIG, None, ALU.add, None, ["negc"], ["negcb"])
                CP("pool", qr[:], qf[:], ["qf"], ["qr"])
                q4 = qf[:].rearrange("p (h d) -> p h d", h=8)
                qr4 = qr[:].rearrange("p (h d) -> p h d", h=8)
                cosq = cstab[:, t, 0, :].unsqueeze(1).to_broadcast([128, 8, 8])
                sinq = cstab[:, t, 1, :].unsqueeze(1).to_broadcast([128, 8, 8])
                TT("dve", qrt[:, 0], q4[:, :, 0:8], cosq, ALU.mult, ["qf", "cstab"], ["qrt0"])
                TT("dve", qrt[:, 1], q4[:, :, 8:16], sinq, ALU.mult, ["qf", "cstab"], ["qrt1"])
                TT("dve", qrt[:, 2], q4[:, :, 8:16], cosq, ALU.mult, ["qf", "cstab"], ["qrt2"])
                TT("dve", qrt[:, 3], q4[:, :, 0:8], sinq, ALU.mult, ["qf", "cstab"], ["qrt3"])
                TT("dve", qr4[:, :, 0:8], qrt[:, 0], qrt[:, 1], ALU.subtract, ["qrt0", "qrt1"], ["qr"])
                TT("dve", qr4[:, :, 8:16], qrt[:, 2], qrt[:, 3], ALU.add, ["qrt2", "qrt3"], ["qr"])
                TS("dve", QC[:, :, 0:64], q4, 0.125, None, ALU.mult, None, ["qf"], ["QC"])
                CP("dve", QC[:, :, 64], negc[:], ["negc"], ["QC"])
                TS("pool", QS[:, :, 0:64], qr4, 0.125, None, ALU.mult, None, ["qr"], ["QS"])
                for h in range(8):
                    TR(ps_tr[0:65, h, :], QC[:, h, :], identb[:], ["QC", "identb"], ["ps_tr"])
                CP("act", QcT[:], ps_tr[0:65, :, :], ["ps_tr"], ["QcT"])
                for g in range(2):
                    for nt_, nn in ((0, 128), (1, 127)):
                        sb_, sbn = bank()
                        MM(sb_[0:nn, :], KcA[g][:, nt_ * 128:nt_ * 128 + nn], QcT[:, 4 * g:4 * g + 4, :], ["KcA%d" % g, "QcT"], [sbn])
                        Pm, pmn = nextPT()
                        ACT(Pm[0:nn, :], sb_[0:nn, :], AF.Exp, [sbn], [pmn])
                        TT("dve", Pm[0:nn, :].rearrange("p (h t) -> p h t", h=4), Pm[0:nn, :].rearrange("p (h t) -> p h t", h=4),
                           maskc[0:nn, nt_, m * 128:(m + 1) * 128].unsqueeze(1).to_broadcast([nn, 4, 128]), ALU.mult, [pmn, "maskc"], [pmn])
                        for hh in range(4):
                            MM(hbank[hh][:, 0:129], Pm[0:nn, hh * 128:(hh + 1) * 128], VcA[g][0:nn, nt_, :], [pmn, "VcA%d" % g], [hbn[hh]],
                               start=(nt_ == 0), stop=(nt_ == 1))
                    accum_branch(g, 0, True)
                    for hh in range(4):
                        if hh == 0:
                            TS("dve", imp[:], hbank[0][:, 65:129], rden[:, 0:1], None, ALU.mult, None, [hbn[0], "rden"], ["imp"])
                        else:
                            STT("dve", imp[:], hbank[hh][:, 65:129], rden[:, hh:hh + 1], imp[:], ALU.mult, ALU.add, [hbn[hh], "rden", "imp"], ["imp"])
                    TT("dve", sc[:], imp[:], allowed[:, m, :], ALU.mult, ["imp", "allowed"], ["sc"])
                    TT("dve", sc[:], sc[:], fbias[:, m, :], ALU.add, ["sc", "fbias"], ["sc"])
                    P.op("dve", lambda e: e.max(out=mx8[:], in_=sc[:]), ["sc"], ["mx8"])
                    P.op("dve", lambda e: e.match_replace(out=sc2[:], in_to_replace=mx8[:], in_values=sc[:], imm_value=-2e9), ["sc", "mx8"], ["sc2"])
                    P.op("dve", lambda e: e.max(out=mx8[:], in_=sc2[:]), ["sc2"], ["mx8"])
                    TS("dve", sel[:], sc[:], mx8[:, 7:8], None, ALU.is_ge, None, ["sc", "mx8"], ["sel"])
                    TT("dve", sel[:], sel[:], allowed[:, m, :], ALU.mult, ["sel", "allowed"], ["sel"])
                    MSET("dve", sel[:, 0:1], 1.0, ["sel"])
                    for hh in range(4):
                        h = 4 * g + hh
                        TS("dve", QS[:, h, 64:128], sel[:], BIG, negcb[:, h:h + 1], ALU.mult, ALU.add, ["sel", "negcb"], ["QS"])
                    for hh in range(4):
                        h = 4 * g + hh
                        TR(ps_tr[:, h, :], QS[:, h, :], identb[:], ["QS", "identb"], ["ps_tr"])
                    CP("act", QsT[:, 4 * g:4 * g + 4, :], ps_tr[:, 4 * g:4 * g + 4, :], ["ps_tr"], ["QsT"])
                    def pv_sel(kt, Pm, pmn, g=g, t=t):
                        for hh in range(4):
                            MM(hbank[hh][:, 0:65], Pm[:, hh * 128:(hh + 1) * 128], VsA[:, kt, g, :], [pmn, "VsA"], [hbn[hh]],
                               start=(kt == 0), stop=(kt == t))
                    pend = None
                    for kt in range(t + 1):
                        sb_, sbn = bank()
                        MM(sb_[:, :], KsA[g][:, kt * 128:(kt + 1) * 128], QsT[:, 4 * g:4 * g + 4, :], ["KsA%d" % g, "QsT"], [sbn])
                        Pm, pmn = nextPT()
                        ACT(Pm[:], sb_[:, :], AF.Exp, [sbn], [pmn])
                        if kt == t:
                            TT("pool", Pm[:].rearrange("p (h t) -> p h t", h=4), Pm[:].rearrange("p (h t) -> p h t", h=4),
                               tri[:, 0, :].unsqueeze(1).to_broadcast([128, 4, 128]), ALU.mult, [pmn, "tri"], [pmn])
                        if pend is not None:
                            pv_sel(*pend)
                        pend = (kt, Pm, pmn)
                    pv_sel(*pend)
                    accum_branch(g, 1, False)
                    kts = [kt for kt in range(t - 4, t + 1) if kt >= 0]
                    def pv_win(kt, Pm, pmn, g=g, kts=kts):
                        for hh in range(4):
                            MM(hbank[hh][:, 0:65], Pm[:, hh * 128:(hh + 1) * 128], VwA[:, kt, g, :], [pmn, "VwA"], [hbn[hh]],
                               start=(kt == kts[0]), stop=(kt == kts[-1]))
                    pend = None
                    for kt in kts:
                        sb_, sbn = bank()
                        MM(sb_[:, :], KwA[g][:, kt * 128:(kt + 1) * 128], QsT[0:65, 4 * g:4 * g + 4, :], ["KwA%d" % g, "QsT"], [sbn])
                        Pm, pmn = nextPT()
                        ACT(Pm[:], sb_[:, :], AF.Exp, [sbn], [pmn])
                        if kt == t or kt == t - 4:
                            TT("pool", Pm[:].rearrange("p (h t) -> p h t", h=4), Pm[:].rearrange("p (h t) -> p h t", h=4),
                               tri[:, 0 if kt == t else 1, :].unsqueeze(1).to_broadcast([128, 4, 128]), ALU.mult, [pmn, "tri"], [pmn])
                        if pend is not None:
                            pv_win(*pend)
                        pend = (kt, Pm, pmn)
                    pv_win(*pend)
                    accum_branch(g, 2, False)
                if dbg:
                    DMA("sp", o_yn[m * 128:(m + 1) * 128, :], ynf[:], "ynf", r=["ynf"])
                    if dbg_br:
                        DMA("sp", o_br[m * 128:(m + 1) * 128, :, :], dbgo[:], "dbgo", r=["dbgo"])
                CP("act", ynb[:], ynf[:], ["ynf"], ["ynb"])
                for q in range(4):
                    TR(ps_tr[:, q, :], ynb[:, q * 128:(q + 1) * 128], identb[:], ["ynb", "identb"], ["ps_tr"])
                CP("dve", ynTt[:], ps_tr[:, 0:4, :], ["ps_tr"], ["ynTt"])
                DMA("sp", ynd[:, :, m * 128:(m + 1) * 128], ynTt[:], "ynTt", r=["ynTt"])
            NSEQ = nsa_seq
            R32 = slice(0, 32)
            rawc = sb("rawc", [128, 16, 256], BF16)
            raws = sb("raws", [128, 16, 256], BF16)
            raww = sb("raww", [128, 4, 256], BF16)
            KsS, KwS, VsS, VwS, kcS, vcS, KcS = KsA, KwA, VsA, VwA, kcT, vcT, KcA
            VcS = [VcA[g][:, 0, :] for g in range(2)]
            pti = sb("pti", [1, 256], I32)
            ptf = sb("ptf", [1, 256])
            idxf = sb("idxf", [128, 256])
            idxi = sb("idxi", [128, 256], I32)
            iop = sb("iop", [128, 1])
            ovla = sb("ovla", [128, 64], BF16)
            alls = sb("alls", [128, 64])
            fbs = sb("fbs", [128, 64])
            mw0 = sb("mw0", [128, 32], BF16)
            xs32 = xt[0][0:32, :]
            hTs = sb("hTs", [128, 8, 32], BF16)
            kvs = kv[0][0:32, :]
            kvsb = kvb[0:32, :]
            sqr = maskc[:].rearrange("p a (k c) -> p (a k) c", c=256)
            rms_ = sb("rms_", [128, 64])
            rmx = sb("rmx", [128, 1])
            kmx8 = sb("kmx8", [128, 1])
            ynTs = sb("ynTs", [128, 4, 32], BF16)
            DMA("sp", pti[:], page_tab[:, :], "pti", w=["pti"])
            DMA("sp", iop[:], iota_p[:, :], "iop", w=["iop"])
            DMA("pool", ovla[:], ovl_abs[:, :], "ovla", w=["ovla"])
            DMA("sp", alls[:], allowed_s[:, :], "alls", w=["alls"])
            DMA("sp", fbs[:], fbias_s[:, :], "fbs", w=["fbs"])
            DMA("pool", mw0[:], maskw0[:, :], "mw0", w=["mw0"])
            CP("dve", ptf[:], pti[:], ["pti"], ["ptf"])
            sbk, sbkn = bank()
            MM(sbk[:, 0:256], onesf[0:1, :], ptf[0:1, :], ["onesf", "ptf"], [sbkn])
            TS("dve", idxf[:], sbk[:, 0:256], 128.0, iop[:, 0:1], ALU.mult, ALU.add, [sbkn, "iop"], ["idxf"])
            CP("dve", idxi[:], idxf[:], ["idxf"], ["idxi"])
            for g in range(2):
                MSET("pool", VcS[g][:, 64:65], 1.0, ["VcA%d" % g])
                CP("pool", VcS[g][:, 65:129], ovla[:], ["ovla"], ["VcA%d" % g])
            MSET("pool", VsS[:, 0:16, :, 64:65], 1.0, ["VsA"])
            MSET("pool", VwS[:, 0:4, :, 64:65], 1.0, ["VwA"])
            cache_c2 = cache_c.rearrange("n t c -> (n t) c")
            cache_s2 = cache_s.rearrange("n t c -> (n t) c")
            for j in range(NSEQ):
                ts_ = NPT + j // 4
                r0 = 32 * (j % 4)
                row0 = ts_ * 128 + r0
                for pg in range(16):
                    col = j * 16 + pg
                    P.dma("pool", (lambda e, pg=pg, col=col: e.indirect_dma_start(
                        out=rawc[:, pg, :], out_offset=None, in_=cache_c2[:, :],
                        in_offset=bass.IndirectOffsetOnAxis(ap=idxi[:, col:col + 1], axis=0))), "rawc", ["idxi"], ["rawc"])
                    P.dma("pool", (lambda e, pg=pg, col=col: e.indirect_dma_start(
                        out=raws[:, pg, :], out_offset=None, in_=cache_s2[:, :],
                        in_offset=bass.IndirectOffsetOnAxis(ap=idxi[:, col:col + 1], axis=0))), "raws", ["idxi"], ["raws"])
                DMA("pool", raww[:], cwin[j].rearrange("(k p) c -> p k c", p=128), "raww", w=["raww"])
                DMA("sp", xs32, xrows[row0:row0 + 32, :], "xt0", w=["xt0"])
                ACT(sq[R32, :], xs32, AF.Square, ["xt0"], ["sq", "ss"], accum=ss[R32, :])
                TS("dve", rstd[R32, :], ss[R32, :], 1.0 / D, 1e-6, ALU.mult, ALU.add, ["ss"], ["rstd"])
                P.op("act", lambda e: e.sqrt(out=rstd[R32, :], in_=rstd[R32, :]), ["rstd"], ["rstd"])
                RECIP(rstd[R32, :], rstd[R32, :], ["rstd"], ["rstd"])
                TS("dve", xn[R32, :], xs32, rstd[R32, 0:1], None, ALU.mult, None, ["xt0", "rstd"], ["xn"])
                for k in range(8):
                    TR(ps_tr[:, k, 0:32], xn[R32, k * 128:(k + 1) * 128], identb[R32, R32], ["xn", "identb"], ["ps_tr"])
                TT("dve", hTs[:], ps_tr[:, :, 0:32], gmix[:].unsqueeze(2).to_broadcast([128, 8, 32]), ALU.mult, ["ps_tr", "gmix"], ["hTs"])
                for half in range(2):
                    n0 = half * 384
                    pb, pbn = bank()
                    for k in range(8):
                        MM(pb[R32, 0:384], hTs[:, k, :], wkv[:, k, n0:n0 + 384], ["hTs", "wkv"], [pbn], start=(k == 0), stop=(k == 7))
                    CP("act", kvs[:, n0:n0 + 384], pb[R32, 0:384], [pbn], ["kv0"])
                kview = kvs[:, 256:768].rearrange("p (a b) -> p a b", a=2)[:, :, 0:128].rearrange("p a (g d) -> p a g d", g=2)
                x1 = kview[:, :, :, 0:8]
                x2 = kview[:, :, :, 8:16]
                cosb = cstab[R32, NPT, 0, :].unsqueeze(1).unsqueeze(1).to_broadcast([32, 2, 2, 8])
                sinb = cstab[R32, NPT, 1, :].unsqueeze(1).unsqueeze(1).to_broadcast([32, 2, 2, 8])
                TT("dve", rt[R32, 0], x1, cosb, ALU.mult, ["kv0", "cstab"], ["rt0"])
                TT("dve", rt[R32, 1], x2, sinb, ALU.mult, ["kv0", "cstab"], ["rt1"])
                TT("dve", rt[R32, 2], x2, cosb, ALU.mult, ["kv0", "cstab"], ["rt2"])
                TT("dve", rt[R32, 3], x1, sinb, ALU.mult, ["kv0", "cstab"], ["rt3"])
                TT("dve", x1, rt[R32, 0], rt[R32, 1], ALU.subtract, ["rt0", "rt1"], ["kv0"])
                TT("dve", x2, rt[R32, 2], rt[R32, 3], ALU.add, ["rt2", "rt3"], ["kv0"])
                CP("act", kvsb, kvs, ["kv0"], ["kvb"])
                qb, qbn = bank()
                gb2, gb2n = bank()
                for k in range(8):
                    MM(qb[R32, :], hTs[:, k, :], wq[:, k, 0:512], ["hTs", "wq"], [qbn], start=(k == 0), stop=(k == 7))
                for k in range(8):
                    MM(gb2[R32, 0:24], hTs[:, k, :], wq[:, k, 512:536], ["hTs", "wq"], [gb2n], start=(k == 0), stop=(k == 7))
                CP("act", qf[R32, :], qb[R32, :], [qbn], ["qf"])
                ACT(gts[R32, :], gb2[R32, 0:24], AF.Sigmoid, [gb2n], ["gts"])
                CP("pool", VsS[:, 0:16, :, 0:64], raws[:, :, 128:256].rearrange("p k (g d) -> p k g d", g=2), ["raws"], ["VsA"])
                CP("pool", VwS[:, 0:4, :, 0:64], raww[:, :, 128:256].rearrange("p k (g d) -> p k g d", g=2), ["raww"], ["VwA"])
                CP("pool", VsS[R32, 16, :, 0:64], kvs[:, 384:512].rearrange("p (g d) -> p g d", g=2), ["kv0"], ["VsA"])
                CP("pool", VwS[R32, 4, :, 0:64], kvs[:, 640:768].rearrange("p (g d) -> p g d", g=2), ["kv0"], ["VwA"])
                for g in range(2):
                    CP("pool", VsS[R32, 16, g, 64:65], rvalid[R32, NPT:NPT + 1], ["rvalid"], ["VsA"])
                    CP("pool", VwS[R32, 4, g, 64:65], rvalid[R32, NPT:NPT + 1], ["rvalid"], ["VwA"])
                for pg0 in range(0, 16, 8):
                    for g in range(2):
                        for pp in range(8):
                            TR(ps_tr[0:64, pp, :], raws[:, pg0 + pp, g * 64:(g + 1) * 64], identb[:], ["raws", "identb"], ["ps_tr"])
                        CP("dve", KsS[g][0:64, pg0 * 128:(pg0 + 8) * 128].rearrange("p (k t) -> p k t", k=8), ps_tr[0:64, :, :], ["ps_tr"], ["KsA%d" % g])
                    for c in range(2):
                        dst = kcS if c == 0 else vcS
                        for pp in range(8):
                            TR(ps_tr[:, pp, :], rawc[:, pg0 + pp, c * 128:(c + 1) * 128], identb[:], ["rawc", "identb"], ["ps_tr"])
                        CP("dve", dst[:, pg0 * 128:(pg0 + 8) * 128].rearrange("p (k t) -> p k t", k=8), ps_tr[:, :, :], ["ps_tr"], ["kcT" if c == 0 else "vcT"])
                for g in range(2):
                    for pp in range(4):
                        TR(ps_tr[0:64, g * 4 + pp, :], raww[:, pp, g * 64:(g + 1) * 64], identb[:], ["raww", "identb"], ["ps_tr"])
                for g in range(2):
                    CP("dve", KwS[g][0:64, 0:512].rearrange("p (k t) -> p k t", k=4), ps_tr[0:64, 4 * g:4 * g + 4, :], ["ps_tr"], ["KwA%d" % g])
                for g in range(2):
                    TR(ps_tr[0:64, g, 0:32], kvsb[:, 256 + g * 64:256 + (g + 1) * 64], identb[R32, R32], ["kvb", "identb"], ["ps_tr"])
                    TR(ps_tr[0:64, 2 + g, 0:32], kvsb[:, 512 + g * 64:512 + (g + 1) * 64], identb[R32, R32], ["kvb", "identb"], ["ps_tr"])
                for g in range(2):
                    CP("dve", KsS[g][0:64, 2048:2080], ps_tr[0:64, g, 0:32], ["ps_tr"], ["KsA%d" % g])
                    CP("dve", KwS[g][0:64, 512:544], ps_tr[0:64, 2 + g, 0:32], ["ps_tr"], ["KwA%d" % g])
                MSET("dve", rmx[:], 0.0, ["rmx"])
                for (src, srcn, nk) in ((rawc, "rawc", 16), (raws, "raws", 16), (raww, "raww", 4)):
                    TT("pool", sqr[:, 0:nk, :], src[:, 0:nk, :], src[:, 0:nk, :], ALU.mult, [srcn], ["maskc"])
                    RED("dve", rms_[:, 0:nk * 4], sqr[:, 0:nk, :].rearrange("p k (a d) -> p (k a) d", a=4), ALU.add, ["maskc"], ["rms_"])
                    RED("dve", r12[:, 0:1], rms_[:, 0:nk * 4], ALU.max, ["rms_"], ["r12"])
                    TT("dve", rmx[:], rmx[:], r12[:, 0:1], ALU.max, ["rmx", "r12"], ["rmx"])
                TT("pool", sqk[R32, :], kvs, kvs, ALU.mult, ["kv0"], ["sqk"])
                RED("dve", r12[R32, :], sqk[R32, :].rearrange("p (a d) -> p a d", a=12), ALU.add, ["sqk"], ["r12"])
                RED("dve", r12[R32, 0:1], r12[R32, :], ALU.max, ["r12"], ["r12"])
                TT("dve", rmx[R32, :], rmx[R32, :], r12[R32, 0:1], ALU.max, ["rmx", "r12"], ["rmx"])
                for c in range(2):
                    src = kcS if c == 0 else vcS
                    srcn = "kcT" if c == 0 else "vcT"
                    srcv = src[:].rearrange("p (n l) -> p l n", l=16)
                    for g in range(2):
                        gs = slice(64 * g, 64 * g + 64)
                        hb, hbn_ = bank()
                        for l in range(16):
                            MM(hb[:, 0:127], w1d[gs, c, l, :], srcv[gs, l, 0:127], ["w1d", srcn], [hbn_], start=(l == 0), stop=False)
                            MM(hb[:, 0:127], w1d[gs, c, 16 + l, :], srcv[gs, l, 1:128], ["w1d", srcn], [hbn_], start=False, stop=(l == 15))
                        ACT(hc[:, 0:127], hb[:, 0:127], AF.Silu, [hbn_, "bcs"], ["hc"], bias=bcs[:, c:c + 1])
                        ob, obn = bank()
                        if c == 0:
                            MM(ob[0:64, 0:127], w2s[:, 0, :], hc[:, 0:127], ["w2s", "hc"], [obn])
                            CP("dve", KcS[g][0:64, 0:127], ob[0:64, 0:127], [obn], ["KcA%d" % g])
                            ob2, ob2n = bank()
                            MM(ob2[0:127, 0:64], hc[:, 0:127], w2s[:, 0, :], ["hc", "w2s"], [ob2n])
                            ACT(kct[0:127, :], ob2[0:127, 0:64], AF.Square, [ob2n], ["kct", "r12"], accum=r12[0:127, 0:1])
                            TT("dve", rmx[0:127, :], rmx[0:127, :], r12[0:127, 0:1], ALU.max, ["rmx", "r12"], ["rmx"])
                        else:
                            MM(ob[0:127, 0:64], hc[:, 0:127], w2s[:, 1, :], ["hc", "w2s"], [obn])
                            CP("dve", VcS[g][0:127, 0:64], ob[0:127, 0:64], [obn], ["VcA%d" % g])
                sbk, sbkn = bank()
                TR(sbk[0:1, 0:128], rmx[:, 0:1], identf[:], ["rmx", "identf"], [sbkn])
                RED("dve", k1[:], sbk[0:1, 0:128], ALU.max, [sbkn], ["k1"])
                P.op("act", lambda e: e.sqrt(out=k1[:], in_=k1[:]), ["k1"], ["k1"])
                sbk2, sbk2n = bank()
                MM(sbk2[:, 0:1], onesf[0:1, :], k1[0:1, 0:1], ["onesf", "k1"], [sbk2n])
                TS("dve", kmx8[:], sbk2[:, 0:1], 0.125, None, ALU.mult, None, [sbk2n], ["kmx8"])
                TT("pool", qsq[R32, :], qf[R32, :], qf[R32, :], ALU.mult, ["qf"], ["qsq"])
                RED("dve", qss[R32, :], qsq[R32, :].rearrange("p (h d) -> p h d", h=8), ALU.add, ["qsq"], ["qss"])
                P.op("act", lambda e: e.sqrt(out=qss[R32, :], in_=qss[R32, :]), ["qss"], ["qss"])
                TS("dve", negc[R32, :], qss[R32, :], kmx8[R32, 0:1], -1.0, ALU.mult, ALU.mult, ["qss", "kmx8"], ["negc"])
                TS("dve", negcb[R32, :], negc[R32, :], -BIG, None, ALU.add, None, ["negc"], ["negcb"])
                CP("pool", qr[R32, :], qf[R32, :], ["qf"], ["qr"])
                q4 = qf[R32, :].rearrange("p (h d) -> p h d", h=8)
                qr4 = qr[R32, :].rearrange("p (h d) -> p h d", h=8)
                cosq = cstab[R32, NPT, 0, :].unsqueeze(1).to_broadcast([32, 8, 8])
                sinq = cstab[R32, NPT, 1, :].unsqueeze(1).to_broadcast([32, 8, 8])
                TT("dve", qrt[R32, 0], q4[:, :, 0:8], cosq, ALU.mult, ["qf", "cstab"], ["qrt0"])
                TT("dve", qrt[R32, 1], q4[:, :, 8:16], sinq, ALU.mult, ["qf", "cstab"], ["qrt1"])
                TT("dve", qrt[R32, 2], q4[:, :, 8:16], cosq, ALU.mult, ["qf", "cstab"], ["qrt2"])
                TT("dve", qrt[R32, 3], q4[:, :, 0:8], sinq, ALU.mult, ["qf", "cstab"], ["qrt3"])
                TT("dve", qr4[:, :, 0:8], qrt[R32, 0], qrt[R32, 1], ALU.subtract, ["qrt0", "qrt1"], ["qr"])
                TT("dve", qr4[:, :, 8:16], qrt[R32, 2], qrt[R32, 3], ALU.add, ["qrt2", "qrt3"], ["qr"])
                TS("dve", QC[R32, :, 0:64], q4, 0.125, None, ALU.mult, None, ["qf"], ["QC"])
                CP("dve", QC[R32, :, 64], negc[R32, :], ["negc"], ["QC"])
                TS("pool", QS[R32, :, 0:64], qr4, 0.125, None, ALU.mult, None, ["qr"], ["QS"])
                for h in range(8):
                    TR(ps_tr[0:65, h, 0:32], QC[R32, h, :], identb[R32, R32], ["QC", "identb"], ["ps_tr"])
                CP("act", QcT[:, :, 0:32], ps_tr[0:65, :, 0:32], ["ps_tr"], ["QcT"])
                for g in range(2):
                    sb_, sbn = bank()
                    MM(sb_[0:127, 0:128], KcS[g][:, 0:127], QcT[:, 4 * g:4 * g + 4, 0:32], ["KcA%d" % g, "QcT"], [sbn])
                    Pm, pmn = nextPT()
                    ACT(Pm[0:127, 0:128], sb_[0:127, 0:128], AF.Exp, [sbn], [pmn])
                    for hh in range(4):
                        MM(hbank[hh][R32, 0:129], Pm[0:127, hh * 32:(hh + 1) * 32], VcS[g][0:127, :], [pmn, "VcA%d" % g], [hbn[hh]])
                    accum_branch(g, 0, True, R32)
                    for hh in range(4):
                        if hh == 0:
                            TS("dve", imp[R32, :], hbank[0][R32, 65:129], rden[R32, 0:1], None, ALU.mult, None, [hbn[0], "rden"], ["imp"])
                        else:
                            STT("dve", imp[R32, :], hbank[hh][R32, 65:129], rden[R32, hh:hh + 1], imp[R32, :], ALU.mult, ALU.add, [hbn[hh], "rden", "imp"], ["imp"])
                    TT("dve", sc[R32, :], imp[R32, :], alls[R32, :], ALU.mult, ["imp", "alls"], ["sc"])
                    TT("dve", sc[R32, :], sc[R32, :], fbs[R32, :], ALU.add, ["sc", "fbs"], ["sc"])
                    P.op("dve", lambda e: e.max(out=mx8[R32, :], in_=sc[R32, :]), ["sc"], ["mx8"])
                    P.op("dve", lambda e: e.match_replace(out=sc2[R32, :], in_to_replace=mx8[R32, :], in_values=sc[R32, :], imm_value=-2e9), ["sc", "mx8"], ["sc2"])
                    P.op("dve", lambda e: e.max(out=mx8[R32, :], in_=sc2[R32, :]), ["sc2"], ["mx8"])
                    TS("dve", sel[R32, :], sc[R32, :], mx8[R32, 7:8], None, ALU.is_ge, None, ["sc", "mx8"], ["sel"])
                    TT("dve", sel[R32, :], sel[R32, :], alls[R32, :], ALU.mult, ["sel", "alls"], ["sel"])
                    MSET("dve", sel[R32, 0:1], 1.0, ["sel"])
                    for hh in range(4):
                        h = 4 * g + hh
                        TS("dve", QS[R32, h, 64:128], sel[R32, :], BIG, negcb[R32, h:h + 1], ALU.mult, ALU.add, ["sel", "negcb"], ["QS"])
                    for hh in range(4):
                        h = 4 * g + hh
                        TR(ps_tr[:, h, 0:32], QS[R32, h, :], identb[R32, R32], ["QS", "identb"], ["ps_tr"])
                    CP("act", QsT[:, 4 * g:4 * g + 4, 0:32], ps_tr[:, 4 * g:4 * g + 4, 0:32], ["ps_tr"], ["QsT"])
                    def pv_ssel(kt, nk, Pm, pmn, g=g):
                        for hh in range(4):
                            MM(hbank[hh][R32, 0:65], Pm[0:nk, hh * 32:(hh + 1) * 32], VsS[0:nk, kt, g, :], [pmn, "VsA"], [hbn[hh]],
                               start=(kt == 0), stop=(kt == 16))
                    pend = None
                    for kt in range(17):
                        nk = 128 if kt < 16 else 32
                        sb_, sbn = bank()
                        MM(sb_[0:nk, 0:128], KsS[g][:, kt * 128:kt * 128 + nk], QsT[:, 4 * g:4 * g + 4, 0:32], ["KsA%d" % g, "QsT"], [sbn])
                        Pm, pmn = nextPT()
                        ACT(Pm[0:nk, 0:128], sb_[0:nk, 0:128], AF.Exp, [sbn], [pmn])
                        if kt == 16:
                            TT("pool", Pm[R32, 0:128].rearrange("p (h t) -> p h t", h=4), Pm[R32, 0:128].rearrange("p (h t) -> p h t", h=4),
                               tri[R32, 0, 0:32].unsqueeze(1).to_broadcast([32, 4, 32]), ALU.mult, [pmn, "tri"], [pmn])
                        if pend is not None:
                            pv_ssel(*pend)
                        pend = (kt, nk, Pm, pmn)
                    pv_ssel(*pend)
                    accum_branch(g, 1, False, R32)
                    def pv_swin(kt, nk, Pm, pmn, g=g):
                        for hh in range(4):
                            MM(hbank[hh][R32, 0:65], Pm[0:nk, hh * 32:(hh + 1) * 32], VwS[0:nk, kt, g, :], [pmn, "VwA"], [hbn[hh]],
                               start=(kt == 0), stop=(kt == 4))
                    pend = None
                    for kt in range(5):
                        nk = 128 if kt < 4 else 32
                        sb_, sbn = bank()
                        MM(sb_[0:nk, 0:128], KwS[g][:, kt * 128:kt * 128 + nk], QsT[0:65, 4 * g:4 * g + 4, 0:32], ["KwA%d" % g, "QsT"], [sbn])
                        Pm, pmn = nextPT()
                        ACT(Pm[0:nk, 0:128], sb_[0:nk, 0:128], AF.Exp, [sbn], [pmn])
                        if kt == 0:
                            TT("pool", Pm[:, 0:128].rearrange("p (h t) -> p h t", h=4), Pm[:, 0:128].rearrange("p (h t) -> p h t", h=4),
                               mw0[:, :].unsqueeze(1).to_broadcast([128, 4, 32]), ALU.mult, [pmn, "mw0"], [pmn])
                        if kt == 4:
                            TT("pool", Pm[R32, 0:128].rearrange("p (h t) -> p h t", h=4), Pm[R32, 0:128].rearrange("p (h t) -> p h t", h=4),
                               tri[R32, 0, 0:32].unsqueeze(1).to_broadcast([32, 4, 32]), ALU.mult, [pmn, "tri"], [pmn])
                        if pend is not None:
                            pv_swin(*pend)
                        pend = (kt, nk, Pm, pmn)
                    pv_swin(*pend)
                    accum_branch(g, 2, False, R32)
                to = NMAIN + j // 4
                if dbg:
                    DMA("sp", o_yn[to * 128 + r0:to * 128 + r0 + 32, :], ynf[R32, :], "ynf", r=["ynf"])
                CP("act", ynb[R32, :], ynf[R32, :], ["ynf"], ["ynb"])
                for q in range(4):
                    TR(ps_tr[:, q, 0:32], ynb[R32, q * 128:(q + 1) * 128], identb[R32, R32], ["ynb", "identb"], ["ps_tr"])
                CP("dve", ynTs[:], ps_tr[:, 0:4, 0:32], ["ps_tr"], ["ynTs"])
                DMA("sp", ynd[:, :, to * 128 + r0:to * 128 + r0 + 32], ynTs[:], "ynTs", r=["ynTs"])
            if NSEQ < 16:
                MSET("pool", ynTt[:], 0.0, ["ynTt"])
                for to in range(NMAIN, NOUT):
                    DMA("sp", ynd[:, :, to * 128:(to + 1) * 128], ynTt[:], "ynTt", r=["ynTt"])
            banks.extend(hbank)

        P.barrier()
        cur_stack.pop()
        st_kv.close()
        bank_i[0] = 0

        def bank():
            i = bank_i[0] % len(banks)
            bank_i[0] += 1
            return banks[i], "bank%d" % banks_id[id(banks[i])]

        if do_rwkv:
            st_rw = ExitStack()
            cur_stack.append(st_rw)
            mub = sb("mub", [128, RW])
            parb = sb("parb", [128, 7, 512])
            wrw = sb("wrw", [128, 8, RW], BF16)
            wdec = sb("wdec", [64, 512], BF16)
            waaa = sb("waaa", [64, 512], BF16)
            wgat = sb("wgat", [128, 2, 512], BF16)
            mskb = sb("mskb", [128, 3, 128], BF16)
            mskf = sb("mskf", [128, 2, 128])
            cind = sb("cind", [128, 4])
            DMA("sp", mub[:], mu_b[:, :], "mub", w=["mub"])
            DMA("act", parb[:], par_b[:, :, :], "parb", w=["parb"])
            for k in range(8):
                DMA("pool", wrw[:, k, :], w_in_v[:, k, 0:RW], "wrw", w=["wrw"])
            DMA("pool", wdec[:], w_decay[:, :], "wdec", w=["wdec"])
            DMA("pool", waaa[:], w_aaa[:, :], "waaa", w=["waaa"])
            DMA("pool", wgat[:, 0, :], w_gate[0:128, :], "wgat", w=["wgat"])
            DMA("pool", wgat[0:32, 1, :], w_gate[128:160, :], "wgat", w=["wgat"])
            DMA("pool", mskb[:], masks_bf[:, :, :], "mskb", w=["mskb"])
            DMA("sp", mskf[:], masks_f[:, :, :], "mskf", w=["mskf"])
            DMA("sp", cind[:], chunk_ind[:, :], "cind", w=["cind"])
            rowm = sb("rowm_sb", [128, 2])
            DMA("sp", rowm[:], rowm_d[:, :], "rowm", w=["rowm"])
            ps_y1 = banks.pop()
            NBv = len(banks)
            bank_i[0] = 0

            def bank():
                i = bank_i[0] % NBv
                bank_i[0] += 1
                return banks[i], "bank%d" % banks_id[id(banks[i])]

            p_sb = [sb("p_sb0", [128, RW])] * 2
            sh = sb("sh", [128, RW])
            xs = sb("xs", [128, RW])
            lr = sb("lr", [128, 288], BF16)
            lrT = sb("lrT", [128, 4, 128], BF16)
            F = {n: sb("f_" + n, [128, 512]) for n in
                 ("ld", "asig", "gsb", "kkn", "kh", "bv", "tA", "tB", "tC", "Ein", "Eneg", "Eex", "Etot", "Vbar")}
            F["kk"] = F["kkn"]
            F["cum"] = F["tB"]
            F["Y1"], F["Y"], F["ym"], F["yn"] = F["Ein"], F["Eneg"], F["Eex"], F["Etot"]
            Bq = {n: sb("b_" + n, [128, 512], BF16) for n in ("Rt", "At", "Bt", "Kt", "Bh", "Kh", "Vb", "U", "MV", "yrb", "U3", "Vb3")}
            s8 = {n: sb("s8_" + n, [128, 8]) for n in ("ssq", "rn", "bsum", "m8", "v8", "r8")}
            ART = sb("ART", [64, 8, 2, 128], BF16)
            BT = sb("BT", [64, 8, 128], BF16)
            KT = sb("KT", [64, 8, 128], BF16)
            AbT = sb("AbT", [64, 8, 128], BF16)
            G = sb("G", [128, 8, 4, 128], BF16)
            QL = [sb("QL%d" % i, [128, 2, 8, 128], BF16) for i in range(2)]
            ZZ = [sb("ZZ%d" % i, [128, 2, 8, 128], BF16) for i in range(2)]
            Hs = sb("Hs", [64, 8, 64])
            Hb = sb("Hb", [64, 8, 64], BF16)
            WcT = sb("WcT", [64, 8, 4])
            Sraw = sb("Sraw", [128, 4, 64])
            Sout = sb("Sout", [128, 4, 64])
            yrf = sb("yrf", [128, 512])
            yrTt = sb("yrTt", [128, 4, 128], BF16)

            def par(i):
                return parb[:, i, :]

            def v8(ap):
                return ap.rearrange("p (h d) -> p h d", h=8)

            def b8(ap):
                return ap.unsqueeze(2).to_broadcast([128, 8, 64])

            MSET("dve", Hs[:], 0.0, ["Hs"])
            MSET("dve", Hb[:], 0.0, ["Hb"])
            MSET("pool", sh[:], 0.0, ["sh"])
            MSET("pool", Bq["U"][:], 0.0, ["b_U"])
            MSET("pool", Bq["U3"][:], 0.0, ["b_U3"])

            def state_out(dst):
                for hp in range(4):
                    TR(ps_f[:, hp * 64:(hp + 1) * 64], Hs[:, 2 * hp:2 * hp + 2, :], identf[0:64, 0:64], ["Hs", "identf"], ["ps_f"])
                CP("dve", Sout[:], ps_f[:, 0:256].rearrange("p (a b) -> p a b", a=4), ["ps_f"], ["Sout"])
                DMA("sp", dst.rearrange("(hp two) i j -> (two i) hp j", two=2), Sout[:], "Sout", r=["Sout"])

            for t in range(NT):
                is_smp = t >= NCTX + NMAIN
                to = t - NCTX
                load_norm(t)
                Pt = p_sb[t % 2]
                ptn = "p_sb0"
                if (not is_smp) and t > 0:
                    DMA("act", sh[0:1, :], Pt[127:128, :], "sh", r=[ptn], w=["sh"])
                for ci, n0 in enumerate((0, 512, 1024, 1536)):
                    wd = min(512, RW - n0)
                    pb, pbn = bank()
                    for k in range(8):
                        MM(pb[:, 0:wd], hT[:, k, :], wrw[:, k, n0:n0 + wd], ["hT", "wrw"], [pbn], start=(k == 0), stop=(k == 7))
                    CP("act" if ci % 2 == 0 else "dve", Pt[:, n0:n0 + wd], pb[:, 0:wd], [pbn], [ptn])
                if not is_smp:
                    DMA("act", sh[1:128, :], Pt[0:127, :], "sh", r=[ptn], w=["sh"])
                    if t == NCTX + NMAIN - 1:
                        DMA("sp", o_pshift[0:1, :], Pt[127:128, :], ptn, r=[ptn])
                else:
                    if t == NCTX + NMAIN:
                        MSET("pool", sh[:], 0.0, ["sh"])
                    for cb in range(4):
                        j = 4 * (t - NCTX - NMAIN) + cb
                        DMA("act", sh[32 * cb + 25:32 * cb + 32, :], Pt[32 * cb + 24:32 * cb + 31, :], "sh", r=[ptn], w=["sh"])
                        DMA("act", sh[32 * cb + 24:32 * cb + 25, :], shift0[j:j + 1, :], "sh", w=["sh"])
                        DMA("sp", o_sshift[j:j + 1, :], Pt[32 * cb + 31:32 * cb + 32, :], ptn, r=[ptn])
                if dbg and t >= NCTX:
                    DMA("sp", o_prow[to * 128:(to + 1) * 128, :], Pt[:], ptn, r=[ptn])
                TT("pool", sh[:], sh[:], Pt[:], ALU.subtract, ["sh", ptn], ["sh"])
                TT("pool", sh[:], sh[:], mub[:], ALU.mult, ["sh", "mub"], ["sh"])
                TT("dve", xs[:], Pt[:], sh[:], ALU.add, [ptn, "sh"], ["xs"])
                r_ = xs[:, 0:512]
                k_ = xs[:, 576:1088]
                v_ = xs[:, 1088:1600]
                ACT(lr[:, 0:64], xs[:, 512:576], AF.Tanh, ["xs"], ["lr"])
                ACT(lr[:, 128:288], xs[:, 1664:1824], AF.Sigmoid, ["xs"], ["lr"])
                CP("dve", lr[:, 64:128], xs[:, 1600:1664], ["xs"], ["lr"])
                TR(ps_tr[0:64, 0, :], lr[:, 0:64], identb[:], ["lr", "identb"], ["ps_tr"])
                TR(ps_tr[0:64, 1, :], lr[:, 64:128], identb[:], ["lr", "identb"], ["ps_tr"])
                TR(ps_tr[:, 2, :], lr[:, 128:256], identb[:], ["lr", "identb"], ["ps_tr"])
                TR(ps_tr[0:32, 3, :], lr[:, 256:288], identb[:], ["lr", "identb"], ["ps_tr"])
                CP("dve", lrT[0:64, 0:2, :], ps_tr[0:64, 0:2, :], ["ps_tr"], ["lrT"])
                CP("dve", lrT[:, 2, :], ps_tr[:, 2, :], ["ps_tr"], ["lrT"])
                CP("dve", lrT[0:32, 3, :], ps_tr[0:32, 3, :], ["ps_tr"], ["lrT"])
                zb, zbn = bank()
                MM(zb[:, :], lrT[0:64, 0, :], wdec[:, :], ["lrT", "wdec"], [zbn])
                ab, abn = bank()
                MM(ab[:, :], lrT[0:64, 1, :], waaa[:, :], ["lrT", "waaa"], [abn])
                gb, gbn = bank()
                MM(gb[:, :], lrT[:, 2, :], wgat[:, 0, :], ["lrT", "wgat"], [gbn], start=True, stop=False)
                MM(gb[:, :], lrT[0:32, 3, :], wgat[0:32, 1, :], ["lrT", "wgat"], [gbn], start=False, stop=True)
                TT("dve", F["tA"][:], zb[:, :], par(0), ALU.add, [zbn, "parb"], ["f_tA"])
                ACT(F["tA"][:], F["tA"][:], AF.Sigmoid, ["f_tA"], ["f_tA"])
                TS("dve", F["ld"][:], F["tA"][:], -EXPM05, rvalid[:, t:t + 1], ALU.mult, ALU.mult, ["f_tA", "rvalid"], ["f_ld"])
                TT("dve", F["tB"][:], ab[:, :], par(1), ALU.add, [abn, "parb"], ["f_tB"])
                ACT(F["asig"][:], F["tB"][:], AF.Sigmoid, ["f_tB"], ["f_asig"])
                CP("act", F["gsb"][:], gb[:, :], [gbn], ["f_gsb"])
                TT("dve", F["kk"][:], k_, par(2), ALU.mult, ["xs", "parb"], ["f_kkn"])
                TT("pool", F["tC"][:], F["kk"][:], F["kk"][:], ALU.mult, ["f_kkn"], ["f_tC"])
                RED("dve", s8["ssq"][:], v8(F["tC"][:]), ALU.add, ["f_tC"], ["s8_ssq"])
                TS("dve", s8["ssq"][:], s8["ssq"][:], 1e-24, None, ALU.max, None, ["s8_ssq"], ["s8_ssq"])
                P.op("act", lambda e: e.sqrt(out=s8["rn"][:], in_=s8["ssq"][:]), ["s8_ssq"], ["s8_rn"])
                RECIP(s8["rn"][:], s8["rn"][:], ["s8_rn"], ["s8_rn"])
                TT("dve", v8(F["kkn"][:]), v8(F["kk"][:]), b8(s8["rn"][:]), ALU.mult, ["f_kkn", "s8_rn"], ["f_kkn"])
                STT("dve", F["tB"][:], F["asig"][:], -1.0, par(3), ALU.add, ALU.mult, ["f_asig", "parb"], ["f_tB"])
                TT("pool", F["tB"][:], F["tB"][:], k_, ALU.mult, ["f_tB", "xs"], ["f_tB"])
                TT("pool", F["kh"][:], F["tB"][:], k_, ALU.add, ["f_tB", "xs"], ["f_kh"])
                TT("pool", F["bv"][:], F["kkn"][:], F["asig"][:], ALU.mult, ["f_kkn", "f_asig"], ["f_bv"])
                TT("pool", F["tC"][:], r_, F["kh"][:], ALU.mult, ["xs", "f_kh"], ["f_tC"])
                TT("pool", F["tC"][:], F["tC"][:], par(4), ALU.mult, ["f_tC", "parb"], ["f_tC"])
                RED("dve", s8["bsum"][:], v8(F["tC"][:]), ALU.add, ["f_tC"], ["s8_bsum"])
                cb_, cbn = bank()
                MM(cb_[:, :], mskf[:, 0, :], F["ld"][:], ["mskf", "f_ld"], [cbn])
                tb_, tbn = bank()
                MM(tb_[:, :], mskf[:, 1, :], F["ld"][:], ["mskf", "f_ld"], [tbn])
                for h in range(8):
                    MM(ps_f[0:64, h * 4:(h + 1) * 4], F["ld"][:, h * 64:(h + 1) * 64], cind[:, :], ["f_ld", "cind"], ["ps_f"])
                ACT(WcT[:], ps_f[0:64, 0:32].rearrange("p (h c) -> p h c", h=8), AF.Exp, ["ps_f"], ["WcT"])
                ACT(F["Ein"][:], cb_[:, :], AF.Exp, [cbn], ["f_Ein"])
                ACT(F["Eneg"][:], cb_[:, :], AF.Exp, [cbn], ["f_Eneg"], scale=-1.0)
                TT("dve", F["tA"][:], cb_[:, :], F["ld"][:], ALU.subtract, [cbn, "f_ld"], ["f_tA"])
                ACT(F["Eex"][:], F["tA"][:], AF.Exp, ["f_tA"], ["f_Eex"])
                CP("dve", F["cum"][:], cb_[:, :], [cbn], ["f_tB"])
                TT("dve", F["tC"][:], tb_[:, :], F["cum"][:], ALU.subtract, [tbn, "f_tB"], ["f_tC"])
                ACT(F["Etot"][:], F["tC"][:], AF.Exp, ["f_tC"], ["f_Etot"])
                TT("dve", Bq["Rt"][:], r_, F["Ein"][:], ALU.mult, ["xs", "f_Ein"], ["b_Rt"])
                STT("dve", Bq["At"][:], F["kkn"][:], -1.0, F["Eex"][:], ALU.mult, ALU.mult, ["f_kkn", "f_Eex"], ["b_At"])
                TT("pool", Bq["Bt"][:], F["bv"][:], F["Eneg"][:], ALU.mult, ["f_bv", "f_Eneg"], ["b_Bt"])
                TT("dve", Bq["Kt"][:], F["kh"][:], F["Eneg"][:], ALU.mult, ["f_kh", "f_Eneg"], ["b_Kt"])
                TT("pool", Bq["Bh"][:], F["bv"][:], F["Etot"][:], ALU.mult, ["f_bv", "f_Etot"], ["b_Bh"])
                TT("dve", Bq["Kh"][:], F["kh"][:], F["Etot"][:], ALU.mult, ["f_kh", "f_Etot"], ["b_Kh"])
                CP("act", Bq["Vb"][:], v_, ["xs"], ["b_Vb"])
                TS("pool", Bq["Vb3"][64:128, :], v_[64:128, :], rowm[64:128, 0:1], None, ALU.mult, None, ["xs", "rowm"], ["b_Vb3"])
                for (src, dst, dn) in (("At", ART[:, :, 0, :], "ART"), ("Rt", ART[:, :, 1, :], "ART"), ("Bt", BT[:], "BT"), ("Kt", KT[:], "KT")):
                    for h in range(8):
                        TR(ps_tr[0:64, h, :], Bq[src][:, h * 64:(h + 1) * 64], identb[:], ["b_" + src, "identb"], ["ps_tr"])
                    CP("act" if src in ("At", "Bt") else "dve", dst, ps_tr[0:64, :, :], ["ps_tr"], [dn])
                m4 = mskb[:, 0:2, :].unsqueeze(1).to_broadcast([128, 2, 2, 128])
                for h in range(8):
                    b1, b1n = bank()
                    MM(b1[:, 0:256], BT[:, h, :], ART[:, h, :, :], ["BT", "ART"], [b1n])
                    MM(b1[:, 256:512], KT[:, h, :], ART[:, h, :, :], ["KT", "ART"], [b1n])
                    TT("dve", G[:, h, :, :].rearrange("p (a b) s -> p a b s", a=2), b1[:, :].rearrange("p (a b s) -> p a b s", a=2, b=2), m4, ALU.mult,
                       [b1n, "mskb"], ["G"])
                cur = 0
                for g in range(2):
                    b2, b2n = bank()
                    for hh in range(4):
                        h = 4 * g + hh
                        MM(b2[:, hh * 128:(hh + 1) * 128], ART[:, h, 0, :], BT[:, h, :], ["ART", "BT"], [b2n])
                    TT("dve", QL[cur][:, 1, 4 * g:4 * g + 4, :], b2[:, :].rearrange("p (a s) -> p a s", a=4),
                       mskb[:, 2, :].unsqueeze(1).to_broadcast([128, 4, 128]), ALU.mult, [b2n, "mskb"], ["QL%d" % cur])
                CP("pool", QL[cur][:, 0, :, :], G[:, :, 0, :], ["G"], ["QL%d" % cur])
                idb8 = identb[:].unsqueeze(1).unsqueeze(1).to_broadcast([128, 2, 8, 128])
                TT("pool", ZZ[0][:], QL[cur][:], idb8, ALU.add, ["QL%d" % cur, "identb"], ["ZZ0"])
                zc = 0
                nlev = 2 if is_smp else 4
                for lev in range(1, nlev + 1):
                    last = lev == nlev
                    nxt = 1 - cur
                    zn = 1 - zc
                    for g in range(2):
                        hs = slice(4 * g, 4 * g + 4)
                        bq, bqn = bank()
                        for hh in range(4):
                            h = 4 * g + hh
                            MM(bq[:, hh * 128:(hh + 1) * 128], QL[cur][:, 1, h, :], QL[cur][:, 0, h, :], ["QL%d" % cur], [bqn])
                        CP("act", QL[nxt][:, 0, hs, :], bq[:, :].rearrange("p (a s) -> p a s", a=4), [bqn], ["QL%d" % nxt])
                        if not last:
                            bl, bln = bank()
                            for hh in range(4):
                                h = 4 * g + hh
                                MM(bl[:, hh * 128:(hh + 1) * 128], QL[cur][:, 0, h, :], QL[cur][:, 1, h, :], ["QL%d" % cur], [bln])
                            CP("act", QL[nxt][:, 1, hs, :], bl[:, :].rearrange("p (a s) -> p a s", a=4), [bln], ["QL%d" % nxt])
                        bz, bzn = bank()
                        for hh in range(4):
                            h = 4 * g + hh
                            MM(bz[:, hh * 128:(hh + 1) * 128], ZZ[zc][:, 1, h, :], QL[nxt][:, 0, h, :], ["ZZ%d" % zc, "QL%d" % nxt], [bzn])
                        TT("dve", ZZ[zn][:, 0, hs, :], bz[:, :].rearrange("p (a s) -> p a s", a=4), ZZ[zc][:, 0, hs, :], ALU.add,
                           [bzn, "ZZ%d" % zc], ["ZZ%d" % zn])
                        if not last:
                            bzt, bztn = bank()
                            for hh in range(4):
                                h = 4 * g + hh
                                MM(bzt[:, hh * 128:(hh + 1) * 128], QL[nxt][:, 0, h, :], ZZ[zc][:, 1, h, :], ["ZZ%d" % zc, "QL%d" % nxt], [bztn])
                            TT("dve", ZZ[zn][:, 1, hs, :], bzt[:, :].rearrange("p (a s) -> p a s", a=4), ZZ[zc][:, 1, hs, :], ALU.add,
                               [bztn, "ZZ%d" % zc], ["ZZ%d" % zn])
                    cur = nxt
                    zc = zn
                Zf = ZZ[zc]
                zfn = "ZZ%d" % zc
                bm, bmn = bank()
                for h in range(8):
                    MM(bm[:, h * 64:(h + 1) * 64], G[:, h, 2, :], Bq["Vb"][:, h * 64:(h + 1) * 64], ["G", "b_Vb"], [bmn])
                CP("act", Bq["MV"][:], bm[:, :], [bmn], ["b_MV"])
                bvb, bvbn = bank()
                for h in range(8):
                    MM(bvb[:, h * 64:(h + 1) * 64], Zf[:, 0, h, :], Bq["MV"][:, h * 64:(h + 1) * 64], [zfn, "b_MV"], [bvbn])
                CP("act", F["Vbar"][:], bvb[:, :], [bvbn], ["f_Vbar"])
                for g in range(2):
                    ba, ban = bank()
                    for hh in range(4):
                        h = 4 * g + hh
                        MM(ba[0:64, hh * 128:(hh + 1) * 128], Bq["At"][:, h * 64:(h + 1) * 64], Zf[:, 0, h, :], ["b_At", zfn], [ban])
                    CP("dve", AbT[:, 4 * g:4 * g + 4, :], ba[0:64, :].rearrange("p (a s) -> p a s", a=4), [ban], ["AbT"])
                for c in range(4):
                    r0 = 32 * c
                    rs = slice(r0, r0 + 32)
                    if is_smp:
                        j = 4 * (t - NCTX - NMAIN) + c
                        DMA("sp", Sraw[:], state0[j].rearrange("(hp two) i j -> (two i) hp j", two=2), "Sraw", w=["Sraw"])
                        for hp in range(4):
                            TR(ps_f[0:64, hp * 128:(hp + 1) * 128], Sraw[:, hp, :], identf[:], ["Sraw", "identf"], ["ps_f"])
                        CP("dve", Hs[:], ps_f[0:64, :].rearrange("p (h i) -> p h i", h=8), ["ps_f"], ["Hs"])
                        CP("dve", Hb[:], Hs[:], ["Hs"], ["Hb"])
                    bu, bun = bank()
                    hn, hnn = bank()
                    if c < 3:
                        ms = rs if c < 2 else slice(64, 128)
                        for h in range(8):
                            MM(bu[rs, h * 64:(h + 1) * 64], AbT[:, h, rs], Hb[:, h, :], ["AbT", "Hb"], [bun])
                        TT("dve", Bq["U"][rs, :], bu[rs, :], F["Vbar"][rs, :], ALU.add, [bun, "f_Vbar"], ["b_U"])
                        for h in range(8):
                            MM(ps_y1[ms, h * 64:(h + 1) * 64], ART[:, h, 1, ms], Hb[:, h, :], ["ART", "Hb"], ["ps_y1"])
                        for h in range(8):
                            hsl = slice(h * 64, (h + 1) * 64)
                            MM(hn[0:64, hsl], Bq["Bh"][rs, hsl], Bq["U"][rs, hsl], ["b_Bh", "b_U"], [hnn], start=True, stop=False)
                            MM(hn[0:64, hsl], Bq["Kh"][rs, hsl], Bq["Vb"][rs, hsl], ["b_Kh", "b_Vb"], [hnn], start=False, stop=True)
                    else:
                        ms = slice(64, 128)
                        for h in range(8):
                            MM(bu[ms, h * 64:(h + 1) * 64], AbT[:, h, ms], Hb[:, h, :], ["AbT", "Hb"], [bun])
                        TT("dve", F["tA"][ms, :], bu[ms, :], F["Vbar"][ms, :], ALU.add, [bun, "f_Vbar"], ["f_tA"])
                        TS("dve", Bq["U3"][ms, :], F["tA"][ms, :], rowm[ms, 0:1], None, ALU.mult, None, ["f_tA", "rowm"], ["b_U3"])
                        STT("dve", Bq["U"][ms, :], Bq["U"][ms, :], rowm[ms, 1:2], Bq["U3"][ms, :], ALU.mult, ALU.add, ["b_U", "b_U3", "rowm"], ["b_U"])
                        by3, by3n = bank()
                        for h in range(8):
                            MM(by3[ms, h * 64:(h + 1) * 64], ART[:, h, 1, ms], Hb[:, h, :], ["ART", "Hb"], [by3n])
                        TS("dve", F["tC"][ms, :], by3[ms, :], rowm[ms, 0:1], None, ALU.mult, None, [by3n, "rowm"], ["f_tC"])
                        for h in range(8):
                            hsl = slice(h * 64, (h + 1) * 64)
                            MM(hn[0:64, hsl], Bq["Bh"][ms, hsl], Bq["U3"][ms, hsl], ["b_Bh", "b_U3"], [hnn], start=True, stop=False)
                            MM(hn[0:64, hsl], Bq["Kh"][ms, hsl], Bq["Vb3"][ms, hsl], ["b_Kh", "b_Vb3"], [hnn], start=False, stop=True)
                    TT("dve", Hs[:], Hs[:], WcT[:, :, c:c + 1].to_broadcast([64, 8, 64]), ALU.mult, ["Hs", "WcT"], ["Hs"])
                    TT("dve", Hs[:], Hs[:], hn[0:64, :].rearrange("p (h i) -> p h i", h=8), ALU.add, ["Hs", hnn], ["Hs"])
                    CP("dve", Hb[:], Hs[:], ["Hs"], ["Hb"])
                    if is_smp:
                        state_out(o_sstate[j])
                if t == NCTX + NMAIN - 1:
                    state_out(o_pstate[:, :, :])
                if t < NCTX:
                    continue
                CP("act", F["Y1"][:], ps_y1[:, :], ["ps_y1"], ["f_Ein"])
                STT("dve", F["Y1"][64:128, :], F["Y1"][64:128, :], rowm[64:128, 1:2], F["tC"][64:128, :], ALU.mult, ALU.add,
                    ["f_Ein", "f_tC", "rowm"], ["f_Ein"])
                by, byn = bank()
                for h in range(8):
                    hsl = slice(h * 64, (h + 1) * 64)
                    MM(by[:, hsl], G[:, h, 1, :], Bq["U"][:, hsl], ["G", "b_U"], [byn], start=True, stop=False)
                    MM(by[:, hsl], G[:, h, 3, :], Bq["Vb"][:, hsl], ["G", "b_Vb"], [byn], start=False, stop=True)
                TT("dve", F["Y"][:], by[:, :], F["Y1"][:], ALU.add, [byn, "f_Ein"], ["f_Eneg"])
                RED("dve", s8["m8"][:], v8(F["Y"][:]), ALU.add, ["f_Eneg"], ["s8_m8"])
                TS("dve", s8["m8"][:], s8["m8"][:], 1.0 / 64, None, ALU.mult, None, ["s8_m8"], ["s8_m8"])
                TT("dve", v8(F["ym"][:]), v8(F["Y"][:]), b8(s8["m8"][:]), ALU.subtract, ["f_Eneg", "s8_m8"], ["f_Eex"])
                TT("pool", F["tA"][:], F["ym"][:], F["ym"][:], ALU.mult, ["f_Eex"], ["f_tA"])
                RED("dve", s8["v8"][:], v8(F["tA"][:]), ALU.add, ["f_tA"], ["s8_v8"])
                TS("dve", s8["v8"][:], s8["v8"][:], 1.0 / 64, 64e-5, ALU.mult, ALU.add, ["s8_v8"], ["s8_v8"])
                P.op("act", lambda e: e.sqrt(out=s8["r8"][:], in_=s8["v8"][:]), ["s8_v8"], ["s8_r8"])
                RECIP(s8["r8"][:], s8["r8"][:], ["s8_r8"], ["s8_r8"])
                TT("dve", v8(F["yn"][:]), v8(F["ym"][:]), b8(s8["r8"][:]), ALU.mult, ["f_Eex", "s8_r8"], ["f_Etot"])
                TT("pool", F["yn"][:], F["yn"][:], par(5), ALU.mult, ["f_Etot", "parb"], ["f_Etot"])
                TT("pool", F["yn"][:], F["yn"][:], par(6), ALU.add, ["f_Etot", "parb"], ["f_Etot"])
                TT("dve", v8(F["tB"][:]), v8(v_), b8(s8["bsum"][:]), ALU.mult, ["xs", "s8_bsum"], ["f_tB"])
                TT("pool", F["yn"][:], F["yn"][:], F["tB"][:], ALU.add, ["f_Etot", "f_tB"], ["f_Etot"])
                TT("dve", yrf[:], F["yn"][:], F["gsb"][:], ALU.mult, ["f_Etot", "f_gsb"], ["yrf"])
                if dbg:
                    DMA("sp", o_yr[to * 128:(to + 1) * 128, :], yrf[:], "yrf", r=["yrf"])
                CP("act", Bq["yrb"][:], yrf[:], ["yrf"], ["b_yrb"])
                for q in range(4):
                    TR(ps_tr[:, q, :], Bq["yrb"][:, q * 128:(q + 1) * 128], identb[:], ["b_yrb", "identb"], ["ps_tr"])
                CP("dve", yrTt[:], ps_tr[:, 0:4, :], ["ps_tr"], ["yrTt"])
                DMA("sp", yrd[:, :, to * 128:(to + 1) * 128], yrTt[:], "yrTt", r=["yrTt"])

        if do_rwkv:
            P.barrier()
            cur_stack.pop()
            st_rw.close()
            banks.append(ps_y1)

        NBd = len(banks)
        bank_i[0] = 0

        def bank():
            i = bank_i[0] % NBd
            bank_i[0] += 1
            return banks[i], "bank%d" % banks_id[id(banks[i])]

        if do_dense:
            h2T = sb("h2T", [128, 8, NOUT * 128], BF16)
            st_a = ExitStack()
            cur_stack.append(st_a)
            wgr = sb("wgr", [128, 8, 1024], BF16)
            wgn = sb("wgn", [128, 8, 1024], BF16)
            wbr = sb("wbr", [128, 4, 1024], BF16)
            wbn = sb("wbn", [128, 4, 1024], BF16)
            wo = sb("wo", [128, 8, 1024], BF16)
            gffn = sb("gffn", [128, 8])
            DMA("pool", wgr[:], w_in_v[:, :, 3128:4152], "wgr", w=["wgr"])
            DMA("pool", wgn[:], w_in_v[:, :, 4152:5176], "wgn", w=["wgn"])
            DMA("pool", wbr[:], w_br_rwkv.rearrange("(k p) n -> p k n", p=128), "wbr", w=["wbr"])
            DMA("pool", wbn[:], w_br_nsa.rearrange("(k p) n -> p k n", p=128), "wbn", w=["wbn"])
            DMA("pool", wo[:], w_out.rearrange("(k p) n -> p k n", p=128), "wo", w=["wo"])
            DMA("sp", gffn[:], g_ffn[:, :], "gffn", w=["gffn"])
            xg = sb("xg", [128, 4, D])
            hT4 = sb("hT4", [128, 8, 512], BF16)
            mT = sb("mT", [128, 8, 512], BF16)
            sg1 = sb("sg1", [128, 512])
            sg2 = sb("sg2", [128, 512])
            x2t = [sb("x2t%d" % i, [128, D]) for i in range(2)]
            yrT = sb("yrg", [128, 4, 512], BF16)
            ynT = sb("yng", [128, 4, 512], BF16)
            for grp in range(NOUT // 4):
                cols = slice(0, 512)
                DMA("sp", yrT[:], yrd[:, :, grp * 512:(grp + 1) * 512], "yrT", w=["yrT"])
                if do_nsa:
                    DMA("sp", ynT[:], ynd[:, :, grp * 512:(grp + 1) * 512], "ynT", w=["ynT"])
                elif grp == 0:
                    MSET("pool", ynT[:], 0.0, ["ynT"])
                for i in range(4):
                    t = NCTX + 4 * grp + i
                    load_norm(t, xg[:, i, :], "xg%d" % i, hT4[:, :, i * 128:(i + 1) * 128], "hT4")
                for m in range(8):
                    ms = slice(m * 128, (m + 1) * 128)
                    b_gr, n_gr = bank()
                    for k in range(8):
                        MM(b_gr[:, :], wgr[:, k, ms], hT4[:, k, :], ["wgr", "hT4"], [n_gr], start=(k == 0), stop=(k == 7))
                    b_gn, n_gn = bank()
                    for k in range(8):
                        MM(b_gn[:, :], wgn[:, k, ms], hT4[:, k, :], ["wgn", "hT4"], [n_gn], start=(k == 0), stop=(k == 7))
                    b_yr, n_yr = bank()
                    for k in range(4):
                        MM(b_yr[:, :], wbr[:, k, ms], yrT[:, k, cols], ["wbr", "yrT"], [n_yr], start=(k == 0), stop=(k == 3))
                    b_yn, n_yn = bank()
                    for k in range(4):
                        MM(b_yn[:, :], wbn[:, k, ms], ynT[:, k, cols], ["wbn", "ynT"], [n_yn], start=(k == 0), stop=(k == 3))
                    ACT(sg1[:], b_gr[:, :], AF.Sigmoid, [n_gr], ["sg1"])
                    ACT(sg2[:], b_gn[:, :], AF.Sigmoid, [n_gn], ["sg2"])
                    TT("dve", sg1[:], sg1[:], b_yr[:, :], ALU.mult, ["sg1", n_yr], ["sg1"])
                    TT("dve", sg2[:], sg2[:], b_yn[:, :], ALU.mult, ["sg2", n_yn], ["sg2"])
                    TT("pool", mT[:, m, :], sg1[:], sg2[:], ALU.add, ["sg1", "sg2"], ["mT"])
                for i in range(4):
                    to = 4 * grp + i
                    X2 = x2t[to % 2]
                    x2n = "x2t%d" % (to % 2)
                    for half in range(2):
                        hs_ = slice(half * 512, (half + 1) * 512)
                        b, bn = bank()
                        for m in range(8):
                            MM(b[:, :], mT[:, m, i * 128:(i + 1) * 128], wo[:, m, hs_], ["mT", "wo"], [bn], start=(m == 0), stop=(m == 7))
                        TT("dve", X2[:, hs_], b[:, :], xg[:, i, hs_], ALU.add, [bn, "xg%d" % i], [x2n])
                    DMA("sp", x2d[to * 128:(to + 1) * 128, :], X2[:], x2n, r=[x2n])
                    ACT(sq[:], X2[:], AF.Square, [x2n], ["sq", "ss"], accum=ss[:])
                    TS("dve", rstd[:], ss[:], 1.0 / D, 1e-6, ALU.mult, ALU.add, ["ss"], ["rstd"])
                    P.op("act", lambda e: e.sqrt(out=rstd[:], in_=rstd[:]), ["rstd"], ["rstd"])
                    RECIP(rstd[:], rstd[:], ["rstd"], ["rstd"])
                    TS("dve", xn[:], X2[:], rstd[:, 0:1], None, ALU.mult, None, [x2n, "rstd"], ["xn"])
                    for k in range(8):
                        TR(ps_tr[:, k, :], xn[:, k * 128:(k + 1) * 128], identb[:], ["xn", "identb"], ["ps_tr"])
                    TT("dve", h2T[:, :, to * 128:(to + 1) * 128], ps_tr[:], gffn[:].unsqueeze(2).to_broadcast([128, 8, 128]), ALU.mult,
                       ["ps_tr", "gffn"], ["h2T"])
            P.barrier()
            cur_stack.pop()
            st_a.close()
        if do_dense:
            st_b = ExitStack()
            cur_stack.append(st_b)
            NF = 22
            GT = NOUT // 2
            GW = GT * 128
            wd = sb("wd", [128, NF, D], BF16)
            wd_v = w_ffn_down.rearrange("(f p) n -> p f n", p=128)
            for f0 in range(0, NF, 2):
                DMA("pool", wd[:, f0:f0 + 2, :], wd_v[:, f0:f0 + 2, :], "wd", w=["wd"])
            gfin = sb("gfin", [128, D])
            DMA("sp", gfin[:], g_fin_b[:, :], "gfin", w=["gfin"])
            actT = sb("actT", [128, NF, GW], BF16)
            wgu = [sb("wgu%d" % i, [128, 2, 8, 128], BF16) for i in range(3)]
            sgt = sb("sgt", [128, 512])
            x2r = [sb("x2r%d" % i, [128, D]) for i in range(2)]
            yo = [sb("yo%d" % i, [128, D]) for i in range(2)]
            wg_v = w_ffn_gate.rearrange("(k p) n -> p k n", p=128)
            wu_v = w_ffn_up.rearrange("(k p) n -> p k n", p=128)
            for grp in range(2):
                c0 = grp * GW
                for f in range(NF):
                    W = wgu[f % 3]
                    wn = "wgu%d" % (f % 3)
                    DMA("pool", W[:, 0, :, :], wg_v[:, :, f * 128:(f + 1) * 128], wn, w=[wn])
                    DMA("pool", W[:, 1, :, :], wu_v[:, :, f * 128:(f + 1) * 128], wn, w=[wn])
                    for (n0, nw) in ((0, 512), (512, 512), (1024, 256)):
                        bg, bgn = bank()
                        for k in range(8):
                            MM(bg[:, 0:nw], W[:, 0, k, :], h2T[:, k, c0 + n0:c0 + n0 + nw], [wn, "h2T"], [bgn], start=(k == 0), stop=(k == 7))
                        bu, bun = bank()
                        for k in range(8):
                            MM(bu[:, 0:nw], W[:, 1, k, :], h2T[:, k, c0 + n0:c0 + n0 + nw], [wn, "h2T"], [bun], start=(k == 0), stop=(k == 7))
                        ACT(sgt[:, 0:nw], bg[:, 0:nw], AF.Silu, [bgn], ["sgt"])
                        TT("dve", actT[:, f, n0:n0 + nw], sgt[:, 0:nw], bu[:, 0:nw], ALU.mult, ["sgt", bun], ["actT"])
                for i in range(GT):
                    to = grp * GT + i
                    X2 = x2r[to % 2]
                    x2n = "x2r%d" % (to % 2)
                    Y = yo[to % 2]
                    yn_ = "yo%d" % (to % 2)
                    DMA("sp", X2[:], x2d[to * 128:(to + 1) * 128, :], x2n, w=[x2n])
                    for half in range(2):
                        hs_ = slice(half * 512, (half + 1) * 512)
                        b, bn = bank()
                        for f in range(NF):
                            MM(b[:, :], actT[:, f, i * 128:(i + 1) * 128], wd[:, f, hs_], ["actT", "wd"], [bn], start=(f == 0), stop=(f == NF - 1))
                        TT("dve", Y[:, hs_], b[:, :], X2[:, hs_], ALU.add, [bn, x2n], [yn_])
                    ACT(sq[:], Y[:], AF.Square, [yn_], ["sq", "ss"], accum=ss[:])
                    TS("dve", rstd[:], ss[:], 1.0 / D, 1e-6, ALU.mult, ALU.add, ["ss"], ["rstd"])
                    P.op("act", lambda e: e.sqrt(out=rstd[:], in_=rstd[:]), ["rstd"], ["rstd"])
                    RECIP(rstd[:], rstd[:], ["rstd"], ["rstd"])
                    STT("dve", Y[:], Y[:], rstd[:, 0:1], gfin[:], ALU.mult, ALU.mult, [yn_, "rstd", "gfin"], [yn_])
                    DMA("sp", o_y[to * 128:(to + 1) * 128, :], Y[:], yn_, r=[yn_])
            P.barrier()
            cur_stack.pop()
            st_b.close()
        P.barrier()
        P.emit()
    return nc


def _consts():
    idx = np.arange(128)
    same = (idx[:, None] // CH) == (idx[None, :] // CH)
    MsT = (same & (idx[:, None] < idx[None, :])).astype(np.float32)
    MiT = (same & (idx[:, None] <= idx[None, :])).astype(np.float32)
    Ms = MsT.T.copy()
    masks_bf = np.stack([MsT, MiT, Ms], axis=1).astype(np.float32)
    masks_f = np.stack([MiT, same.astype(np.float32)], axis=1).astype(np.float32)
    chunk_ind = (idx[:, None] // CH == np.arange(4)[None, :]).astype(np.float32)
    rowm = np.stack([(idx >= 96), (idx < 96)], 1).astype(np.float32)
    return dict(ident=np.eye(128, dtype=np.float32), masks_bf=masks_bf, masks_f=masks_f, chunk_ind=chunk_ind, rowm=rowm)


def _nsa_tables(half):
    NPT = NCTX + NMAIN
    p = np.arange(128)
    valid_t = np.ones((128, NPT), np.float32)
    if half == 0:
        valid_t[:, 0:NCTX] = 0.0
    n = np.arange(256)
    nabs = n - (0 if half == 1 else 128)
    okc = (nabs >= 0) & (n <= 254)
    valid_c = okc.astype(np.float32).reshape(2, 128).T.copy()
    j = np.arange(64)
    ovl = ((n[:, None] * 16 < (j[None, :] + 1) * 64) & (n[:, None] * 16 + 32 > j[None, :] * 64) & okc[:, None]).astype(np.float32)
    ovl = ovl.reshape(2, 128, 64).transpose(1, 0, 2).copy()
    s_ = np.arange(NPT * 128)
    E = (s_[None, :] // 64 == j[:, None]).astype(np.float32)
    tq = np.arange(NMAIN * 128)
    pos = half * 2048 + tq
    mc = (okc[:, None] & (16 * nabs[:, None] + 31 <= pos[None, :])).astype(np.float32)
    maskc = mc.reshape(2, 128, NMAIN * 128).transpose(1, 0, 2).copy()
    cur = pos // 64
    jabs = j - (0 if half == 1 else 32)
    allowed = ((jabs[None, :] >= 0) & (jabs[None, :] <= cur[:, None]))
    forced = allowed & ((jabs[None, :] == 0) | (jabs[None, :] == cur[:, None]) | (jabs[None, :] == cur[:, None] - 1))
    fb = 1e6 * forced.astype(np.float32) - 1e9 * (1.0 - allowed.astype(np.float32))
    allowed = allowed.astype(np.float32).reshape(NMAIN, 128, 64).transpose(1, 0, 2).copy()
    fb = fb.astype(np.float32).reshape(NMAIN, 128, 64).transpose(1, 0, 2).copy()
    tri = np.stack([(p[:, None] <= p[None, :]), (p[:, None] > p[None, :])], 1).astype(np.float32)
    return dict(valid_t=valid_t, valid_c=valid_c, ovl=ovl, E_ind=E, ones_row=np.ones((1, NPT * 128), np.float32),
                maskc=maskc, allowed=allowed, fbias=fb, tri=tri)


def _smp_tables():
    p = np.arange(128)
    n = np.arange(128)
    j = np.arange(64)
    ovl = ((n[:, None] * 16 < (j[None, :] + 1) * 64) & (n[:, None] * 16 + 32 > j[None, :] * 64) & (n[:, None] <= 126)).astype(np.float32)
    cur = 32
    allowed = np.broadcast_to((j <= cur)[None, :], (128, 64))
    forced = allowed & ((j == 0) | (j == cur) | (j == cur - 1))[None, :]
    fb = (1e6 * forced - 1e9 * (1.0 - allowed)).astype(np.float32)
    t = np.arange(32)
    mw0 = ((p[:, None] > t[None, :] - 24) | (t[None, :] < 24)).astype(np.float32)
    return dict(iota_p=p.astype(np.float32).reshape(128, 1), ovl_abs=ovl, allowed_s=allowed.astype(np.float32).copy(),
                fbias_s=fb.copy(), maskw0=mw0)


def _rope_tab(pos):
    half = 8
    inv = (500000.0 ** (-np.arange(half, dtype=np.float32) / half)).astype(np.float32)
    ang = pos.astype(np.float32)[:, None] * inv[None, :]
    return np.cos(ang).astype(np.float32), np.sin(ang).astype(np.float32)


def make_in_maps(inputs):
    f = lambda a: np.ascontiguousarray(np.asarray(a, dtype=np.float32))
    x_prompt = f(inputs["x_prompt"])
    x_sample = f(inputs["x_sample"])
    C = _consts()
    w_in = f(inputs["w_in"][0])
    g_mix = f(inputs["g_mix"][0]).reshape(8, 128).T.copy()
    mu_b = np.broadcast_to(f(inputs["rwkv_mu"][0])[None, :], (128, RW)).copy()
    pars = [inputs["rwkv_w0"][0], inputs["rwkv_a0"][0], inputs["rwkv_k_k"][0], inputs["rwkv_k_a"][0],
            np.asarray(inputs["rwkv_r_k"][0]).reshape(512), inputs["rwkv_ln_w"][0], inputs["rwkv_ln_b"][0]]
    par_b = np.broadcast_to(np.stack([f(p) for p in pars], 0)[None], (128, 7, 512)).copy()
    shared = dict(w_in=w_in, g_mix=g_mix, mu_b=mu_b, par_b=par_b,
                  w_decay=f(inputs["rwkv_w_decay"][0]), w_aaa=f(inputs["rwkv_w_aaa"][0]),
                  w_gate=f(inputs["rwkv_w_gate"][0]),
                  w_br_rwkv=f(inputs["w_br_rwkv"][0]), w_br_nsa=f(inputs["w_br_nsa"][0]), w_out=f(inputs["w_out"][0]),
                  g_ffn=f(inputs["g_ffn"][0]).reshape(8, 128).T.copy(),
                  w_ffn_gate=f(inputs["w_ffn_gate"][0]), w_ffn_up=f(inputs["w_ffn_up"][0]), w_ffn_down=f(inputs["w_ffn_down"][0]),
                  g_fin_b=np.broadcast_to(f(inputs["g_final"])[None, :], (128, D)).copy(),
                  w1h=f(inputs["nsa_w_cmp1"][0]).transpose(2, 0, 1, 3).copy(),
                  peh=f(inputs["nsa_pe_cmp"][0]).transpose(2, 0, 1).copy(),
                  w2h=f(inputs["nsa_w_cmp2"][0]).transpose(1, 0, 2).copy(), **C)
    cache_c = f(inputs["cache_cmp_kv"][0]).reshape(2560, 128, 256)
    cache_s = f(inputs["cache_slc_kv"][0]).reshape(2560, 128, 256)
    maps = []
    for c in range(8):
        s, half = c // 2, c % 2
        xr = np.zeros((NT * 128, D), np.float32)
        if half == 1:
            xr[0:2048] = x_prompt[s, 0:2048]
        xr[2048:4096] = x_prompt[s, half * 2048:(half + 1) * 2048]
        pos = np.zeros(NT * 128, np.float32)
        pos[0:2048] = np.arange(2048) if half == 1 else 0
        pos[2048:4096] = half * 2048 + np.arange(2048)
        rv = np.ones((NT, 128), np.float32)
        for j in range(16):
            b = 16 * c + j
            r0 = 4096 + j * 32
            xr[r0 + 24:r0 + 32] = x_sample[b]
            pos[r0 + 24:r0 + 32] = 2048 + np.arange(8)
            rv[(r0 // 128), (r0 % 128):(r0 % 128) + 24] = 0.0
        cos, sin = _rope_tab(pos)
        cs = np.stack([cos, sin], 1).reshape(NT, 128, 2, 8).transpose(1, 0, 2, 3).copy()
        nsa = _nsa_tables(half)
        m = dict(shared)
        m.update(nsa)
        m.update(_smp_tables())
        m["page_tab"] = np.ascontiguousarray(np.asarray(inputs["page_table"])[16 * c:16 * c + 16].reshape(1, 256).astype(np.int32))
        m["cache_c"] = cache_c
        m["cache_s"] = cache_s
        m.update(xrows=xr, rowvalid=rv.T.copy(), cs_tab=cs,
                 shift0=f(inputs["state_rwkv_shift"][0, 16 * c:16 * c + 16]),
                 state0=f(inputs["state_rwkv"][0, 16 * c:16 * c + 16]),
                 cwin=f(inputs["cache_win_kv"][0, 16 * c:16 * c + 16]).reshape(16, 512, 256))
        maps.append(m)
    return maps


_NC_CACHE = {}


def kernel(**inputs):
    if "nc" not in _NC_CACHE:
        _NC_CACHE["nc"] = build_program()
    nc = _NC_CACHE["nc"]
    maps = make_in_maps(inputs)
    res = run_bass_kernel_spmd(nc, maps, core_ids=list(range(8)))
    R = res.results
    _NC_CACHE["last"] = R
    B, T = 4, 4096
    p_cmp = np.zeros((1, B, T, 2, 2, 64), np.float32)
    p_slc = np.zeros((1, B, T, 2, 2, 64), np.float32)
    p_win = np.zeros((1, B, 512, 2, 2, 64), np.float32)
    p_rwkv = np.zeros((1, B, 8, 64, 64), np.float32)
    p_shift = np.zeros((1, B, RW), np.float32)
    s_cmp = np.zeros((1, 128, 8, 2, 2, 64), np.float32)
    s_slc = np.zeros((1, 128, 8, 2, 2, 64), np.float32)
    s_win = np.zeros((1, 128, 512, 2, 2, 64), np.float32)
    s_rwkv = np.zeros((1, 128, 8, 64, 64), np.float32)
    s_shift = np.zeros((1, 128, RW), np.float32)
    y_p = np.zeros((B, T, D), np.float32)
    y_s = np.zeros((128, 8, D), np.float32)
    for c in range(8):
        s, half = c // 2, c % 2
        kvr = R[c]["o_kv"]
        main = kvr[0:2048]
        p_cmp[0, s, half * 2048:(half + 1) * 2048] = main[:, 0:256].reshape(2048, 2, 2, 64)
        p_slc[0, s, half * 2048:(half + 1) * 2048] = main[:, 256:512].reshape(2048, 2, 2, 64)
        if half == 1:
            p_win[0, s] = main[2048 - 512:, 512:768].reshape(512, 2, 2, 64)
        smp = kvr[2048:].reshape(16, 32, 768)[:, 24:32]
        s_cmp[0, 16 * c:16 * c + 16] = smp[:, :, 0:256].reshape(16, 8, 2, 2, 64)
        s_slc[0, 16 * c:16 * c + 16] = smp[:, :, 256:512].reshape(16, 8, 2, 2, 64)
        if "o_swin" in R[c]:
            s_win[0, 16 * c:16 * c + 16] = R[c]["o_swin"].reshape(16, 512, 2, 2, 64)
        if "o_pshift" in R[c] and half == 1:
            p_shift[0, s] = R[c]["o_pshift"][0]
            p_rwkv[0, s] = R[c]["o_pstate"]
        if "o_sshift" in R[c]:
            s_shift[0, 16 * c:16 * c + 16] = R[c]["o_sshift"]
            s_rwkv[0, 16 * c:16 * c + 16] = R[c]["o_sstate"]
        if "o_y" in R[c]:
            yr = R[c]["o_y"]
            y_p[s, half * 2048:(half + 1) * 2048] = yr[0:2048]
            y_s[16 * c:16 * c + 16] = yr[2048:].reshape(16, 32, D)[:, 24:32]
    return (y_p, y_s, p_cmp, p_slc, p_win, p_rwkv, p_shift, s_cmp, s_slc, s_win, s_rwkv, s_shift)
```
